# Optimizing a Trainium2 kernel written in Bass

```python
import jax, jax.numpy as jnp
from jax import lax
import numpy as np

D_MODEL = 4096
BATCH = 4
SEQ = 2048
DEPTH = 1

MIX_WIDTH = D_MODEL
HEAD_DIM = 128
CONV_WIDTH = MIX_WIDTH // 2
GMLP_WIDTH = MIX_WIDTH - CONV_WIDTH
CONV_GROUPS = CONV_WIDTH // HEAD_DIM
GMLP_HEADS = GMLP_WIDTH // HEAD_DIM
CONV_K = 3
CHUNK = 128
D_FF = 4 * D_MODEL
IN_PROJ_WIDTH = 3 * CONV_WIDTH + 2 * GMLP_WIDTH
EPS = 1e-5

kernel_name = "hybrid_shortconv_gmlp_block"


def rmsnorm(x, g):
    xf = x.astype(jnp.float32)
    inv = lax.rsqrt(jnp.mean(xf * xf, axis=-1, keepdims=True) + EPS)
    return (xf * inv * g.astype(jnp.float32)).astype(x.dtype)


def short_conv_mixer(b_gate, c_gate, h_in, conv_w):
    h = c_gate * h_in
    S = h.shape[1]
    hp = jnp.pad(h, ((0, 0), (CONV_K - 1, 0), (0, 0)))
    y = conv_w[0] * hp[:, 0:S]
    for k in range(1, CONV_K):
        y = y + conv_w[k] * hp[:, k:k + S]
    return b_gate * y


def chunked_spatial_gating(u, v, spatial_w, spatial_b):
    bsz, S, _ = v.shape
    n_chunks = S // CHUNK
    causal = jnp.tril(jnp.ones((CHUNK, CHUNK), dtype=bool))
    w = jnp.where(causal[None], spatial_w, jnp.zeros((), spatial_w.dtype))
    vc = v.reshape(bsz, n_chunks, CHUNK, GMLP_HEADS, HEAD_DIM)
    s = jnp.einsum('hts,bcshd->bcthd', w, vc) + spatial_b.T[None, None, :, :, None]
    return u * s.reshape(bsz, S, GMLP_WIDTH)


def setup_inputs(seed: int = 0) -> dict:
    key = jax.random.key(seed)
    ks = jax.random.split(key, 13)
    f32 = jnp.float32
    x = jax.random.normal(ks[0], (BATCH, SEQ, D_MODEL), f32)
    mix_norm_g = 1.0 + 0.02 * jax.random.normal(ks[1], (DEPTH, D_MODEL), f32)
    w_in = jax.random.normal(ks[2], (DEPTH, D_MODEL, IN_PROJ_WIDTH), f32) * D_MODEL ** -0.5
    conv_w = jax.random.normal(ks[3], (DEPTH, CONV_K, CONV_WIDTH), f32) * CONV_K ** -0.5
    spatial_w = jax.random.normal(ks[4], (DEPTH, GMLP_HEADS, CHUNK, CHUNK), f32) * (0.5 * CHUNK ** -0.5)
    spatial_b = 1.0 + 0.02 * jax.random.normal(ks[5], (DEPTH, GMLP_HEADS, CHUNK), f32)
    conv_out_norm_g = 1.0 + 0.02 * jax.random.normal(ks[6], (DEPTH, CONV_WIDTH), f32)
    gmlp_out_norm_g = 1.0 + 0.02 * jax.random.normal(ks[7], (DEPTH, GMLP_WIDTH), f32)
    w_out = jax.random.normal(ks[8], (DEPTH, MIX_WIDTH, D_MODEL), f32) * MIX_WIDTH ** -0.5
    mlp_norm_g = 1.0 + 0.02 * jax.random.normal(ks[9], (DEPTH, D_MODEL), f32)
    w_up = jax.random.normal(ks[10], (DEPTH, D_MODEL, D_FF), f32) * D_MODEL ** -0.5
    w_down = jax.random.normal(ks[11], (DEPTH, D_FF, D_MODEL), f32) * D_FF ** -0.5
    final_norm_g = 1.0 + 0.02 * jax.random.normal(ks[12], (D_MODEL,), f32)
    return {"x": x, "mix_norm_g": mix_norm_g, "w_in": w_in, "conv_w": conv_w,
            "spatial_w": spatial_w, "spatial_b": spatial_b,
            "conv_out_norm_g": conv_out_norm_g, "gmlp_out_norm_g": gmlp_out_norm_g,
            "w_out": w_out, "mlp_norm_g": mlp_norm_g, "w_up": w_up, "w_down": w_down,
            "final_norm_g": final_norm_g}


def reference(x, mix_norm_g, w_in, conv_w, spatial_w, spatial_b, conv_out_norm_g,
              gmlp_out_norm_g, w_out, mlp_norm_g, w_up, w_down, final_norm_g):
    split_points = [CONV_WIDTH, 2 * CONV_WIDTH, 3 * CONV_WIDTH, 3 * CONV_WIDTH + GMLP_WIDTH]
    h = x
    for l in range(DEPTH):
        xn = rmsnorm(h, mix_norm_g[l])
        proj = jnp.einsum('bsd,de->bse', xn, w_in[l])
        b_gate, c_gate, h_in, u, v = jnp.split(proj, split_points, axis=-1)
        y_a = short_conv_mixer(b_gate, c_gate, h_in, conv_w[l])
        y_b = chunked_spatial_gating(jax.nn.gelu(u), jax.nn.gelu(v),
                                     spatial_w[l], spatial_b[l])
        y = jnp.concatenate([rmsnorm(y_a, conv_out_norm_g[l]),
                             rmsnorm(y_b, gmlp_out_norm_g[l])], axis=-1)
        h = h + jnp.einsum('bse,ed->bsd', y, w_out[l])
        xn = rmsnorm(h, mlp_norm_g[l])
        a = jnp.square(jax.nn.relu(jnp.einsum('bsd,df->bsf', xn, w_up[l])))
        h = h + jnp.einsum('bsf,fd->bsd', a, w_down[l])
    return rmsnorm(h, final_norm_g)
```

```python
import numpy as np
from contextlib import ExitStack
import concourse.bass as bass
import concourse.mybir as mybir
from concourse.bass_utils import run_bass_kernel_spmd

F32 = mybir.dt.float32
BF16 = mybir.dt.bfloat16
AF = mybir.ActivationFunctionType
ALU = mybir.AluOpType

D = 4096
T = 512
NTT = 4
NPASS = 2
TOKC = 1024
KC = 32
EIN = 10240
CW = 2048
DFF = 16384
EPS = 1e-5
NSLOT = 4
FB = 1024
NFB = DFF // FB


class Tok:
    __slots__ = ("sem", "val")

    def __init__(self, sem, val):
        self.sem = sem
        self.val = val


class Buf:
    def __init__(self, name):
        self.name = name
        self.w = None
        self.r = {}


class DmaSem:
    def __init__(self, sem):
        self.sem = sem
        self.val = 0


class Prog:
    def __init__(self, nc, es):
        self.nc = nc
        self.q = {e: [] for e in ("pe", "act", "dve", "pool", "sp")}
        self.sems = {}
        for e in ("pe", "act", "dve", "pool"):
            self.sems[e] = es.enter_context(nc.semaphore("c_" + e))
        self.cnt = {e: 0 for e in self.sems}
        self.waited = {}

    def wait(self, eng, tok):
        if tok is None:
            return
        key = (eng, id(tok.sem))
        if self.waited.get(key, 0) >= tok.val:
            return
        self.waited[key] = tok.val
        sem, val = tok.sem, tok.val
        self.q[eng].append(lambda e: e.wait_ge(sem, val))

    def deps(self, eng, reads, writes):
        for b in reads:
            self.wait(eng, b.w)
        for b in writes:
            self.wait(eng, b.w)
            for t in list(b.r.values()):
                self.wait(eng, t)

    def commit(self, tok, reads, writes):
        for b in reads:
            old = b.r.get(id(tok.sem))
            if old is None or old.val < tok.val:
                b.r[id(tok.sem)] = tok
        for b in writes:
            b.w = tok
            b.r = {}

    def fence(self, engs, bufs):
        for e in engs:
            self.deps(e, (), bufs)

    def op(self, eng, fn, reads=(), writes=()):
        self.deps(eng, reads, writes)
        self.cnt[eng] += 1
        sem = self.sems[eng]
        tok = Tok(sem, self.cnt[eng])
        self.q[eng].append(lambda e: fn(e).then_inc(sem, 1))
        self.commit(tok, reads, writes)
        return tok

    def pe(self, fns, reads=(), writes=()):
        self.deps("pe", reads, writes)
        self.cnt["pe"] += 1
        sem = self.sems["pe"]
        tok = Tok(sem, self.cnt["pe"])
        for f in fns[:-1]:
            self.q["pe"].append(f)
        last = fns[-1]
        self.q["pe"].append(lambda e: last(e).then_inc(sem, 1))
        self.commit(tok, reads, writes)
        return tok

    def dma(self, eng, dsem, fn, reads=(), writes=()):
        self.deps(eng, reads, writes)
        dsem.val += 16
        sem = dsem.sem
        tok = Tok(sem, dsem.val)
        self.q[eng].append(lambda e: fn(e).then_inc(sem, 16))
        self.commit(tok, reads, writes)
        return tok


def build_nc(debug=False):
    nc = bass.Bass("TRN2", target_bir_lowering=False)
    dt_in = lambda n, s: nc.dram_tensor(n, s, F32, kind="ExternalInput").ap()
    x = dt_in("x", [TOKC, D])
    xh = dt_in("xh", [NPASS, 2, D])
    w_in = dt_in("w_in", [D, EIN])
    w_out = dt_in("w_out", [D, D])
    w_up = dt_in("w_up", [D, DFF])
    w_down = dt_in("w_down", [DFF, D])
    mix_g = dt_in("mix_g", [D])
    mlp_g = dt_in("mlp_g", [D])
    fin_g = dt_in("fin_g", [D])
    conv_w = dt_in("conv_w", [3, CW])
    sp_w = dt_in("sp_w", [16, 128, 128])
    sp_b = dt_in("sp_b", [16, 128])
    ga_d = dt_in("ga", [CW])
    gb_d = dt_in("gb", [CW])
    out = nc.dram_tensor("out", [TOKC, D], F32, kind="ExternalOutput").ap()

    with ExitStack() as es:
        sb = lambda n, s, d: es.enter_context(nc.sbuf_tensor(n, s, d))
        R1 = sb("R1", [128, KC * 514], BF16)
        R2 = sb("R2", [128, 8192], F32)
        R3 = sb("R3", [128, 16384], F32)
        slots = [sb("ws%d" % i, [128, 8192], BF16) for i in range(NSLOT)]
        ident_f = sb("ident_f", [128, 128], F32)
        ident_b = sb("ident_b", [128, 128], BF16)
        ones_f = sb("ones_f", [128, 128], F32)
        convw = sb("convw", [128, 3, 16], F32)
        ga = sb("ga_sb", [128, 16], F32)
        gb = sb("gb_sb", [128, 16], F32)
        bias_bc = sb("bias_bc", [128, 16 * 128], F32)
        WmT = sb("WmT", [128, 16 * 128], BF16)
        stat = sb("stat", [128, 32], F32)
        inv_tok = sb("inv_tok", [128, 8], F32)
        statp = sb("statp", [128, 128], F32)
        hsave = sb("hsave", [128, 32], F32)
        pst = [es.enter_context(nc.psum_tensor("ps%d" % i, [128, 512], F32)) for i in range(8)]

        P = Prog(nc, es)
        newsem = lambda n: DmaSem(es.enter_context(nc.semaphore(n)))
        s_slot = [newsem("s_slot%d" % i) for i in range(NSLOT)]
        s_x = [newsem("s_x%d" % i) for i in range(NTT)]
        s_o = [newsem("s_o%d" % i) for i in range(NTT)]
        s_h = newsem("s_h")
        s_g = newsem("s_g")
        s_par = [newsem("s_par%d" % i) for i in range(6)]

        dbg = {"n": 0}

        def dump(name, ap, bufs):
            if not debug:
                return
            shp = list(ap.shape)
            dd = nc.dram_tensor(name, shp, ap.dtype, kind="ExternalOutput").ap()
            ds = newsem("s_dbg%d" % dbg["n"])
            dbg["n"] += 1
            t = P.dma("sp", ds, lambda e: e.dma_start(out=dd, in_=ap), reads=bufs)
            P.wait("sp", t)

        x1 = R1[:].rearrange("p (c t) -> p c t", t=514)
        yT = R2[:].bitcast(BF16).rearrange("p (c t) -> p c t", t=512)
        xnb = R2[:, 0:2048].bitcast(BF16)
        gbc = R2[:, 2048:6144]
        aT = [R2[:, 6144:8192].bitcast(BF16).rearrange("p (c t) -> p c t", t=512),
              R2[:, 0:2048].bitcast(BF16).rearrange("p (c t) -> p c t", t=512)]
        rt = [R2[:, 2048:2560], R2[:, 2560:3072]]
        xnbs = [R2[:, 0:2048].bitcast(BF16), R2[:, 6144:8192].bitcast(BF16)]
        R1f = R1[:].bitcast(F32)
        hst = R1f[0:2, 0:4096]
        acc = R3[:].rearrange("p (a d) -> p a d", d=D)
        Xb = pst[7][:].bitcast(BF16)
        Sb = pst[6][:].bitcast(BF16)
        wm3 = WmT[:].rearrange("p (h t) -> p h t", t=128)
        bias3 = bias_bc[:].rearrange("p (h t) -> p h t", t=128)

        def r3(off, n):
            return R3[:, off:off + n]
        tB = []
        o = 0
        for par in range(2):
            d = {}
            for nm, n in (("bS", 512), ("cS", 516), ("hf", 516), ("t1", 512), ("ya", 512),
                          ("sq", 512), ("gu", 512), ("tb", 512), ("yb", 512)):
                d[nm] = r3(o, n)
                o += n
            d["gvT"] = r3(o, 256).bitcast(BF16)
            o += 256
            d["gv"] = r3(o, 256).bitcast(BF16)
            o += 256
            tB.append(d)
        inv_bc = r3(o, 512)
        o += 512
        spw_f = R1f[:, 4096:6144]
        spw_b = R1[:, 12288:14336]

        B_x1 = [Buf("x1_%d" % i) for i in range(NTT)]
        B_x1h = Buf("x1h")
        B_yT = Buf("yT")
        B_xnb = Buf("xnb")
        B_gbc = Buf("gbc")
        B_aT0 = Buf("aT0")
        B_aT = [B_aT0, B_xnb]
        B_acc = [[Buf("acc%d_%d" % (t, c)) for c in range(8)] for t in range(NTT)]
        B_slot = [Buf("slot%d" % i) for i in range(NSLOT)]
        B_ps = [Buf("ps%d" % i) for i in range(8)]
        B_tB = [{k: Buf("tB%d_%s" % (par, k)) for k in tB[par]} for par in range(2)]
        B_invbc = Buf("inv_bc")
        B_stat = Buf("stat")
        B_st = [Buf("st%d" % i) for i in range(16)]
        B_stp = [Buf("stp%d" % i) for i in range(16)]
        B_hst = Buf("hst")
        B_hsave = Buf("hsave")
        B_xn = [B_xnb, B_aT0]
        B_invtok = Buf("inv_tok")
        B_const = Buf("const")
        B_cw = Buf("convw")
        B_gab = Buf("gab")
        B_bias = Buf("bias")
        B_wm = Buf("WmT")
        B_spw = Buf("spw")
        all_tB = [b for dd in B_tB for b in dd.values()] + [B_invbc]
        all_acc = [b for row in B_acc for b in row]
        all_x1 = B_x1 + [B_x1h]

        wstate = {"n": 0}

        pre = []

        def wissue(src_ap, view_fn):
            n = wstate["n"]
            wstate["n"] += 1
            s = n % NSLOT
            dst = view_fn(slots[s])
            P.dma("pool", s_slot[s], lambda e: e.dma_start(out=dst, in_=src_ap), writes=[B_slot[s]])
            return dst, B_slot[s]

        def wtile(src_ap, view_fn, key=None):
            if pre:
                k, r = pre.pop(0)
                assert k == key, (k, key)
                return r
            return wissue(src_ap, view_fn)

        v_in = lambda sl: sl[:].rearrange("p (k n) -> p k n", n=256)
        v_out = lambda sl: sl[:].rearrange("p (k n) -> p k n", n=512)
        v_dn = lambda sl: sl[:].rearrange("p (k n) -> p k n", n=1024)

        for tt in range(NTT):
            P.dma("sp", s_x[tt], lambda e, tt=tt: e.dma_start(out=acc[:, tt, :],
                                                               in_=x[tt * 128:(tt + 1) * 128, :]),
                  writes=B_acc[tt])
        P.dma("sp", s_g, lambda e: e.dma_start(out=gbc, in_=mix_g.partition_broadcast(128)), writes=[B_gbc])
        P.wait("pool", Tok(s_x[1].sem, 16))
        for c0 in (0, CW, 2 * CW, 256):
            pre.append((("in", c0), wissue(w_in[:, c0:c0 + 256].rearrange("(kc p) n -> p kc n", p=128), v_in)))

        pdma = [
            (convw[:], conv_w.rearrange("k (j p) -> p k j", p=128)),
            (ga[:], ga_d.rearrange("(j p) -> p j", p=128)),
            (gb[:], gb_d.rearrange("(j p) -> p j", p=128)),
            (bias_bc[:], sp_b.rearrange("h t -> (h t)").partition_broadcast(128)),
            (spw_f.rearrange("p (h s) -> p h s", s=128), sp_w.rearrange("h t s -> t h s")),
        ]
        for i, (o_ap, i_ap) in enumerate(pdma):
            def f(e, o_ap=o_ap, i_ap=i_ap):
                return e.dma_start(out=o_ap, in_=i_ap, allow_slow_non_contiguous=True)
            P.dma("sp", s_par[i], f, writes=[[B_cw], [B_gab], [B_gab], [B_bias], [B_spw]][i])

        P.op("dve", lambda e: e.memset(ones_f[:], 1.0), writes=[B_const])
        P.op("dve", lambda e: e.memset(ident_f[:], 0.0), writes=[B_const])
        P.op("pool", lambda e: e.affine_select(out=ident_f[:], in_=ident_f[:], pattern=[[-1, 128]],
                                                compare_op=ALU.not_equal, fill=1.0, base=0,
                                                channel_multiplier=1), writes=[B_const])
        P.op("dve", lambda e: e.tensor_copy(out=ident_b[:], in_=ident_f[:]), writes=[B_const])
        P.op("dve", lambda e: e.memset(stat[:], 0.0), writes=[B_stat] + B_st)
        P.op("dve", lambda e: e.memset(statp[:], 0.0), writes=B_stp)
        P.op("pool", lambda e: e.affine_select(out=spw_f.rearrange("p (h s) -> p h s", s=128),
                                                in_=spw_f.rearrange("p (h s) -> p h s", s=128),
                                                pattern=[[0, 16], [-1, 128]], compare_op=ALU.is_ge,
                                                fill=0.0, base=0, channel_multiplier=1),
             writes=[B_spw])
        P.op("dve", lambda e: e.tensor_copy(out=spw_b, in_=spw_f), reads=[B_spw], writes=[B_spw])
        for half in range(2):
            bank, bb = (Xb, B_ps[7]) if half == 0 else (Sb, B_ps[6])
            fns = []
            for i in range(8):
                h = half * 8 + i
                fns.append(lambda e, i=i, h=h, bank=bank: e.transpose(
                    out=bank[:, i * 128:(i + 1) * 128], in_=spw_b[:, h * 128:(h + 1) * 128],
                    identity=ident_b[:]))
            P.pe(fns, reads=[B_spw, B_const], writes=[bb])
            P.op("dve", lambda e, half=half, bank=bank: e.tensor_copy(
                out=WmT[:, half * 1024:(half + 1) * 1024], in_=bank), reads=[bb], writes=[B_wm])

        bankrot = {"m": 0, "d": 0, "c": 0}

        def main_bank():
            i = bankrot["m"] % 4
            bankrot["m"] += 1
            return pst[i], B_ps[i]

        deferred = []

        def defer(n, fn):
            deferred.append([n, fn])

        def tick():
            ready = []
            for it in deferred:
                it[0] -= 1
            for it in list(deferred):
                if it[0] <= 0:
                    deferred.remove(it)
                    ready.append(it[1])
            for f in ready:
                f()

        def flush():
            while deferred:
                tick()

        def stats_sq(src, src_bufs, slot, n=128):
            for k in range(8):
                jb = k % 2
                P.op("act", lambda e, k=k, jb=jb: e.activation(
                    out=pst[jb][0:n, :], in_=src[0:n, k * 512:(k + 1) * 512], func=AF.Square,
                    accum_out=statp[0:n, slot * 8 + k:slot * 8 + k + 1]),
                    reads=src_bufs, writes=[B_ps[jb], B_stp[slot]])

        def sq_block(tt, dc, slot, jb=7):
            P.op("act", lambda e: e.activation(
                out=pst[jb][:], in_=acc[:, tt, dc * 512:(dc + 1) * 512], func=AF.Square,
                accum_out=statp[:, slot * 8 + dc:slot * 8 + dc + 1]),
                reads=[B_acc[tt][dc]], writes=[B_ps[jb], B_stp[slot]])

        def stats_fin(slot, n=128):
            P.op("dve", lambda e: e.reduce_sum(out=stat[0:n, slot:slot + 1], in_=statp[0:n, slot * 8:slot * 8 + 8],
                                               axis=mybir.AxisListType.X),
                 reads=[B_stp[slot]], writes=[B_st[slot]])
            P.op("act", lambda e: e.activation(out=stat[0:n, slot:slot + 1], in_=stat[0:n, slot:slot + 1],
                                               func=AF.Sqrt, scale=1.0 / D, bias=EPS),
                 reads=[B_st[slot]], writes=[B_st[slot]])
            P.op("dve", lambda e: e.reciprocal(out=stat[0:n, slot:slot + 1], in_=stat[0:n, slot:slot + 1]),
                 reads=[B_st[slot]], writes=[B_st[slot]])

        TB = [(pst[7][:].bitcast(BF16), B_ps[7]), (pst[6][:].bitcast(BF16), B_ps[6]),
              (pst[5][:].bitcast(BF16), B_ps[5]), (pst[4][:].bitcast(BF16), B_ps[4])]

        def nt(tt, slot, g_ap, g_buf, extra_w=()):
            src = acc[:, tt, :]
            xb, bxb = xnbs[tt % 2], B_xn[tt % 2]
            P.op("dve", lambda e: e.scalar_tensor_tensor(out=xb, in0=src, scalar=stat[:, slot:slot + 1],
                                                         in1=g_ap, op0=ALU.mult, op1=ALU.mult),
                 reads=B_acc[tt] + [B_st[slot], g_buf], writes=[bxb] + list(extra_w))
            for f4 in range(4):
                bank, bb = TB[f4]
                fns = []
                for i in range(8):
                    c = f4 * 8 + i
                    fns.append(lambda e, i=i, c=c, bank=bank: e.transpose(
                        out=bank[:, i * 128:(i + 1) * 128], in_=xb[:, c * 128:(c + 1) * 128],
                        identity=ident_b[:]))
                P.pe(fns, reads=[bxb, B_const], writes=[bb])
            for f4 in range(4):
                bank, bb = TB[f4]
                dst = x1[:, f4 * 8:(f4 + 1) * 8, 2 + tt * 128:2 + (tt + 1) * 128]
                srcp = bank.rearrange("p (c t) -> p c t", t=128)
                if tt == 3 and f4 % 2 == 1:
                    P.op("act", lambda e, dst=dst, srcp=srcp: e.copy(out=dst, in_=srcp),
                         reads=[bb], writes=[B_x1[tt]])
                else:
                    P.op("dve", lambda e, dst=dst, srcp=srcp: e.tensor_copy(out=dst, in_=srcp),
                         reads=[bb], writes=[B_x1[tt]])

        for ps_ in range(NPASS):
            tok0 = ps_ * T
            if ps_ > 0:
                P.fence(["sp"], all_tB)
                for tt in range(NTT):
                    r0 = tok0 + tt * 128
                    P.dma("sp", s_x[tt], lambda e, tt=tt, r0=r0: e.dma_start(out=acc[:, tt, :],
                                                                             in_=x[r0:r0 + 128, :]),
                          writes=B_acc[tt])
            if ps_ > 0:
                P.dma("sp", s_g, lambda e: e.dma_start(out=gbc, in_=mix_g.partition_broadcast(128)),
                      writes=[B_gbc])
            gm, bgm = gbc, B_gbc
            P.fence(["dve"], [B_yT, B_aT0])
            if ps_ == 0:
                P.dma("sp", s_h, lambda e: e.dma_start(out=hst, in_=xh[0]), writes=[B_hst])
                stats_sq(hst, [B_hst], 8, n=2)
                stats_fin(8, n=2)
                P.op("dve", lambda e, gm=gm: e.scalar_tensor_tensor(out=xnbs[1][0:2, :], in0=hst,
                                                             scalar=stat[0:2, 8:9], in1=gm[0:2, :],
                                                             op0=ALU.mult, op1=ALU.mult),
                     reads=[B_hst, B_st[8], bgm], writes=[B_xn[1]])
                fns = []
                for c in range(KC):
                    fns.append(lambda e, c=c: e.transpose(out=Sb[:, c * 2:(c + 1) * 2],
                                                          in_=xnbs[1][0:2, c * 128:(c + 1) * 128],
                                                          identity=ident_b[0:2, 0:2]))
                P.pe(fns, reads=[B_xn[1], B_const], writes=[B_ps[6]])
                P.op("dve", lambda e: e.tensor_copy(out=x1[:, :, 0:2],
                                                    in_=Sb[:, 0:64].rearrange("p (c t) -> p c t", t=2)),
                     reads=[B_ps[6]], writes=[B_x1h, B_hst])
                P.fence(["act", "dve"], [B_hst, B_spw])
            for tt in range(NTT):
                stats_sq(acc[:, tt, :], B_acc[tt], tt)
                stats_fin(tt)
                nt(tt, tt, gm, bgm)

            if ps_ == 0:
                dump("d_x1", R1[:], all_x1)
            P.fence(["act", "dve"], all_acc + [B_xnb, B_aT0, B_gbc])
            S_bank, B_S = pst[6], B_ps[6]
            Hb = [(pst[4], B_ps[4]), (pst[5], B_ps[5])]
            hcnt = {"n": 0}

            def in_chunk(slot, bslot, i, with_halo):
                hal = None
                if with_halo:
                    hb, bhb = Hb[hcnt["n"] % 2]
                    hcnt["n"] += 1
                    fns = []
                    for kc in range(KC):
                        fns.append(lambda e, kc=kc, hb=hb: e.matmul(
                            hb[:, 0:2], lhsT=slot[:, kc, i * 128:(i + 1) * 128], rhs=x1[:, kc, 0:2],
                            start=(kc == 0), stop=(kc == KC - 1)))
                    P.pe(fns, reads=[bslot, B_x1h], writes=[bhb])
                    hal = (hb[:, 0:2], bhb)
                bank, bb = main_bank()
                fns = []
                for kc in range(KC):
                    fns.append(lambda e, kc=kc, bank=bank: e.matmul(
                        bank[:], lhsT=slot[:, kc, i * 128:(i + 1) * 128], rhs=x1[:, kc, 2:514],
                        start=(kc == 0), stop=(kc == KC - 1)))
                P.pe(fns, reads=[bslot] + B_x1, writes=[bb])
                return bank, bb, hal

            def ssq_mm(sq_ap, bsq, first, last):
                P.pe([lambda e: e.matmul(S_bank[:], lhsT=ones_f[:], rhs=sq_ap, start=first, stop=last)],
                     reads=[bsq, B_const], writes=[B_S])

            def extract_inv(col0):
                P.op("act", lambda e: e.activation(out=inv_bc, in_=S_bank[:], func=AF.Sqrt,
                                                   scale=1.0 / CW, bias=EPS),
                     reads=[B_S], writes=[B_invbc])
                P.op("dve", lambda e: e.reciprocal(out=inv_bc, in_=inv_bc), reads=[B_invbc], writes=[B_invbc])
                bank, bb = main_bank()
                fns = []
                for tt in range(NTT):
                    fns.append(lambda e, tt=tt, bank=bank: e.transpose(
                        out=bank[:, tt * 128:(tt + 1) * 128], in_=inv_bc[:, tt * 128:(tt + 1) * 128],
                        identity=ident_f[:]))
                P.pe(fns, reads=[B_invbc, B_const], writes=[bb])
                P.op("dve", lambda e, bank=bank: e.tensor_copy(out=inv_tok[:, col0:col0 + 4],
                                                               in_=bank[:, 0:512:128]),
                     reads=[bb], writes=[B_invtok])

            for g in range(8):
                for kind, base in (("b", 0), ("c", CW), ("h", 2 * CW)):
                    c0 = base + g * 256
                    slot, bslot = wtile(w_in[:, c0:c0 + 256].rearrange("(kc p) n -> p kc n", p=128), v_in,
                                        key=("in", c0))
                    for i in range(2):
                        j = g * 2 + i
                        tb, btb = tB[i], B_tB[i]
                        bank, bb, hal = in_chunk(slot, bslot, i, kind != "b" and ps_ == 0)
                        if kind == "b":
                            P.op("act", lambda e, tb=tb, bank=bank: e.copy(out=tb["bS"], in_=bank[:]),
                                 reads=[bb], writes=[btb["bS"]])
                        elif kind == "c":
                            P.op("act", lambda e, tb=tb, bank=bank: e.copy(out=tb["cS"][:, 2:514], in_=bank[:]),
                                 reads=[bb], writes=[btb["cS"]])
                            if ps_ == 0:
                                P.op("act", lambda e, tb=tb, hal=hal: e.copy(out=tb["cS"][:, 0:2], in_=hal[0]),
                                     reads=[hal[1]], writes=[btb["cS"]])
                        else:
                            P.op("dve", lambda e, tb=tb, bank=bank: e.tensor_tensor(
                                out=tb["hf"][:, 2:514], in0=bank[:], in1=tb["cS"][:, 2:514], op=ALU.mult),
                                reads=[bb, btb["cS"]], writes=[btb["hf"]])
                            if ps_ == 0:
                                P.op("dve", lambda e, tb=tb, hal=hal: e.tensor_tensor(
                                    out=tb["hf"][:, 0:2], in0=hal[0], in1=tb["cS"][:, 0:2], op=ALU.mult),
                                    reads=[hal[1], btb["cS"]], writes=[btb["hf"]])
                                P.op("dve", lambda e, tb=tb, j=j: e.tensor_copy(
                                    out=hsave[:, 2 * j:2 * j + 2], in_=tb["hf"][:, 512:514]),
                                    reads=[btb["hf"]], writes=[B_hsave])
                            else:
                                P.op("dve", lambda e, tb=tb, j=j: e.tensor_copy(
                                    out=tb["hf"][:, 0:2], in_=hsave[:, 2 * j:2 * j + 2]),
                                    reads=[B_hsave], writes=[btb["hf"]])
                            P.op("dve", lambda e, tb=tb, j=j: e.tensor_scalar(
                                out=tb["t1"], in0=tb["hf"][:, 0:512], scalar1=convw[:, 0, j:j + 1],
                                scalar2=None, op0=ALU.mult),
                                reads=[btb["hf"], B_cw], writes=[btb["t1"]])
                            for k in (1, 2):
                                P.op("dve", lambda e, tb=tb, j=j, k=k: e.scalar_tensor_tensor(
                                    out=tb["t1"], in0=tb["hf"][:, k:k + 512], scalar=convw[:, k, j:j + 1],
                                    in1=tb["t1"], op0=ALU.mult, op1=ALU.add),
                                    reads=[btb["hf"], btb["t1"], B_cw], writes=[btb["t1"]])
                            P.op("dve", lambda e, tb=tb: e.tensor_tensor(
                                out=tb["ya"], in0=tb["t1"], in1=tb["bS"], op=ALU.mult),
                                reads=[btb["t1"], btb["bS"]], writes=[btb["ya"]])
                            P.op("act", lambda e, tb=tb: e.activation(out=tb["sq"], in_=tb["ya"], func=AF.Square),
                                 reads=[btb["ya"]], writes=[btb["sq"]])
                            P.op("act", lambda e, tb=tb, j=j: e.activation(
                                out=yT[:, j, :], in_=tb["ya"], func=AF.Copy, scale=ga[:, j:j + 1]),
                                reads=[btb["ya"], B_gab], writes=[B_yT])
                            defer(3, lambda tb=tb, btb=btb, j=j: ssq_mm(tb["sq"], btb["sq"], j == 0, j == 15))
                        tick()
            flush()
            extract_inv(0)

            for g in range(8):
                for kind, base in (("v", 4 * CW), ("u", 3 * CW)):
                    c0 = base + g * 256
                    slot, bslot = wtile(w_in[:, c0:c0 + 256].rearrange("(kc p) n -> p kc n", p=128), v_in)
                    for i in range(2):
                        hd = g * 2 + i
                        tb, btb = tB[i], B_tB[i]
                        bank, bb, _ = in_chunk(slot, bslot, i, False)
                        if kind == "u":
                            P.op("act", lambda e, tb=tb, bank=bank: e.activation(
                                out=tb["gu"], in_=bank[:], func=AF.Gelu_apprx_tanh),
                                reads=[bb], writes=[btb["gu"]])
                        else:
                            P.op("act", lambda e, tb=tb, bank=bank: e.activation(
                                out=tb["gvT"], in_=bank[:], func=AF.Gelu_apprx_tanh),
                                reads=[bb], writes=[btb["gvT"]])

                            def stage1(tb=tb, btb=btb, hd=hd):
                                fns = []
                                for ck in range(4):
                                    fns.append(lambda e, ck=ck: e.transpose(
                                        out=Xb[:, ck * 128:(ck + 1) * 128],
                                        in_=tb["gvT"][:, ck * 128:(ck + 1) * 128], identity=ident_b[:]))
                                P.pe(fns, reads=[btb["gvT"], B_const], writes=[B_ps[7]])
                                P.op("dve", lambda e: e.tensor_copy(out=tb["gv"], in_=Xb[:, 0:512]),
                                     reads=[B_ps[7]], writes=[btb["gv"]])
                                defer(1, lambda: stage2(tb, btb, hd))

                            def stage2(tb, btb, hd):
                                hs, bhs = Hb[hd % 2]
                                fns = []
                                for ck in range(4):
                                    fns.append(lambda e, ck=ck: e.matmul(
                                        hs[:, ck * 128:(ck + 1) * 128],
                                        lhsT=tb["gv"][:, ck * 128:(ck + 1) * 128], rhs=wm3[:, hd, :],
                                        start=True, stop=True))
                                P.pe(fns, reads=[btb["gv"], B_wm], writes=[bhs])
                                P.op("dve", lambda e: e.tensor_tensor(
                                    out=tb["tb"].rearrange("p (c t) -> p c t", t=128),
                                    in0=hs[:].rearrange("p (c t) -> p c t", t=128),
                                    in1=bias3[:, hd, :].unsqueeze(1).to_broadcast([128, 4, 128]), op=ALU.add),
                                    reads=[bhs, B_bias], writes=[btb["tb"]])
                                P.op("dve", lambda e: e.tensor_tensor(out=tb["yb"], in0=tb["tb"], in1=tb["gu"],
                                                                      op=ALU.mult),
                                     reads=[btb["tb"], btb["gu"]], writes=[btb["yb"]])
                                P.op("act", lambda e: e.activation(out=tb["sq"], in_=tb["yb"], func=AF.Square),
                                     reads=[btb["yb"]], writes=[btb["sq"]])
                                P.op("act", lambda e: e.activation(out=yT[:, 16 + hd, :], in_=tb["yb"],
                                                                   func=AF.Copy, scale=gb[:, hd:hd + 1]),
                                     reads=[btb["yb"], B_gab], writes=[B_yT])
                                defer(1, lambda: ssq_mm(tb["sq"], btb["sq"], hd == 0, hd == 15))

                            defer(2, stage1)
                        tick()
            flush()
            extract_inv(4)

            if ps_ == 0:
                dump("d_yT", R2[:], [B_yT])
                dump("d_inv", inv_tok[:], [B_invtok])
            g2stage = R1f[:, 0:4096]
            P.dma("sp", s_g, lambda e: e.dma_start(out=g2stage, in_=mlp_g.partition_broadcast(128)),
                  writes=all_x1)
            P.fence(["sp"], all_tB)
            for tt in range(NTT):
                r0 = tok0 + tt * 128
                P.dma("sp", s_x[tt], lambda e, tt=tt, r0=r0: e.dma_start(out=acc[:, tt, :],
                                                                         in_=x[r0:r0 + 128, :]),
                      writes=B_acc[tt] + all_tB)
            for dc in range(8):
                for half in range(2):
                    slot, bslot = wtile(
                        w_out[half * CW:(half + 1) * CW, dc * 512:(dc + 1) * 512].rearrange(
                            "(kc p) n -> p kc n", p=128), v_out)
                    for tt in range(NTT):
                        bi = bankrot["c"] % 7
                        bankrot["c"] += 1
                        bank, bb = pst[bi], B_ps[bi]
                        fns = []
                        for cc in range(16):
                            fns.append(lambda e, cc=cc, bank=bank, tt=tt, half=half, slot=slot: e.matmul(
                                bank[:], lhsT=yT[:, half * 16 + cc, tt * 128:(tt + 1) * 128],
                                rhs=slot[:, cc, :], start=(cc == 0), stop=(cc == 15)))
                        P.pe(fns, reads=[bslot, B_yT], writes=[bb])
                        dst = acc[:, tt, dc * 512:(dc + 1) * 512]
                        P.op("dve", lambda e, bank=bank, dst=dst, tt=tt, half=half: e.scalar_tensor_tensor(
                            out=dst, in0=bank[:], scalar=inv_tok[:, half * 4 + tt:half * 4 + tt + 1],
                            in1=dst, op0=ALU.mult, op1=ALU.add),
                            reads=[bb, B_invtok, B_acc[tt][dc]], writes=[B_acc[tt][dc]])
                        if half == 1:
                            sq_block(tt, dc, 4 + tt)

            if ps_ == 0:
                dump("d_h", R3[:], all_acc)
            P.op("act", lambda e: e.copy(out=gbc, in_=g2stage), reads=all_x1, writes=[B_gbc, B_yT])
            g2, bg2 = gbc, B_gbc
            P.fence(["act", "dve"], all_x1)
            for tt in range(NTT):
                stats_fin(4 + tt)
            for tt in range(NTT):
                nt(tt, 4 + tt, g2, bg2, extra_w=[B_yT] if tt < 2 else ())

            if ps_ == 0:
                dump("d_x2", R1[:], all_x1)
            P.fence(["act", "dve"], [B_xnb])
            dbank = {"n": 0}
            upb = {"n": 0}
            P.dma("sp", s_g, lambda e: e.dma_start(out=gbc, in_=fin_g.partition_broadcast(128)),
                  writes=[B_gbc])

            def up_block(fb):
                for u in range(4):
                    c0 = fb * FB + u * 256
                    slot, bslot = wtile(w_up[:, c0:c0 + 256].rearrange("(kc p) n -> p kc n", p=128), v_in)
                    for i in range(2):
                        fcl = u * 2 + i
                        bi = upb["n"] % 3
                        upb["n"] += 1
                        bank, bb = pst[bi], B_ps[bi]
                        fns = []
                        for kc in range(KC):
                            fns.append(lambda e, kc=kc, bank=bank, slot=slot, i=i: e.matmul(
                                bank[:], lhsT=slot[:, kc, i * 128:(i + 1) * 128], rhs=x1[:, kc, 2:514],
                                start=(kc == 0), stop=(kc == KC - 1)))
                        P.pe(fns, reads=[bslot] + B_x1, writes=[bb])
                        P.op("act", lambda e, bank=bank: e.activation(out=pst[3][:], in_=bank[:], func=AF.Relu),
                             reads=[bb], writes=[B_ps[3]])
                        P.op("act", lambda e, fb=fb, fcl=fcl: e.activation(
                            out=aT[fb % 2][:, fcl, :], in_=pst[3][:], func=AF.Square),
                            reads=[B_ps[3]], writes=[B_aT[fb % 2]])

            def down_block(fb):
                for dq in range(4):
                    slot, bslot = wtile(
                        w_down[fb * FB:(fb + 1) * FB, dq * 1024:(dq + 1) * 1024].rearrange(
                            "(fc p) n -> p fc n", p=128), v_dn)
                    for dcl in range(2):
                        dc = dq * 2 + dcl
                        for tt in range(NTT):
                            bi = 4 + dbank["n"] % 4
                            dbank["n"] += 1
                            bank, bb = pst[bi], B_ps[bi]
                            fns = []
                            for fc in range(8):
                                fns.append(lambda e, fc=fc, bank=bank, slot=slot, tt=tt, dcl=dcl, fb=fb: e.matmul(
                                    bank[:], lhsT=aT[fb % 2][:, fc, tt * 128:(tt + 1) * 128],
                                    rhs=slot[:, fc, dcl * 512:(dcl + 1) * 512], start=(fc == 0), stop=(fc == 7)))
                            P.pe(fns, reads=[bslot, B_aT[fb % 2]], writes=[bb])
                            dst = acc[:, tt, dc * 512:(dc + 1) * 512]
                            P.op("dve", lambda e, bank=bank, dst=dst: e.tensor_tensor(
                                out=dst, in0=bank[:], in1=dst, op=ALU.add),
                                reads=[bb, B_acc[tt][dc]], writes=[B_acc[tt][dc]])
                            if fb == NFB - 1:
                                sq_block(tt, dc, 12 + tt, jb=0)

            up_block(0)
            for fb in range(NFB):
                if fb + 1 < NFB:
                    up_block(fb + 1)
                down_block(fb)

            g3, bg3 = gbc, B_gbc
            for tt in range(NTT):
                stats_fin(12 + tt)
                src = acc[:, tt, :]
                P.op("dve", lambda e, src=src, tt=tt, g3=g3: e.scalar_tensor_tensor(
                    out=src, in0=src, scalar=stat[:, 12 + tt:13 + tt], in1=g3, op0=ALU.mult, op1=ALU.mult),
                    reads=B_acc[tt] + [B_st[12 + tt], bg3], writes=B_acc[tt])
                r0 = tok0 + tt * 128
                P.dma("sp", s_o[tt], lambda e, src=src, r0=r0: e.dma_start(out=out[r0:r0 + 128, :], in_=src),
                      reads=B_acc[tt])
            P.op("dve", lambda e: e.memset(stat[:], 0.0), writes=[B_stat] + B_st)
            P.op("dve", lambda e: e.memset(statp[:], 0.0), writes=B_stp)

        for tt in range(NTT):
            P.wait("sp", Tok(s_o[tt].sem, s_o[tt].val))

        with nc.Block() as block:
            @block.sync
            def _(e):
                for f in P.q["sp"]:
                    f(e)

            @block.gpsimd
            def _(e):
                for f in P.q["pool"]:
                    f(e)

            @block.tensor
            def _(e):
                for f in P.q["pe"]:
                    f(e)

            @block.scalar
            def _(e):
                for f in P.q["act"]:
                    f(e)

            @block.vector
            def _(e):
                for f in P.q["dve"]:
                    f(e)
    return nc


def kernel(x, mix_norm_g, w_in, conv_w, spatial_w, spatial_b, conv_out_norm_g, gmlp_out_norm_g,
           w_out, mlp_norm_g, w_up, w_down, final_norm_g):
    f32 = lambda a: np.ascontiguousarray(np.asarray(a, dtype=np.float32))
    x = f32(x)
    Bn, S, Dm = x.shape
    xf = x.reshape(Bn * S, Dm)
    ncores = 8
    shared = {
        "w_in": f32(w_in)[0], "w_out": f32(w_out)[0], "w_up": f32(w_up)[0], "w_down": f32(w_down)[0],
        "mix_g": f32(mix_norm_g)[0], "mlp_g": f32(mlp_norm_g)[0], "fin_g": f32(final_norm_g),
        "conv_w": f32(conv_w)[0], "sp_w": f32(spatial_w)[0], "sp_b": f32(spatial_b)[0],
        "ga": f32(conv_out_norm_g)[0], "gb": f32(gmlp_out_norm_g)[0],
    }
    in_maps = []
    for c in range(ncores):
        r0 = c * TOKC
        xc = xf[r0:r0 + TOKC]
        xh = np.zeros((NPASS, 2, Dm), np.float32)
        if r0 % S != 0:
            xh[0] = xf[r0 - 2:r0]
        xh[1] = xc[T - 2:T]
        m = {"x": np.ascontiguousarray(xc), "xh": xh}
        m.update(shared)
        in_maps.append(m)
    nc = build_nc()
    res = run_bass_kernel_spmd(nc, in_maps, core_ids=list(range(ncores)))
    outs = [np.asarray(r["out"], dtype=np.float32) for r in res.results]
    return np.concatenate(outs, axis=0).reshape(Bn, S, Dm)
```

```python
import numpy as np
from contextlib import ExitStack
import concourse.bass as bass
import concourse.mybir as mybir
from concourse.bass_utils import run_bass_kernel_spmd

F32 = mybir.dt.float32
BF16 = mybir.dt.bfloat16
AF = mybir.ActivationFunctionType
ALU = mybir.AluOpType

D = 4096
T = 512
NTT = 4
NPASS = 2
TOKC = 1024
KC = 32
EIN = 10240
CW = 2048
DFF = 16384
EPS = 1e-5
NSLOT = 4
FB = 1024
NFB = DFF // FB


class Tok:
    __slots__ = ("sem", "val")

    def __init__(self, sem, val):
        self.sem = sem
        self.val = val


class Buf:
    def __init__(self, name):
        self.name = name
        self.w = None
        self.r = {}


class DmaSem:
    def __init__(self, sem):
        self.sem = sem
        self.val = 0


class Prog:
    def __init__(self, nc, es):
        self.nc = nc
        self.q = {e: [] for e in ("pe", "act", "dve", "pool", "sp")}
        self.sems = {}
        for e in ("pe", "act", "dve", "pool"):
            self.sems[e] = es.enter_context(nc.semaphore("c_" + e))
        self.cnt = {e: 0 for e in self.sems}
        self.waited = {}

    def wait(self, eng, tok):
        if tok is None:
            return
        key = (eng, id(tok.sem))
        if self.waited.get(key, 0) >= tok.val:
            return
        self.waited[key] = tok.val
        sem, val = tok.sem, tok.val
        self.q[eng].append(lambda e: e.wait_ge(sem, val))

    def deps(self, eng, reads, writes):
        for b in reads:
            self.wait(eng, b.w)
        for b in writes:
            self.wait(eng, b.w)
            for t in list(b.r.values()):
                self.wait(eng, t)

    def commit(self, tok, reads, writes):
        for b in reads:
            old = b.r.get(id(tok.sem))
            if old is None or old.val < tok.val:
                b.r[id(tok.sem)] = tok
        for b in writes:
            b.w = tok
            b.r = {}

    def fence(self, engs, bufs):
        for e in engs:
            self.deps(e, (), bufs)

    def op(self, eng, fn, reads=(), writes=()):
        self.deps(eng, reads, writes)
        self.cnt[eng] += 1
        sem = self.sems[eng]
        tok = Tok(sem, self.cnt[eng])
        self.q[eng].append(lambda e: fn(e).then_inc(sem, 1))
        self.commit(tok, reads, writes)
        return tok

    def pe(self, fns, reads=(), writes=()):
        self.deps("pe", reads, writes)
        self.cnt["pe"] += 1
        sem = self.sems["pe"]
        tok = Tok(sem, self.cnt["pe"])
        for f in fns[:-1]:
            self.q["pe"].append(f)
        last = fns[-1]
        self.q["pe"].append(lambda e: last(e).then_inc(sem, 1))
        self.commit(tok, reads, writes)
        return tok

    def dma(self, eng, dsem, fn, reads=(), writes=()):
        self.deps(eng, reads, writes)
        dsem.val += 16
        sem = dsem.sem
        tok = Tok(sem, dsem.val)
        self.q[eng].append(lambda e: fn(e).then_inc(sem, 16))
        self.commit(tok, reads, writes)
        return tok


def build_nc(debug=False):
    nc = bass.Bass("TRN2", target_bir_lowering=False)
    dt_in = lambda n, s: nc.dram_tensor(n, s, F32, kind="ExternalInput").ap()
    x = dt_in("x", [TOKC, D])
    xh = dt_in("xh", [NPASS, 2, D])
    w_in = dt_in("w_in", [EIN // 256, 128, 8192])
    w_out = dt_in("w_out", [16, 128, 8192])
    w_up = dt_in("w_up", [DFF // 256, 128, 8192])
    w_down = dt_in("w_down", [64, 128, 8192])
    mix_g = dt_in("mix_g", [D])
    mlp_g = dt_in("mlp_g", [D])
    fin_g = dt_in("fin_g", [D])
    conv_w = dt_in("conv_w", [3, CW])
    sp_w = dt_in("sp_w", [16, 128, 128])
    sp_b = dt_in("sp_b", [16, 128])
    ga_d = dt_in("ga", [CW])
    gb_d = dt_in("gb", [CW])
    out = nc.dram_tensor("out", [TOKC, D], F32, kind="ExternalOutput").ap()

    with ExitStack() as es:
        sb = lambda n, s, d: es.enter_context(nc.sbuf_tensor(n, s, d))
        R1 = sb("R1", [128, KC * 514], BF16)
        R2 = sb("R2", [128, 8192], F32)
        R3 = sb("R3", [128, 16384], F32)
        slots = [sb("ws%d" % i, [128, 8192], BF16) for i in range(NSLOT)]
        ident_f = sb("ident_f", [128, 128], F32)
        ident_b = sb("ident_b", [128, 128], BF16)
        ones_f = sb("ones_f", [128, 128], F32)
        convw = sb("convw", [128, 3, 16], F32)
        ga = sb("ga_sb", [128, 16], F32)
        gb = sb("gb_sb", [128, 16], F32)
        bias_bc = sb("bias_bc", [128, 16 * 128], F32)
        WmT = sb("WmT", [128, 16 * 128], BF16)
        stat = sb("stat", [128, 32], F32)
        inv_tok = sb("inv_tok", [128, 8], F32)
        statp = sb("statp", [128, 128], F32)
        hsave = sb("hsave", [128, 32], F32)
        pst = [es.enter_context(nc.psum_tensor("ps%d" % i, [128, 512], F32)) for i in range(8)]

        P = Prog(nc, es)
        newsem = lambda n: DmaSem(es.enter_context(nc.semaphore(n)))
        s_slot = [newsem("s_slot%d" % i) for i in range(NSLOT)]
        s_x = [newsem("s_x%d" % i) for i in range(NTT)]
        s_o = [newsem("s_o%d" % i) for i in range(NTT)]
        s_h = newsem("s_h")
        s_g = newsem("s_g")
        s_par = [newsem("s_par%d" % i) for i in range(6)]

        dbg = {"n": 0}

        def dump(name, ap, bufs):
            if not debug:
                return
            shp = list(ap.shape)
            dd = nc.dram_tensor(name, shp, ap.dtype, kind="ExternalOutput").ap()
            ds = newsem("s_dbg%d" % dbg["n"])
            dbg["n"] += 1
            t = P.dma("sp", ds, lambda e: e.dma_start(out=dd, in_=ap), reads=bufs)
            P.wait("sp", t)

        x1 = R1[:].rearrange("p (c t) -> p c t", t=514)
        yT = R2[:].bitcast(BF16).rearrange("p (c t) -> p c t", t=512)
        xnb = R2[:, 0:2048].bitcast(BF16)
        gbc = R2[:, 2048:6144]
        aT = [R2[:, 6144:8192].bitcast(BF16).rearrange("p (c t) -> p c t", t=512),
              R2[:, 0:2048].bitcast(BF16).rearrange("p (c t) -> p c t", t=512)]
        rt = [R2[:, 2048:2560], R2[:, 2560:3072]]
        xnbs = [R2[:, 0:2048].bitcast(BF16), R2[:, 6144:8192].bitcast(BF16)]
        R1f = R1[:].bitcast(F32)
        hst = R1f[0:2, 0:4096]
        acc = R3[:].rearrange("p (a d) -> p a d", d=D)
        Xb = pst[7][:].bitcast(BF16)
        Sb = pst[6][:].bitcast(BF16)
        wm3 = WmT[:].rearrange("p (h t) -> p h t", t=128)
        bias3 = bias_bc[:].rearrange("p (h t) -> p h t", t=128)

        def r3(off, n):
            return R3[:, off:off + n]
        tB = []
        o = 0
        for par in range(2):
            d = {}
            for nm, n in (("bS", 512), ("cS", 516), ("hf", 516), ("t1", 512), ("ya", 512),
                          ("sq", 512), ("gu", 512), ("tb", 512), ("yb", 512)):
                d[nm] = r3(o, n)
                o += n
            d["gvT"] = r3(o, 256).bitcast(BF16)
            o += 256
            d["gv"] = r3(o, 256).bitcast(BF16)
            o += 256
            tB.append(d)
        inv_bc = r3(o, 512)
        o += 512
        spw_f = R1f[:, 4096:6144]
        spw_b = R1[:, 12288:14336]

        B_x1 = [Buf("x1_%d" % i) for i in range(NTT)]
        B_x1h = Buf("x1h")
        B_yT = Buf("yT")
        B_xnb = Buf("xnb")
        B_gbc = Buf("gbc")
        B_aT0 = Buf("aT0")
        B_aT = [B_aT0, B_xnb]
        B_acc = [[Buf("acc%d_%d" % (t, c)) for c in range(8)] for t in range(NTT)]
        B_slot = [Buf("slot%d" % i) for i in range(NSLOT)]
        B_ps = [Buf("ps%d" % i) for i in range(8)]
        B_tB = [{k: Buf("tB%d_%s" % (par, k)) for k in tB[par]} for par in range(2)]
        B_invbc = Buf("inv_bc")
        B_stat = Buf("stat")
        B_st = [Buf("st%d" % i) for i in range(16)]
        B_stp = [Buf("stp%d" % i) for i in range(16)]
        B_hst = Buf("hst")
        B_hsave = Buf("hsave")
        B_xn = [B_xnb, B_aT0]
        B_invtok = Buf("inv_tok")
        B_const = Buf("const")
        B_cw = Buf("convw")
        B_gab = Buf("gab")
        B_bias = Buf("bias")
        B_wm = Buf("WmT")
        B_spw = Buf("spw")
        all_tB = [b for dd in B_tB for b in dd.values()] + [B_invbc]
        all_acc = [b for row in B_acc for b in row]
        all_x1 = B_x1 + [B_x1h]

        wstate = {"n": 0}

        pre = []

        def wissue(src_ap, view_fn):
            n = wstate["n"]
            wstate["n"] += 1
            s = n % NSLOT
            dst = view_fn(slots[s])
            flat = slots[s][:]
            P.dma("pool", s_slot[s], lambda e: e.dma_start(out=flat, in_=src_ap), writes=[B_slot[s]])
            return dst, B_slot[s]

        def wtile(src_ap, view_fn, key=None):
            if pre:
                k, r = pre.pop(0)
                assert k == key, (k, key)
                return r
            return wissue(src_ap, view_fn)

        v_in = lambda sl: sl[:].rearrange("p (k n) -> p k n", n=256)
        v_out = lambda sl: sl[:].rearrange("p (k n) -> p k n", n=512)
        v_dn = lambda sl: sl[:].rearrange("p (k n) -> p k n", n=1024)

        for tt in range(NTT):
            P.dma("sp", s_x[tt], lambda e, tt=tt: e.dma_start(out=acc[:, tt, :],
                                                               in_=x[tt * 128:(tt + 1) * 128, :]),
                  writes=B_acc[tt])
        P.dma("sp", s_g, lambda e: e.dma_start(out=gbc, in_=mix_g.partition_broadcast(128)), writes=[B_gbc])
        P.wait("pool", Tok(s_x[1].sem, 16))
        for c0 in (0, CW, 2 * CW, 256):
            pre.append((("in", c0), wissue(w_in[c0 // 256], v_in)))

        pdma = [
            (convw[:], conv_w.rearrange("k (j p) -> p k j", p=128)),
            (ga[:], ga_d.rearrange("(j p) -> p j", p=128)),
            (gb[:], gb_d.rearrange("(j p) -> p j", p=128)),
            (bias_bc[:], sp_b.rearrange("h t -> (h t)").partition_broadcast(128)),
            (spw_f.rearrange("p (h s) -> p h s", s=128), sp_w.rearrange("h t s -> t h s")),
        ]
        for i, (o_ap, i_ap) in enumerate(pdma):
            def f(e, o_ap=o_ap, i_ap=i_ap):
                return e.dma_start(out=o_ap, in_=i_ap, allow_slow_non_contiguous=True)
            P.dma("sp", s_par[i], f, writes=[[B_cw], [B_gab], [B_gab], [B_bias], [B_spw]][i])

        P.op("dve", lambda e: e.memset(ones_f[:], 1.0), writes=[B_const])
        P.op("dve", lambda e: e.memset(ident_f[:], 0.0), writes=[B_const])
        P.op("pool", lambda e: e.affine_select(out=ident_f[:], in_=ident_f[:], pattern=[[-1, 128]],
                                                compare_op=ALU.not_equal, fill=1.0, base=0,
                                                channel_multiplier=1), writes=[B_const])
        P.op("dve", lambda e: e.tensor_copy(out=ident_b[:], in_=ident_f[:]), writes=[B_const])
        P.op("dve", lambda e: e.memset(stat[:], 0.0), writes=[B_stat] + B_st)
        P.op("dve", lambda e: e.memset(statp[:], 0.0), writes=B_stp)
        P.op("pool", lambda e: e.affine_select(out=spw_f.rearrange("p (h s) -> p h s", s=128),
                                                in_=spw_f.rearrange("p (h s) -> p h s", s=128),
                                                pattern=[[0, 16], [-1, 128]], compare_op=ALU.is_ge,
                                                fill=0.0, base=0, channel_multiplier=1),
             writes=[B_spw])
        P.op("dve", lambda e: e.tensor_copy(out=spw_b, in_=spw_f), reads=[B_spw], writes=[B_spw])
        for half in range(2):
            bank, bb = (Xb, B_ps[7]) if half == 0 else (Sb, B_ps[6])
            fns = []
            for i in range(8):
                h = half * 8 + i
                fns.append(lambda e, i=i, h=h, bank=bank: e.transpose(
                    out=bank[:, i * 128:(i + 1) * 128], in_=spw_b[:, h * 128:(h + 1) * 128],
                    identity=ident_b[:]))
            P.pe(fns, reads=[B_spw, B_const], writes=[bb])
            P.op("dve", lambda e, half=half, bank=bank: e.tensor_copy(
                out=WmT[:, half * 1024:(half + 1) * 1024], in_=bank), reads=[bb], writes=[B_wm])

        bankrot = {"m": 0, "d": 0, "c": 0}

        def main_bank():
            i = bankrot["m"] % 4
            bankrot["m"] += 1
            return pst[i], B_ps[i]

        deferred = []

        def defer(n, fn):
            deferred.append([n, fn])

        def tick():
            ready = []
            for it in deferred:
                it[0] -= 1
            for it in list(deferred):
                if it[0] <= 0:
                    deferred.remove(it)
                    ready.append(it[1])
            for f in ready:
                f()

        def flush():
            while deferred:
                tick()

        def stats_sq(src, src_bufs, slot, n=128):
            for k in range(8):
                jb = k % 2
                P.op("act", lambda e, k=k, jb=jb: e.activation(
                    out=pst[jb][0:n, :], in_=src[0:n, k * 512:(k + 1) * 512], func=AF.Square,
                    accum_out=statp[0:n, slot * 8 + k:slot * 8 + k + 1]),
                    reads=src_bufs, writes=[B_ps[jb], B_stp[slot]])

        def sq_block(tt, dc, slot, jb=7):
            P.op("act", lambda e: e.activation(
                out=pst[jb][:], in_=acc[:, tt, dc * 512:(dc + 1) * 512], func=AF.Square,
                accum_out=statp[:, slot * 8 + dc:slot * 8 + dc + 1]),
                reads=[B_acc[tt][dc]], writes=[B_ps[jb], B_stp[slot]])

        def stats_fin(slot, n=128):
            P.op("dve", lambda e: e.reduce_sum(out=stat[0:n, slot:slot + 1], in_=statp[0:n, slot * 8:slot * 8 + 8],
                                               axis=mybir.AxisListType.X),
                 reads=[B_stp[slot]], writes=[B_st[slot]])
            P.op("act", lambda e: e.activation(out=stat[0:n, slot:slot + 1], in_=stat[0:n, slot:slot + 1],
                                               func=AF.Sqrt, scale=1.0 / D, bias=EPS),
                 reads=[B_st[slot]], writes=[B_st[slot]])
            P.op("dve", lambda e: e.reciprocal(out=stat[0:n, slot:slot + 1], in_=stat[0:n, slot:slot + 1]),
                 reads=[B_st[slot]], writes=[B_st[slot]])

        TB = [(pst[7][:].bitcast(BF16), B_ps[7]), (pst[6][:].bitcast(BF16), B_ps[6]),
              (pst[5][:].bitcast(BF16), B_ps[5]), (pst[4][:].bitcast(BF16), B_ps[4])]

        def nt(tt, slot, g_ap, g_buf, extra_w=()):
            src = acc[:, tt, :]
            xb, bxb = xnbs[tt % 2], B_xn[tt % 2]
            P.op("dve", lambda e: e.scalar_tensor_tensor(out=xb, in0=src, scalar=stat[:, slot:slot + 1],
                                                         in1=g_ap, op0=ALU.mult, op1=ALU.mult),
                 reads=B_acc[tt] + [B_st[slot], g_buf], writes=[bxb] + list(extra_w))
            for f4 in range(4):
                bank, bb = TB[f4]
                fns = []
                for i in range(8):
                    c = f4 * 8 + i
                    fns.append(lambda e, i=i, c=c, bank=bank: e.transpose(
                        out=bank[:, i * 128:(i + 1) * 128], in_=xb[:, c * 128:(c + 1) * 128],
                        identity=ident_b[:]))
                P.pe(fns, reads=[bxb, B_const], writes=[bb])
            for f4 in range(4):
                bank, bb = TB[f4]
                dst = x1[:, f4 * 8:(f4 + 1) * 8, 2 + tt * 128:2 + (tt + 1) * 128]
                srcp = bank.rearrange("p (c t) -> p c t", t=128)
                if tt == 3 and f4 % 2 == 1:
                    P.op("act", lambda e, dst=dst, srcp=srcp: e.copy(out=dst, in_=srcp),
                         reads=[bb], writes=[B_x1[tt]])
                else:
                    P.op("dve", lambda e, dst=dst, srcp=srcp: e.tensor_copy(out=dst, in_=srcp),
                         reads=[bb], writes=[B_x1[tt]])

        for ps_ in range(NPASS):
            tok0 = ps_ * T
            if ps_ > 0:
                P.fence(["sp"], all_tB)
                for tt in range(NTT):
                    r0 = tok0 + tt * 128
                    P.dma("sp", s_x[tt], lambda e, tt=tt, r0=r0: e.dma_start(out=acc[:, tt, :],
                                                                             in_=x[r0:r0 + 128, :]),
                          writes=B_acc[tt])
            if ps_ > 0:
                P.dma("sp", s_g, lambda e: e.dma_start(out=gbc, in_=mix_g.partition_broadcast(128)),
                      writes=[B_gbc])
            gm, bgm = gbc, B_gbc
            P.fence(["dve"], [B_yT, B_aT0])
            if ps_ == 0:
                P.dma("sp", s_h, lambda e: e.dma_start(out=hst, in_=xh[0]), writes=[B_hst])
                stats_sq(hst, [B_hst], 8, n=2)
                stats_fin(8, n=2)
                P.op("dve", lambda e, gm=gm: e.scalar_tensor_tensor(out=xnbs[1][0:2, :], in0=hst,
                                                             scalar=stat[0:2, 8:9], in1=gm[0:2, :],
                                                             op0=ALU.mult, op1=ALU.mult),
                     reads=[B_hst, B_st[8], bgm], writes=[B_xn[1]])
                fns = []
                for c in range(KC):
                    fns.append(lambda e, c=c: e.transpose(out=Sb[:, c * 2:(c + 1) * 2],
                                                          in_=xnbs[1][0:2, c * 128:(c + 1) * 128],
                                                          identity=ident_b[0:2, 0:2]))
                P.pe(fns, reads=[B_xn[1], B_const], writes=[B_ps[6]])
                P.op("dve", lambda e: e.tensor_copy(out=x1[:, :, 0:2],
                                                    in_=Sb[:, 0:64].rearrange("p (c t) -> p c t", t=2)),
                     reads=[B_ps[6]], writes=[B_x1h, B_hst])
                P.fence(["act", "dve"], [B_hst, B_spw])
            for tt in range(NTT):
                stats_sq(acc[:, tt, :], B_acc[tt], tt)
                stats_fin(tt)
                nt(tt, tt, gm, bgm)

            if ps_ == 0:
                dump("d_x1", R1[:], all_x1)
            P.fence(["act", "dve"], all_acc + [B_xnb, B_aT0, B_gbc])
            S_bank, B_S = pst[6], B_ps[6]
            Hb = [(pst[4], B_ps[4]), (pst[5], B_ps[5])]
            hcnt = {"n": 0}

            def in_chunk(slot, bslot, i, with_halo):
                hal = None
                if with_halo:
                    hb, bhb = Hb[hcnt["n"] % 2]
                    hcnt["n"] += 1
                    fns = []
                    for kc in range(KC):
                        fns.append(lambda e, kc=kc, hb=hb: e.matmul(
                            hb[:, 0:2], lhsT=slot[:, kc, i * 128:(i + 1) * 128], rhs=x1[:, kc, 0:2],
                            start=(kc == 0), stop=(kc == KC - 1)))
                    P.pe(fns, reads=[bslot, B_x1h], writes=[bhb])
                    hal = (hb[:, 0:2], bhb)
                bank, bb = main_bank()
                fns = []
                for kc in range(KC):
                    fns.append(lambda e, kc=kc, bank=bank: e.matmul(
                        bank[:], lhsT=slot[:, kc, i * 128:(i + 1) * 128], rhs=x1[:, kc, 2:514],
                        start=(kc == 0), stop=(kc == KC - 1)))
                P.pe(fns, reads=[bslot] + B_x1, writes=[bb])
                return bank, bb, hal

            def ssq_mm(sq_ap, bsq, first, last):
                P.pe([lambda e: e.matmul(S_bank[:], lhsT=ones_f[:], rhs=sq_ap, start=first, stop=last)],
                     reads=[bsq, B_const], writes=[B_S])

            def extract_inv(col0):
                P.op("act", lambda e: e.activation(out=inv_bc, in_=S_bank[:], func=AF.Sqrt,
                                                   scale=1.0 / CW, bias=EPS),
                     reads=[B_S], writes=[B_invbc])
                P.op("dve", lambda e: e.reciprocal(out=inv_bc, in_=inv_bc), reads=[B_invbc], writes=[B_invbc])
                bank, bb = main_bank()
                fns = []
                for tt in range(NTT):
                    fns.append(lambda e, tt=tt, bank=bank: e.transpose(
                        out=bank[:, tt * 128:(tt + 1) * 128], in_=inv_bc[:, tt * 128:(tt + 1) * 128],
                        identity=ident_f[:]))
                P.pe(fns, reads=[B_invbc, B_const], writes=[bb])
                P.op("dve", lambda e, bank=bank: e.tensor_copy(out=inv_tok[:, col0:col0 + 4],
                                                               in_=bank[:, 0:512:128]),
                     reads=[bb], writes=[B_invtok])

            for g in range(8):
                for kind, base in (("b", 0), ("c", CW), ("h", 2 * CW)):
                    c0 = base + g * 256
                    slot, bslot = wtile(w_in[c0 // 256], v_in,
                                        key=("in", c0))
                    for i in range(2):
                        j = g * 2 + i
                        tb, btb = tB[i], B_tB[i]
                        bank, bb, hal = in_chunk(slot, bslot, i, kind != "b" and ps_ == 0)
                        if kind == "b":
                            P.op("act", lambda e, tb=tb, bank=bank: e.copy(out=tb["bS"], in_=bank[:]),
                                 reads=[bb], writes=[btb["bS"]])
                        elif kind == "c":
                            P.op("act", lambda e, tb=tb, bank=bank: e.copy(out=tb["cS"][:, 2:514], in_=bank[:]),
                                 reads=[bb], writes=[btb["cS"]])
                            if ps_ == 0:
                                P.op("act", lambda e, tb=tb, hal=hal: e.copy(out=tb["cS"][:, 0:2], in_=hal[0]),
                                     reads=[hal[1]], writes=[btb["cS"]])
                        else:
                            P.op("dve", lambda e, tb=tb, bank=bank: e.tensor_tensor(
                                out=tb["hf"][:, 2:514], in0=bank[:], in1=tb["cS"][:, 2:514], op=ALU.mult),
                                reads=[bb, btb["cS"]], writes=[btb["hf"]])
                            if ps_ == 0:
                                P.op("dve", lambda e, tb=tb, hal=hal: e.tensor_tensor(
                                    out=tb["hf"][:, 0:2], in0=hal[0], in1=tb["cS"][:, 0:2], op=ALU.mult),
                                    reads=[hal[1], btb["cS"]], writes=[btb["hf"]])
                                P.op("dve", lambda e, tb=tb, j=j: e.tensor_copy(
                                    out=hsave[:, 2 * j:2 * j + 2], in_=tb["hf"][:, 512:514]),
                                    reads=[btb["hf"]], writes=[B_hsave])
                            else:
                                P.op("dve", lambda e, tb=tb, j=j: e.tensor_copy(
                                    out=tb["hf"][:, 0:2], in_=hsave[:, 2 * j:2 * j + 2]),
                                    reads=[B_hsave], writes=[btb["hf"]])
                            P.op("dve", lambda e, tb=tb, j=j: e.tensor_scalar(
                                out=tb["t1"], in0=tb["hf"][:, 0:512], scalar1=convw[:, 0, j:j + 1],
                                scalar2=None, op0=ALU.mult),
                                reads=[btb["hf"], B_cw], writes=[btb["t1"]])
                            for k in (1, 2):
                                P.op("dve", lambda e, tb=tb, j=j, k=k: e.scalar_tensor_tensor(
                                    out=tb["t1"], in0=tb["hf"][:, k:k + 512], scalar=convw[:, k, j:j + 1],
                                    in1=tb["t1"], op0=ALU.mult, op1=ALU.add),
                                    reads=[btb["hf"], btb["t1"], B_cw], writes=[btb["t1"]])
                            P.op("dve", lambda e, tb=tb: e.tensor_tensor(
                                out=tb["ya"], in0=tb["t1"], in1=tb["bS"], op=ALU.mult),
                                reads=[btb["t1"], btb["bS"]], writes=[btb["ya"]])
                            P.op("act", lambda e, tb=tb: e.activation(out=tb["sq"], in_=tb["ya"], func=AF.Square),
                                 reads=[btb["ya"]], writes=[btb["sq"]])
                            P.op("act", lambda e, tb=tb, j=j: e.activation(
                                out=yT[:, j, :], in_=tb["ya"], func=AF.Copy, scale=ga[:, j:j + 1]),
                                reads=[btb["ya"], B_gab], writes=[B_yT])
                            defer(3, lambda tb=tb, btb=btb, j=j: ssq_mm(tb["sq"], btb["sq"], j == 0, j == 15))
                        tick()
            flush()
            extract_inv(0)

            for g in range(8):
                for kind, base in (("v", 4 * CW), ("u", 3 * CW)):
                    c0 = base + g * 256
                    slot, bslot = wtile(w_in[c0 // 256], v_in)
                    for i in range(2):
                        hd = g * 2 + i
                        tb, btb = tB[i], B_tB[i]
                        bank, bb, _ = in_chunk(slot, bslot, i, False)
                        if kind == "u":
                            P.op("act", lambda e, tb=tb, bank=bank: e.activation(
                                out=tb["gu"], in_=bank[:], func=AF.Gelu_apprx_tanh),
                                reads=[bb], writes=[btb["gu"]])
                        else:
                            P.op("act", lambda e, tb=tb, bank=bank: e.activation(
                                out=tb["gvT"], in_=bank[:], func=AF.Gelu_apprx_tanh),
                                reads=[bb], writes=[btb["gvT"]])

                            def stage1(tb=tb, btb=btb, hd=hd):
                                fns = []
                                for ck in range(4):
                                    fns.append(lambda e, ck=ck: e.transpose(
                                        out=Xb[:, ck * 128:(ck + 1) * 128],
                                        in_=tb["gvT"][:, ck * 128:(ck + 1) * 128], identity=ident_b[:]))
                                P.pe(fns, reads=[btb["gvT"], B_const], writes=[B_ps[7]])
                                P.op("dve", lambda e: e.tensor_copy(out=tb["gv"], in_=Xb[:, 0:512]),
                                     reads=[B_ps[7]], writes=[btb["gv"]])
                                defer(1, lambda: stage2(tb, btb, hd))

                            def stage2(tb, btb, hd):
                                hs, bhs = Hb[hd % 2]
                                fns = []
                                for ck in range(4):
                                    fns.append(lambda e, ck=ck: e.matmul(
                                        hs[:, ck * 128:(ck + 1) * 128],
                                        lhsT=tb["gv"][:, ck * 128:(ck + 1) * 128], rhs=wm3[:, hd, :],
                                        start=True, stop=True))
                                P.pe(fns, reads=[btb["gv"], B_wm], writes=[bhs])
                                P.op("dve", lambda e: e.tensor_tensor(
                                    out=tb["tb"].rearrange("p (c t) -> p c t", t=128),
                                    in0=hs[:].rearrange("p (c t) -> p c t", t=128),
                                    in1=bias3[:, hd, :].unsqueeze(1).to_broadcast([128, 4, 128]), op=ALU.add),
                                    reads=[bhs, B_bias], writes=[btb["tb"]])
                                P.op("dve", lambda e: e.tensor_tensor(out=tb["yb"], in0=tb["tb"], in1=tb["gu"],
                                                                      op=ALU.mult),
                                     reads=[btb["tb"], btb["gu"]], writes=[btb["yb"]])
                                P.op("act", lambda e: e.activation(out=tb["sq"], in_=tb["yb"], func=AF.Square),
                                     reads=[btb["yb"]], writes=[btb["sq"]])
                                P.op("act", lambda e: e.activation(out=yT[:, 16 + hd, :], in_=tb["yb"],
                                                                   func=AF.Copy, scale=gb[:, hd:hd + 1]),
                                     reads=[btb["yb"], B_gab], writes=[B_yT])
                                defer(1, lambda: ssq_mm(tb["sq"], btb["sq"], hd == 0, hd == 15))

                            defer(2, stage1)
                        tick()
            flush()
            extract_inv(4)

            if ps_ == 0:
                dump("d_yT", R2[:], [B_yT])
                dump("d_inv", inv_tok[:], [B_invtok])
            g2stage = R1f[:, 0:4096]
            P.dma("sp", s_g, lambda e: e.dma_start(out=g2stage, in_=mlp_g.partition_broadcast(128)),
                  writes=all_x1)
            P.fence(["sp"], all_tB)
            for tt in range(NTT):
                r0 = tok0 + tt * 128
                P.dma("sp", s_x[tt], lambda e, tt=tt, r0=r0: e.dma_start(out=acc[:, tt, :],
                                                                         in_=x[r0:r0 + 128, :]),
                      writes=B_acc[tt] + all_tB)
            for dc in range(8):
                for half in range(2):
                    slot, bslot = wtile(w_out[dc * 2 + half], v_out)
                    for tt in range(NTT):
                        bi = bankrot["c"] % 7
                        bankrot["c"] += 1
                        bank, bb = pst[bi], B_ps[bi]
                        fns = []
                        for cc in range(16):
                            fns.append(lambda e, cc=cc, bank=bank, tt=tt, half=half, slot=slot: e.matmul(
                                bank[:], lhsT=yT[:, half * 16 + cc, tt * 128:(tt + 1) * 128],
                                rhs=slot[:, cc, :], start=(cc == 0), stop=(cc == 15)))
                        P.pe(fns, reads=[bslot, B_yT], writes=[bb])
                        dst = acc[:, tt, dc * 512:(dc + 1) * 512]
                        P.op("dve", lambda e, bank=bank, dst=dst, tt=tt, half=half: e.scalar_tensor_tensor(
                            out=dst, in0=bank[:], scalar=inv_tok[:, half * 4 + tt:half * 4 + tt + 1],
                            in1=dst, op0=ALU.mult, op1=ALU.add),
                            reads=[bb, B_invtok, B_acc[tt][dc]], writes=[B_acc[tt][dc]])
                        if half == 1:
                            sq_block(tt, dc, 4 + tt)

            if ps_ == 0:
                dump("d_h", R3[:], all_acc)
            P.op("act", lambda e: e.copy(out=gbc, in_=g2stage), reads=all_x1, writes=[B_gbc, B_yT])
            g2, bg2 = gbc, B_gbc
            P.fence(["act", "dve"], all_x1)
            for tt in range(NTT):
                stats_fin(4 + tt)
            for tt in range(NTT):
                nt(tt, 4 + tt, g2, bg2, extra_w=[B_yT] if tt < 2 else ())

            if ps_ == 0:
                dump("d_x2", R1[:], all_x1)
            P.fence(["act", "dve"], [B_xnb])
            dbank = {"n": 0}
            upb = {"n": 0}
            P.dma("sp", s_g, lambda e: e.dma_start(out=gbc, in_=fin_g.partition_broadcast(128)),
                  writes=[B_gbc])

            def up_block(fb):
                for u in range(4):
                    c0 = fb * FB + u * 256
                    slot, bslot = wtile(w_up[c0 // 256], v_in)
                    for i in range(2):
                        fcl = u * 2 + i
                        bi = upb["n"] % 3
                        upb["n"] += 1
                        bank, bb = pst[bi], B_ps[bi]
                        fns = []
                        for kc in range(KC):
                            fns.append(lambda e, kc=kc, bank=bank, slot=slot, i=i: e.matmul(
                                bank[:], lhsT=slot[:, kc, i * 128:(i + 1) * 128], rhs=x1[:, kc, 2:514],
                                start=(kc == 0), stop=(kc == KC - 1)))
                        P.pe(fns, reads=[bslot] + B_x1, writes=[bb])
                        P.op("act", lambda e, bank=bank: e.activation(out=pst[3][:], in_=bank[:], func=AF.Relu),
                             reads=[bb], writes=[B_ps[3]])
                        P.op("act", lambda e, fb=fb, fcl=fcl: e.activation(
                            out=aT[fb % 2][:, fcl, :], in_=pst[3][:], func=AF.Square),
                            reads=[B_ps[3]], writes=[B_aT[fb % 2]])

            def down_block(fb):
                for dq in range(4):
                    slot, bslot = wtile(w_down[fb * 4 + dq], v_dn)
                    for dcl in range(2):
                        dc = dq * 2 + dcl
                        for tt in range(NTT):
                            bi = 4 + dbank["n"] % 4
                            dbank["n"] += 1
                            bank, bb = pst[bi], B_ps[bi]
                            fns = []
                            for fc in range(8):
                                fns.append(lambda e, fc=fc, bank=bank, slot=slot, tt=tt, dcl=dcl, fb=fb: e.matmul(
                                    bank[:], lhsT=aT[fb % 2][:, fc, tt * 128:(tt + 1) * 128],
                                    rhs=slot[:, fc, dcl * 512:(dcl + 1) * 512], start=(fc == 0), stop=(fc == 7)))
                            P.pe(fns, reads=[bslot, B_aT[fb % 2]], writes=[bb])
                            dst = acc[:, tt, dc * 512:(dc + 1) * 512]
                            P.op("dve", lambda e, bank=bank, dst=dst: e.tensor_tensor(
                                out=dst, in0=bank[:], in1=dst, op=ALU.add),
                                reads=[bb, B_acc[tt][dc]], writes=[B_acc[tt][dc]])
                            if fb == NFB - 1:
                                sq_block(tt, dc, 12 + tt, jb=0)

            up_block(0)
            for fb in range(NFB):
                if fb + 1 < NFB:
                    up_block(fb + 1)
                down_block(fb)

            g3, bg3 = gbc, B_gbc
            for tt in range(NTT):
                stats_fin(12 + tt)
                src = acc[:, tt, :]
                P.op("dve", lambda e, src=src, tt=tt, g3=g3: e.scalar_tensor_tensor(
                    out=src, in0=src, scalar=stat[:, 12 + tt:13 + tt], in1=g3, op0=ALU.mult, op1=ALU.mult),
                    reads=B_acc[tt] + [B_st[12 + tt], bg3], writes=B_acc[tt])
                r0 = tok0 + tt * 128
                P.dma("sp", s_o[tt], lambda e, src=src, r0=r0: e.dma_start(out=out[r0:r0 + 128, :], in_=src),
                      reads=B_acc[tt])
            P.op("dve", lambda e: e.memset(stat[:], 0.0), writes=[B_stat] + B_st)
            P.op("dve", lambda e: e.memset(statp[:], 0.0), writes=B_stp)

        for tt in range(NTT):
            P.wait("sp", Tok(s_o[tt].sem, s_o[tt].val))

        with nc.Block() as block:
            @block.sync
            def _(e):
                for f in P.q["sp"]:
                    f(e)

            @block.gpsimd
            def _(e):
                for f in P.q["pool"]:
                    f(e)

            @block.tensor
            def _(e):
                for f in P.q["pe"]:
                    f(e)

            @block.scalar
            def _(e):
                for f in P.q["act"]:
                    f(e)

            @block.vector
            def _(e):
                for f in P.q["dve"]:
                    f(e)
    return nc


def tile_weights(w_in, w_out, w_up, w_down):
    c = np.ascontiguousarray
    wi = c(w_in.reshape(32, 128, EIN // 256, 256).transpose(2, 1, 0, 3)).reshape(EIN // 256, 128, 8192)
    wu = c(w_up.reshape(32, 128, DFF // 256, 256).transpose(2, 1, 0, 3)).reshape(DFF // 256, 128, 8192)
    wo = c(w_out.reshape(2, 16, 128, 8, 512).transpose(3, 0, 2, 1, 4)).reshape(16, 128, 8192)
    wd = c(w_down.reshape(16, 8, 128, 4, 1024).transpose(0, 3, 2, 1, 4)).reshape(64, 128, 8192)
    return wi, wo, wu, wd


def kernel(x, mix_norm_g, w_in, conv_w, spatial_w, spatial_b, conv_out_norm_g, gmlp_out_norm_g,
           w_out, mlp_norm_g, w_up, w_down, final_norm_g):
    f32 = lambda a: np.ascontiguousarray(np.asarray(a, dtype=np.float32))
    x = f32(x)
    Bn, S, Dm = x.shape
    xf = x.reshape(Bn * S, Dm)
    ncores = 8
    wi, wo, wu, wd = tile_weights(f32(w_in)[0], f32(w_out)[0], f32(w_up)[0], f32(w_down)[0])
    shared = {
        "w_in": wi, "w_out": wo, "w_up": wu, "w_down": wd,
        "mix_g": f32(mix_norm_g)[0], "mlp_g": f32(mlp_norm_g)[0], "fin_g": f32(final_norm_g),
        "conv_w": f32(conv_w)[0], "sp_w": f32(spatial_w)[0], "sp_b": f32(spatial_b)[0],
        "ga": f32(conv_out_norm_g)[0], "gb": f32(gmlp_out_norm_g)[0],
    }
    in_maps = []
    for c in range(ncores):
        r0 = c * TOKC
        xc = xf[r0:r0 + TOKC]
        xh = np.zeros((NPASS, 2, Dm), np.float32)
        if r0 % S != 0:
            xh[0] = xf[r0 - 2:r0]
        xh[1] = xc[T - 2:T]
        m = {"x": np.ascontiguousarray(xc), "xh": xh}
        m.update(shared)
        in_maps.append(m)
    nc = build_nc()
    res = run_bass_kernel_spmd(nc, in_maps, core_ids=list(range(ncores)))
    outs = [np.asarray(r["out"], dtype=np.float32) for r in res.results]
    return np.concatenate(outs, axis=0).reshape(Bn, S, Dm)
```

```python
import numpy as np
from contextlib import ExitStack
import concourse.bass as bass
import concourse.mybir as mybir
from concourse.bass_utils import run_bass_kernel_spmd

F32 = mybir.dt.float32
BF16 = mybir.dt.bfloat16
AF = mybir.ActivationFunctionType
ALU = mybir.AluOpType

D = 4096
T = 512
NTT = 4
NPASS = 2
TOKC = 1024
KC = 32
EIN = 10240
CW = 2048
DFF = 16384
EPS = 1e-5
NSLOT = 4
FB = 1024
NFB = DFF // FB


class Tok:
    __slots__ = ("sem", "val")

    def __init__(self, sem, val):
        self.sem = sem
        self.val = val


class Buf:
    def __init__(self, name):
        self.name = name
        self.w = None
        self.r = {}


class DmaSem:
    def __init__(self, sem):
        self.sem = sem
        self.val = 0


class Prog:
    def __init__(self, nc, es):
        self.nc = nc
        self.q = {e: [] for e in ("pe", "act", "dve", "pool", "sp")}
        self.sems = {}
        for e in ("pe", "act", "dve", "pool"):
            self.sems[e] = es.enter_context(nc.semaphore("c_" + e))
        self.cnt = {e: 0 for e in self.sems}
        self.waited = {}

    def wait(self, eng, tok):
        if tok is None:
            return
        key = (eng, id(tok.sem))
        if self.waited.get(key, 0) >= tok.val:
            return
        self.waited[key] = tok.val
        sem, val = tok.sem, tok.val
        self.q[eng].append(lambda e: e.wait_ge(sem, val))

    def deps(self, eng, reads, writes):
        for b in reads:
            self.wait(eng, b.w)
        for b in writes:
            self.wait(eng, b.w)
            for t in list(b.r.values()):
                self.wait(eng, t)

    def commit(self, tok, reads, writes):
        for b in reads:
            old = b.r.get(id(tok.sem))
            if old is None or old.val < tok.val:
                b.r[id(tok.sem)] = tok
        for b in writes:
            b.w = tok
            b.r = {}

    def fence(self, engs, bufs):
        for e in engs:
            self.deps(e, (), bufs)

    def op(self, eng, fn, reads=(), writes=()):
        self.deps(eng, reads, writes)
        self.cnt[eng] += 1
        sem = self.sems[eng]
        tok = Tok(sem, self.cnt[eng])
        self.q[eng].append(lambda e: fn(e).then_inc(sem, 1))
        self.commit(tok, reads, writes)
        return tok

    def pe(self, fns, reads=(), writes=()):
        self.deps("pe", reads, writes)
        self.cnt["pe"] += 1
        sem = self.sems["pe"]
        tok = Tok(sem, self.cnt["pe"])
        for f in fns[:-1]:
            self.q["pe"].append(f)
        last = fns[-1]
        self.q["pe"].append(lambda e: last(e).then_inc(sem, 1))
        self.commit(tok, reads, writes)
        return tok

    def dma(self, eng, dsem, fn, reads=(), writes=()):
        self.deps(eng, reads, writes)
        dsem.val += 16
        sem = dsem.sem
        tok = Tok(sem, dsem.val)
        self.q[eng].append(lambda e: fn(e).then_inc(sem, 16))
        self.commit(tok, reads, writes)
        return tok


def build_nc(debug=False):
    nc = bass.Bass("TRN2", target_bir_lowering=False)
    dt_in = lambda n, s: nc.dram_tensor(n, s, F32, kind="ExternalInput").ap()
    x = dt_in("x", [TOKC, D])
    xh = dt_in("xh", [NPASS, 2, D])
    w_in = dt_in("w_in", [EIN // 256, 128, 8192])
    w_out = dt_in("w_out", [16, 128, 8192])
    w_up = dt_in("w_up", [DFF // 256, 128, 8192])
    w_down = dt_in("w_down", [64, 128, 8192])
    mix_g = dt_in("mix_g", [D])
    mlp_g = dt_in("mlp_g", [D])
    fin_g = dt_in("fin_g", [D])
    conv_w = dt_in("conv_w", [3, CW])
    sp_w = dt_in("sp_w", [16, 128, 128])
    sp_b = dt_in("sp_b", [16, 128])
    ga_d = dt_in("ga", [CW])
    gb_d = dt_in("gb", [CW])
    out = nc.dram_tensor("out", [TOKC, D], F32, kind="ExternalOutput").ap()

    with ExitStack() as es:
        sb = lambda n, s, d: es.enter_context(nc.sbuf_tensor(n, s, d))
        R1 = sb("R1", [128, KC * 514], BF16)
        R2 = sb("R2", [128, 8192], F32)
        R3 = sb("R3", [128, 16384], F32)
        slots = [sb("ws%d" % i, [128, 8192], BF16) for i in range(NSLOT)]
        ident_f = sb("ident_f", [128, 128], F32)
        ident_b = sb("ident_b", [128, 128], BF16)
        ones_f = sb("ones_f", [128, 128], F32)
        convw = sb("convw", [128, 3, 16], F32)
        ga = sb("ga_sb", [128, 16], F32)
        gb = sb("gb_sb", [128, 16], F32)
        bias_bc = sb("bias_bc", [128, 16 * 128], F32)
        WmT = sb("WmT", [128, 16 * 128], BF16)
        stat = sb("stat", [128, 32], F32)
        inv_tok = sb("inv_tok", [128, 8], F32)
        statp = sb("statp", [128, 128], F32)
        hsave = sb("hsave", [128, 32], F32)
        pst = [es.enter_context(nc.psum_tensor("ps%d" % i, [128, 512], F32)) for i in range(8)]

        P = Prog(nc, es)
        newsem = lambda n: DmaSem(es.enter_context(nc.semaphore(n)))
        s_slot = [newsem("s_slot%d" % i) for i in range(NSLOT)]
        s_x = [newsem("s_x%d" % i) for i in range(NTT)]
        s_o = [newsem("s_o%d" % i) for i in range(NTT)]
        s_h = newsem("s_h")
        s_g = newsem("s_g")
        s_par = [newsem("s_par%d" % i) for i in range(6)]

        dbg = {"n": 0}

        def dump(name, ap, bufs):
            if not debug:
                return
            shp = list(ap.shape)
            dd = nc.dram_tensor(name, shp, ap.dtype, kind="ExternalOutput").ap()
            ds = newsem("s_dbg%d" % dbg["n"])
            dbg["n"] += 1
            t = P.dma("sp", ds, lambda e: e.dma_start(out=dd, in_=ap), reads=bufs)
            P.wait("sp", t)

        x1 = R1[:].rearrange("p (c t) -> p c t", t=514)
        yT = R2[:].bitcast(BF16).rearrange("p (c t) -> p c t", t=512)
        xnb = R2[:, 0:2048].bitcast(BF16)
        gbc = R2[:, 2048:6144]
        aT = [R2[:, 6144:8192].bitcast(BF16).rearrange("p (c t) -> p c t", t=512),
              R2[:, 0:2048].bitcast(BF16).rearrange("p (c t) -> p c t", t=512)]
        rt = [R2[:, 2048:2560], R2[:, 2560:3072]]
        xnbs = [R2[:, 0:2048].bitcast(BF16), R2[:, 6144:8192].bitcast(BF16)]
        R1f = R1[:].bitcast(F32)
        hst = R1f[0:2, 0:4096]
        acc = R3[:].rearrange("p (a d) -> p a d", d=D)
        Xb = pst[7][:].bitcast(BF16)
        Sb = pst[6][:].bitcast(BF16)
        wm3 = WmT[:].rearrange("p (h t) -> p h t", t=128)
        bias3 = bias_bc[:].rearrange("p (h t) -> p h t", t=128)

        def r3(off, n):
            return R3[:, off:off + n]
        tB = []
        o = 0
        for par in range(2):
            d = {}
            for nm, n in (("bS", 512), ("cS", 516), ("hf", 516), ("t1", 512), ("ya", 512),
                          ("sq", 512), ("gu", 512), ("tb", 512), ("yb", 512)):
                d[nm] = r3(o, n)
                o += n
            d["gvT"] = r3(o, 256).bitcast(BF16)
            o += 256
            d["gv"] = r3(o, 256).bitcast(BF16)
            o += 256
            tB.append(d)
        inv_bc = r3(o, 512)
        o += 512
        spw_f = R1f[:, 4096:6144]
        spw_b = R1[:, 12288:14336]

        B_x1 = [Buf("x1_%d" % i) for i in range(NTT)]
        B_x1h = Buf("x1h")
        B_yT = Buf("yT")
        B_xnb = Buf("xnb")
        B_gbc = Buf("gbc")
        B_aT0 = Buf("aT0")
        B_aT = [B_aT0, B_xnb]
        B_acc = [[Buf("acc%d_%d" % (t, c)) for c in range(8)] for t in range(NTT)]
        B_slot = [Buf("slot%d" % i) for i in range(NSLOT)]
        B_ps = [Buf("ps%d" % i) for i in range(8)]
        B_tB = [{k: Buf("tB%d_%s" % (par, k)) for k in tB[par]} for par in range(2)]
        B_invbc = Buf("inv_bc")
        B_stat = Buf("stat")
        B_st = [Buf("st%d" % i) for i in range(16)]
        B_stp = [Buf("stp%d" % i) for i in range(16)]
        B_hst = Buf("hst")
        B_hsave = Buf("hsave")
        B_xn = [B_xnb, B_aT0]
        B_invtok = Buf("inv_tok")
        B_const = Buf("const")
        B_cw = Buf("convw")
        B_gab = Buf("ga")
        B_gbb = Buf("gb")
        B_bias = Buf("bias")
        B_wm = Buf("WmT")
        B_spw = Buf("spw")
        all_tB = [b for dd in B_tB for b in dd.values()] + [B_invbc]
        all_acc = [b for row in B_acc for b in row]
        all_x1 = B_x1 + [B_x1h]

        wstate = {"n": 0}

        pre = []

        def wissue(src_ap, view_fn):
            n = wstate["n"]
            wstate["n"] += 1
            s = n % NSLOT
            dst = view_fn(slots[s])
            flat = slots[s][:]
            P.dma("pool", s_slot[s], lambda e: e.dma_start(out=flat, in_=src_ap), writes=[B_slot[s]])
            return dst, B_slot[s]

        def wtile(src_ap, view_fn, key=None):
            if pre:
                k, r = pre.pop(0)
                assert k == key, (k, key)
                return r
            return wissue(src_ap, view_fn)

        v_in = lambda sl: sl[:].rearrange("p (k n) -> p k n", n=256)
        v_out = lambda sl: sl[:].rearrange("p (k n) -> p k n", n=512)
        v_dn = lambda sl: sl[:].rearrange("p (k n) -> p k n", n=1024)

        for tt in range(NTT):
            P.dma("sp", s_x[tt], lambda e, tt=tt: e.dma_start(out=acc[:, tt, :],
                                                               in_=x[tt * 128:(tt + 1) * 128, :]),
                  writes=B_acc[tt])
        P.dma("sp", s_h, lambda e: e.dma_start(out=hst, in_=xh[0]), writes=[B_hst])
        P.dma("sp", s_g, lambda e: e.dma_start(out=gbc, in_=mix_g.partition_broadcast(128)), writes=[B_gbc])
        P.wait("pool", Tok(s_x[1].sem, 16))
        for c0 in (0, CW, 2 * CW, 256):
            pre.append((("in", c0), wissue(w_in[c0 // 256], v_in)))

        pdma = [
            (convw[:], conv_w.rearrange("k (j p) -> p k j", p=128)),
            (ga[:], ga_d.rearrange("(j p) -> p j", p=128)),
            (gb[:], gb_d.rearrange("(j p) -> p j", p=128)),
            (bias_bc[:], sp_b.rearrange("h t -> (h t)").partition_broadcast(128)),
            (spw_f.rearrange("p (h s) -> p h s", s=128), sp_w.rearrange("h t s -> t h s")),
        ]
        for i, (o_ap, i_ap) in enumerate(pdma):
            def f(e, o_ap=o_ap, i_ap=i_ap):
                return e.dma_start(out=o_ap, in_=i_ap, allow_slow_non_contiguous=True)
            P.dma("sp", s_par[i], f, writes=[[B_cw], [B_gab], [B_gbb], [B_bias], [B_spw]][i])

        P.op("dve", lambda e: e.memset(ones_f[:], 1.0), writes=[B_const])
        P.op("dve", lambda e: e.memset(ident_f[:], 0.0), writes=[B_const])
        P.op("pool", lambda e: e.affine_select(out=ident_f[:], in_=ident_f[:], pattern=[[-1, 128]],
                                                compare_op=ALU.not_equal, fill=1.0, base=0,
                                                channel_multiplier=1), writes=[B_const])
        P.op("dve", lambda e: e.tensor_copy(out=ident_b[:], in_=ident_f[:]), writes=[B_const])
        P.op("dve", lambda e: e.memset(stat[:], 0.0), writes=[B_stat] + B_st)
        P.op("dve", lambda e: e.memset(statp[:], 0.0), writes=B_stp)
        P.op("pool", lambda e: e.affine_select(out=spw_f.rearrange("p (h s) -> p h s", s=128),
                                                in_=spw_f.rearrange("p (h s) -> p h s", s=128),
                                                pattern=[[0, 16], [-1, 128]], compare_op=ALU.is_ge,
                                                fill=0.0, base=0, channel_multiplier=1),
             writes=[B_spw])
        P.op("dve", lambda e: e.tensor_copy(out=spw_b, in_=spw_f), reads=[B_spw], writes=[B_spw])
        for half in range(2):
            bank, bb = (Xb, B_ps[7]) if half == 0 else (Sb, B_ps[6])
            fns = []
            for i in range(8):
                h = half * 8 + i
                fns.append(lambda e, i=i, h=h, bank=bank: e.transpose(
                    out=bank[:, i * 128:(i + 1) * 128], in_=spw_b[:, h * 128:(h + 1) * 128],
                    identity=ident_b[:]))
            P.pe(fns, reads=[B_spw, B_const], writes=[bb])
            P.op("dve", lambda e, half=half, bank=bank: e.tensor_copy(
                out=WmT[:, half * 1024:(half + 1) * 1024], in_=bank), reads=[bb], writes=[B_wm])

        bankrot = {"m": 0, "d": 0, "c": 0}

        def main_bank():
            i = bankrot["m"] % 4
            bankrot["m"] += 1
            return pst[i], B_ps[i]

        deferred = []

        def defer(n, fn):
            deferred.append([n, fn])

        def tick():
            ready = []
            for it in deferred:
                it[0] -= 1
            for it in list(deferred):
                if it[0] <= 0:
                    deferred.remove(it)
                    ready.append(it[1])
            for f in ready:
                f()

        def flush():
            while deferred:
                tick()

        def stats_sq(src, src_bufs, slot, n=128):
            for k in range(8):
                jb = k % 2
                P.op("act", lambda e, k=k, jb=jb: e.activation(
                    out=pst[jb][0:n, :], in_=src[0:n, k * 512:(k + 1) * 512], func=AF.Square,
                    accum_out=statp[0:n, slot * 8 + k:slot * 8 + k + 1]),
                    reads=src_bufs, writes=[B_ps[jb], B_stp[slot]])

        def sq_block(tt, dc, slot, jb=7):
            P.op("act", lambda e: e.activation(
                out=pst[jb][:], in_=acc[:, tt, dc * 512:(dc + 1) * 512], func=AF.Square,
                accum_out=statp[:, slot * 8 + dc:slot * 8 + dc + 1]),
                reads=[B_acc[tt][dc]], writes=[B_ps[jb], B_stp[slot]])

        def stats_fin(slot, n=128):
            P.op("dve", lambda e: e.reduce_sum(out=stat[0:n, slot:slot + 1], in_=statp[0:n, slot * 8:slot * 8 + 8],
                                               axis=mybir.AxisListType.X),
                 reads=[B_stp[slot]], writes=[B_st[slot]])
            P.op("act", lambda e: e.activation(out=stat[0:n, slot:slot + 1], in_=stat[0:n, slot:slot + 1],
                                               func=AF.Sqrt, scale=1.0 / D, bias=EPS),
                 reads=[B_st[slot]], writes=[B_st[slot]])
            P.op("dve", lambda e: e.reciprocal(out=stat[0:n, slot:slot + 1], in_=stat[0:n, slot:slot + 1]),
                 reads=[B_st[slot]], writes=[B_st[slot]])

        TB = [(pst[7][:].bitcast(BF16), B_ps[7]), (pst[6][:].bitcast(BF16), B_ps[6]),
              (pst[5][:].bitcast(BF16), B_ps[5]), (pst[4][:].bitcast(BF16), B_ps[4])]

        def nt(tt, slot, g_ap, g_buf, extra_w=()):
            src = acc[:, tt, :]
            xb, bxb = xnbs[tt % 2], B_xn[tt % 2]
            P.op("dve", lambda e: e.scalar_tensor_tensor(out=xb, in0=src, scalar=stat[:, slot:slot + 1],
                                                         in1=g_ap, op0=ALU.mult, op1=ALU.mult),
                 reads=B_acc[tt] + [B_st[slot], g_buf], writes=[bxb] + list(extra_w))
            for f4 in range(4):
                bank, bb = TB[f4]
                fns = []
                for i in range(8):
                    c = f4 * 8 + i
                    fns.append(lambda e, i=i, c=c, bank=bank: e.transpose(
                        out=bank[:, i * 128:(i + 1) * 128], in_=xb[:, c * 128:(c + 1) * 128],
                        identity=ident_b[:]))
                P.pe(fns, reads=[bxb, B_const], writes=[bb])
            for f4 in range(4):
                bank, bb = TB[f4]
                dst = x1[:, f4 * 8:(f4 + 1) * 8, 2 + tt * 128:2 + (tt + 1) * 128]
                srcp = bank.rearrange("p (c t) -> p c t", t=128)
                if tt == 3 and f4 % 2 == 1:
                    P.op("act", lambda e, dst=dst, srcp=srcp: e.copy(out=dst, in_=srcp),
                         reads=[bb], writes=[B_x1[tt]])
                else:
                    P.op("dve", lambda e, dst=dst, srcp=srcp: e.tensor_copy(out=dst, in_=srcp),
                         reads=[bb], writes=[B_x1[tt]])

        for ps_ in range(NPASS):
            tok0 = ps_ * T
            if ps_ > 0:
                P.fence(["sp"], all_tB)
                for tt in range(NTT):
                    r0 = tok0 + tt * 128
                    P.dma("sp", s_x[tt], lambda e, tt=tt, r0=r0: e.dma_start(out=acc[:, tt, :],
                                                                             in_=x[r0:r0 + 128, :]),
                          writes=B_acc[tt])
            if ps_ > 0:
                P.dma("sp", s_g, lambda e: e.dma_start(out=gbc, in_=mix_g.partition_broadcast(128)),
                      writes=[B_gbc])
            gm, bgm = gbc, B_gbc
            P.fence(["dve"], [B_yT, B_aT0])
            if ps_ == 0:
                stats_sq(hst, [B_hst], 8, n=2)
                stats_fin(8, n=2)
                P.op("dve", lambda e, gm=gm: e.scalar_tensor_tensor(out=xnbs[1][0:2, :], in0=hst,
                                                             scalar=stat[0:2, 8:9], in1=gm[0:2, :],
                                                             op0=ALU.mult, op1=ALU.mult),
                     reads=[B_hst, B_st[8], bgm], writes=[B_xn[1]])
                fns = []
                for c in range(KC):
                    fns.append(lambda e, c=c: e.transpose(out=Sb[:, c * 2:(c + 1) * 2],
                                                          in_=xnbs[1][0:2, c * 128:(c + 1) * 128],
                                                          identity=ident_b[0:2, 0:2]))
                P.pe(fns, reads=[B_xn[1], B_const], writes=[B_ps[6]])
                P.op("dve", lambda e: e.tensor_copy(out=x1[:, :, 0:2],
                                                    in_=Sb[:, 0:64].rearrange("p (c t) -> p c t", t=2)),
                     reads=[B_ps[6]], writes=[B_x1h, B_hst])
                P.fence(["act", "dve"], [B_hst, B_spw])
            for tt in range(NTT):
                stats_sq(acc[:, tt, :], B_acc[tt], tt)
                stats_fin(tt)
                nt(tt, tt, gm, bgm)

            if ps_ == 0:
                dump("d_x1", R1[:], all_x1)
            P.fence(["act", "dve"], all_acc + [B_xnb, B_aT0, B_gbc])
            S_bank, B_S = pst[6], B_ps[6]
            Hb = [(pst[4], B_ps[4]), (pst[5], B_ps[5])]
            hcnt = {"n": 0}

            def in_chunk(slot, bslot, i, with_halo):
                hal = None
                if with_halo:
                    hb, bhb = Hb[hcnt["n"] % 2]
                    hcnt["n"] += 1
                    fns = []
                    for kc in range(KC):
                        fns.append(lambda e, kc=kc, hb=hb: e.matmul(
                            hb[:, 0:2], lhsT=slot[:, kc, i * 128:(i + 1) * 128], rhs=x1[:, kc, 0:2],
                            start=(kc == 0), stop=(kc == KC - 1)))
                    P.pe(fns, reads=[bslot, B_x1h], writes=[bhb])
                    hal = (hb[:, 0:2], bhb)
                bank, bb = main_bank()
                fns = []
                for kc in range(KC):
                    fns.append(lambda e, kc=kc, bank=bank: e.matmul(
                        bank[:], lhsT=slot[:, kc, i * 128:(i + 1) * 128], rhs=x1[:, kc, 2:514],
                        start=(kc == 0), stop=(kc == KC - 1)))
                P.pe(fns, reads=[bslot] + B_x1, writes=[bb])
                return bank, bb, hal

            def ssq_mm(sq_ap, bsq, first, last):
                P.pe([lambda e: e.matmul(S_bank[:], lhsT=ones_f[:], rhs=sq_ap, start=first, stop=last)],
                     reads=[bsq, B_const], writes=[B_S])

            def extract_inv(col0):
                P.op("act", lambda e: e.activation(out=inv_bc, in_=S_bank[:], func=AF.Sqrt,
                                                   scale=1.0 / CW, bias=EPS),
                     reads=[B_S], writes=[B_invbc])
                P.op("dve", lambda e: e.reciprocal(out=inv_bc, in_=inv_bc), reads=[B_invbc], writes=[B_invbc])
                bank, bb = main_bank()
                fns = []
                for tt in range(NTT):
                    fns.append(lambda e, tt=tt, bank=bank: e.transpose(
                        out=bank[:, tt * 128:(tt + 1) * 128], in_=inv_bc[:, tt * 128:(tt + 1) * 128],
                        identity=ident_f[:]))
                P.pe(fns, reads=[B_invbc, B_const], writes=[bb])
                P.op("dve", lambda e, bank=bank: e.tensor_copy(out=inv_tok[:, col0:col0 + 4],
                                                               in_=bank[:, 0:512:128]),
                     reads=[bb], writes=[B_invtok])

            for g in range(8):
                for kind, base in (("b", 0), ("c", CW), ("h", 2 * CW)):
                    c0 = base + g * 256
                    slot, bslot = wtile(w_in[c0 // 256], v_in,
                                        key=("in", c0))
                    for i in range(2):
                        j = g * 2 + i
                        tb, btb = tB[i], B_tB[i]
                        bank, bb, hal = in_chunk(slot, bslot, i, kind != "b" and ps_ == 0)
                        if kind == "b":
                            P.op("act", lambda e, tb=tb, bank=bank: e.copy(out=tb["bS"], in_=bank[:]),
                                 reads=[bb], writes=[btb["bS"]])
                        elif kind == "c":
                            P.op("act", lambda e, tb=tb, bank=bank: e.copy(out=tb["cS"][:, 2:514], in_=bank[:]),
                                 reads=[bb], writes=[btb["cS"]])
                            if ps_ == 0:
                                P.op("act", lambda e, tb=tb, hal=hal: e.copy(out=tb["cS"][:, 0:2], in_=hal[0]),
                                     reads=[hal[1]], writes=[btb["cS"]])
                        else:
                            P.op("dve", lambda e, tb=tb, bank=bank: e.tensor_tensor(
                                out=tb["hf"][:, 2:514], in0=bank[:], in1=tb["cS"][:, 2:514], op=ALU.mult),
                                reads=[bb, btb["cS"]], writes=[btb["hf"]])
                            if ps_ == 0:
                                P.op("dve", lambda e, tb=tb, hal=hal: e.tensor_tensor(
                                    out=tb["hf"][:, 0:2], in0=hal[0], in1=tb["cS"][:, 0:2], op=ALU.mult),
                                    reads=[hal[1], btb["cS"]], writes=[btb["hf"]])
                                P.op("dve", lambda e, tb=tb, j=j: e.tensor_copy(
                                    out=hsave[:, 2 * j:2 * j + 2], in_=tb["hf"][:, 512:514]),
                                    reads=[btb["hf"]], writes=[B_hsave])
                            else:
                                P.op("dve", lambda e, tb=tb, j=j: e.tensor_copy(
                                    out=tb["hf"][:, 0:2], in_=hsave[:, 2 * j:2 * j + 2]),
                                    reads=[B_hsave], writes=[btb["hf"]])
                            P.op("dve", lambda e, tb=tb, j=j: e.tensor_scalar(
                                out=tb["t1"], in0=tb["hf"][:, 0:512], scalar1=convw[:, 0, j:j + 1],
                                scalar2=None, op0=ALU.mult),
                                reads=[btb["hf"], B_cw], writes=[btb["t1"]])
                            for k in (1, 2):
                                P.op("dve", lambda e, tb=tb, j=j, k=k: e.scalar_tensor_tensor(
                                    out=tb["t1"], in0=tb["hf"][:, k:k + 512], scalar=convw[:, k, j:j + 1],
                                    in1=tb["t1"], op0=ALU.mult, op1=ALU.add),
                                    reads=[btb["hf"], btb["t1"], B_cw], writes=[btb["t1"]])
                            P.op("dve", lambda e, tb=tb: e.tensor_tensor(
                                out=tb["ya"], in0=tb["t1"], in1=tb["bS"], op=ALU.mult),
                                reads=[btb["t1"], btb["bS"]], writes=[btb["ya"]])
                            P.op("act", lambda e, tb=tb: e.activation(out=tb["sq"], in_=tb["ya"], func=AF.Square),
                                 reads=[btb["ya"]], writes=[btb["sq"]])
                            P.op("act", lambda e, tb=tb, j=j: e.activation(
                                out=yT[:, j, :], in_=tb["ya"], func=AF.Copy, scale=ga[:, j:j + 1]),
                                reads=[btb["ya"], B_gab], writes=[B_yT])
                            defer(3, lambda tb=tb, btb=btb, j=j: ssq_mm(tb["sq"], btb["sq"], j == 0, j == 15))
                        tick()
            defer(4, lambda: extract_inv(0))

            for g in range(8):
                for kind, base in (("v", 4 * CW), ("u", 3 * CW)):
                    c0 = base + g * 256
                    slot, bslot = wtile(w_in[c0 // 256], v_in)
                    for i in range(2):
                        hd = g * 2 + i
                        tb, btb = tB[i], B_tB[i]
                        bank, bb, _ = in_chunk(slot, bslot, i, False)
                        if kind == "u":
                            P.op("act", lambda e, tb=tb, bank=bank: e.activation(
                                out=tb["gu"], in_=bank[:], func=AF.Gelu_apprx_tanh),
                                reads=[bb], writes=[btb["gu"]])
                        else:
                            P.op("act", lambda e, tb=tb, bank=bank: e.activation(
                                out=tb["gvT"], in_=bank[:], func=AF.Gelu_apprx_tanh),
                                reads=[bb], writes=[btb["gvT"]])

                            def stage1(tb=tb, btb=btb, hd=hd):
                                fns = []
                                for ck in range(4):
                                    fns.append(lambda e, ck=ck: e.transpose(
                                        out=Xb[:, ck * 128:(ck + 1) * 128],
                                        in_=tb["gvT"][:, ck * 128:(ck + 1) * 128], identity=ident_b[:]))
                                P.pe(fns, reads=[btb["gvT"], B_const], writes=[B_ps[7]])
                                P.op("dve", lambda e: e.tensor_copy(out=tb["gv"], in_=Xb[:, 0:512]),
                                     reads=[B_ps[7]], writes=[btb["gv"]])
                                defer(1, lambda: stage2(tb, btb, hd))

                            def stage2(tb, btb, hd):
                                hs, bhs = Hb[hd % 2]
                                fns = []
                                for ck in range(4):
                                    fns.append(lambda e, ck=ck: e.matmul(
                                        hs[:, ck * 128:(ck + 1) * 128],
                                        lhsT=tb["gv"][:, ck * 128:(ck + 1) * 128], rhs=wm3[:, hd, :],
                                        start=True, stop=True))
                                P.pe(fns, reads=[btb["gv"], B_wm], writes=[bhs])
                                P.op("dve", lambda e: e.tensor_tensor(
                                    out=tb["tb"].rearrange("p (c t) -> p c t", t=128),
                                    in0=hs[:].rearrange("p (c t) -> p c t", t=128),
                                    in1=bias3[:, hd, :].unsqueeze(1).to_broadcast([128, 4, 128]), op=ALU.add),
                                    reads=[bhs, B_bias], writes=[btb["tb"]])
                                P.op("dve", lambda e: e.tensor_tensor(out=tb["yb"], in0=tb["tb"], in1=tb["gu"],
                                                                      op=ALU.mult),
                                     reads=[btb["tb"], btb["gu"]], writes=[btb["yb"]])
                                P.op("act", lambda e: e.activation(out=tb["sq"], in_=tb["yb"], func=AF.Square),
                                     reads=[btb["yb"]], writes=[btb["sq"]])
                                P.op("act", lambda e: e.activation(out=yT[:, 16 + hd, :], in_=tb["yb"],
                                                                   func=AF.Copy, scale=gb[:, hd:hd + 1]),
                                     reads=[btb["yb"], B_gbb], writes=[B_yT])
                                defer(1, lambda: ssq_mm(tb["sq"], btb["sq"], hd == 0, hd == 15))

                            defer(2, stage1)
                        tick()
            flush()
            extract_inv(4)

            if ps_ == 0:
                dump("d_yT", R2[:], [B_yT])
                dump("d_inv", inv_tok[:], [B_invtok])
            g2stage = R1f[:, 0:4096]
            P.dma("sp", s_g, lambda e: e.dma_start(out=g2stage, in_=mlp_g.partition_broadcast(128)),
                  writes=all_x1)
            P.fence(["sp"], all_tB)
            for tt in range(NTT):
                r0 = tok0 + tt * 128
                P.dma("sp", s_x[tt], lambda e, tt=tt, r0=r0: e.dma_start(out=acc[:, tt, :],
                                                                         in_=x[r0:r0 + 128, :]),
                      writes=B_acc[tt] + all_tB)
            for dc in range(8):
                for half in range(2):
                    slot, bslot = wtile(w_out[dc * 2 + half], v_out)
                    for tt in range(NTT):
                        bi = bankrot["c"] % 7
                        bankrot["c"] += 1
                        bank, bb = pst[bi], B_ps[bi]
                        fns = []
                        for cc in range(16):
                            fns.append(lambda e, cc=cc, bank=bank, tt=tt, half=half, slot=slot: e.matmul(
                                bank[:], lhsT=yT[:, half * 16 + cc, tt * 128:(tt + 1) * 128],
                                rhs=slot[:, cc, :], start=(cc == 0), stop=(cc == 15)))
                        P.pe(fns, reads=[bslot, B_yT], writes=[bb])
                        dst = acc[:, tt, dc * 512:(dc + 1) * 512]
                        P.op("dve", lambda e, bank=bank, dst=dst, tt=tt, half=half: e.scalar_tensor_tensor(
                            out=dst, in0=bank[:], scalar=inv_tok[:, half * 4 + tt:half * 4 + tt + 1],
                            in1=dst, op0=ALU.mult, op1=ALU.add),
                            reads=[bb, B_invtok, B_acc[tt][dc]], writes=[B_acc[tt][dc]])
                        if half == 1:
                            sq_block(tt, dc, 4 + tt)

            if ps_ == 0:
                dump("d_h", R3[:], all_acc)
            P.op("act", lambda e: e.copy(out=gbc, in_=g2stage), reads=all_x1, writes=[B_gbc, B_yT])
            g2, bg2 = gbc, B_gbc
            P.fence(["act", "dve"], all_x1)
            for tt in range(NTT):
                stats_fin(4 + tt)
            for tt in range(NTT):
                nt(tt, 4 + tt, g2, bg2, extra_w=[B_yT] if tt < 2 else ())

            if ps_ == 0:
                dump("d_x2", R1[:], all_x1)
            P.fence(["act", "dve"], [B_xnb])
            dbank = {"n": 0}
            upb = {"n": 0}
            P.dma("sp", s_g, lambda e: e.dma_start(out=gbc, in_=fin_g.partition_broadcast(128)),
                  writes=[B_gbc])

            def up_block(fb):
                for u in range(4):
                    c0 = fb * FB + u * 256
                    slot, bslot = wtile(w_up[c0 // 256], v_in)
                    for i in range(2):
                        fcl = u * 2 + i
                        bi = upb["n"] % 3
                        upb["n"] += 1
                        bank, bb = pst[bi], B_ps[bi]
                        fns = []
                        for kc in range(KC):
                            fns.append(lambda e, kc=kc, bank=bank, slot=slot, i=i: e.matmul(
                                bank[:], lhsT=slot[:, kc, i * 128:(i + 1) * 128], rhs=x1[:, kc, 2:514],
                                start=(kc == 0), stop=(kc == KC - 1)))
                        P.pe(fns, reads=[bslot] + B_x1, writes=[bb])
                        P.op("act", lambda e, bank=bank: e.activation(out=pst[3][:], in_=bank[:], func=AF.Relu),
                             reads=[bb], writes=[B_ps[3]])
                        P.op("act", lambda e, fb=fb, fcl=fcl: e.activation(
                            out=aT[fb % 2][:, fcl, :], in_=pst[3][:], func=AF.Square),
                            reads=[B_ps[3]], writes=[B_aT[fb % 2]])

            def down_block(fb):
                for dq in range(4):
                    slot, bslot = wtile(w_down[fb * 4 + dq], v_dn)
                    for dcl in range(2):
                        dc = dq * 2 + dcl
                        for tt in range(NTT):
                            bi = 4 + dbank["n"] % 4
                            dbank["n"] += 1
                            bank, bb = pst[bi], B_ps[bi]
                            fns = []
                            for fc in range(8):
                                fns.append(lambda e, fc=fc, bank=bank, slot=slot, tt=tt, dcl=dcl, fb=fb: e.matmul(
                                    bank[:], lhsT=aT[fb % 2][:, fc, tt * 128:(tt + 1) * 128],
                                    rhs=slot[:, fc, dcl * 512:(dcl + 1) * 512], start=(fc == 0), stop=(fc == 7)))
                            P.pe(fns, reads=[bslot, B_aT[fb % 2]], writes=[bb])
                            dst = acc[:, tt, dc * 512:(dc + 1) * 512]
                            P.op("dve", lambda e, bank=bank, dst=dst: e.tensor_tensor(
                                out=dst, in0=bank[:], in1=dst, op=ALU.add),
                                reads=[bb, B_acc[tt][dc]], writes=[B_acc[tt][dc]])
                            if fb == NFB - 1:
                                sq_block(tt, dc, 12 + tt, jb=0)

            up_block(0)
            for fb in range(NFB):
                if fb + 1 < NFB:
                    up_block(fb + 1)
                down_block(fb)

            g3, bg3 = gbc, B_gbc
            for tt in range(NTT):
                stats_fin(12 + tt)
                src = acc[:, tt, :]
                P.op("dve", lambda e, src=src, tt=tt, g3=g3: e.scalar_tensor_tensor(
                    out=src, in0=src, scalar=stat[:, 12 + tt:13 + tt], in1=g3, op0=ALU.mult, op1=ALU.mult),
                    reads=B_acc[tt] + [B_st[12 + tt], bg3], writes=B_acc[tt])
                r0 = tok0 + tt * 128
                P.dma("sp", s_o[tt], lambda e, src=src, r0=r0: e.dma_start(out=out[r0:r0 + 128, :], in_=src),
                      reads=B_acc[tt])
            P.op("dve", lambda e: e.memset(stat[:], 0.0), writes=[B_stat] + B_st)
            P.op("dve", lambda e: e.memset(statp[:], 0.0), writes=B_stp)

        for tt in range(NTT):
            P.wait("sp", Tok(s_o[tt].sem, s_o[tt].val))

        with nc.Block() as block:
            @block.sync
            def _(e):
                for f in P.q["sp"]:
                    f(e)

            @block.gpsimd
            def _(e):
                for f in P.q["pool"]:
                    f(e)

            @block.tensor
            def _(e):
                for f in P.q["pe"]:
                    f(e)

            @block.scalar
            def _(e):
                for f in P.q["act"]:
                    f(e)

            @block.vector
            def _(e):
                for f in P.q["dve"]:
                    f(e)
    return nc


def tile_weights(w_in, w_out, w_up, w_down):
    c = np.ascontiguousarray
    wi = c(w_in.reshape(32, 128, EIN // 256, 256).transpose(2, 1, 0, 3)).reshape(EIN // 256, 128, 8192)
    wu = c(w_up.reshape(32, 128, DFF // 256, 256).transpose(2, 1, 0, 3)).reshape(DFF // 256, 128, 8192)
    wo = c(w_out.reshape(2, 16, 128, 8, 512).transpose(3, 0, 2, 1, 4)).reshape(16, 128, 8192)
    wd = c(w_down.reshape(16, 8, 128, 4, 1024).transpose(0, 3, 2, 1, 4)).reshape(64, 128, 8192)
    return wi, wo, wu, wd


def kernel(x, mix_norm_g, w_in, conv_w, spatial_w, spatial_b, conv_out_norm_g, gmlp_out_norm_g,
           w_out, mlp_norm_g, w_up, w_down, final_norm_g):
    f32 = lambda a: np.ascontiguousarray(np.asarray(a, dtype=np.float32))
    x = f32(x)
    Bn, S, Dm = x.shape
    xf = x.reshape(Bn * S, Dm)
    ncores = 8
    wi, wo, wu, wd = tile_weights(f32(w_in)[0], f32(w_out)[0], f32(w_up)[0], f32(w_down)[0])
    shared = {
        "w_in": wi, "w_out": wo, "w_up": wu, "w_down": wd,
        "mix_g": f32(mix_norm_g)[0], "mlp_g": f32(mlp_norm_g)[0], "fin_g": f32(final_norm_g),
        "conv_w": f32(conv_w)[0], "sp_w": f32(spatial_w)[0], "sp_b": f32(spatial_b)[0],
        "ga": f32(conv_out_norm_g)[0], "gb": f32(gmlp_out_norm_g)[0],
    }
    in_maps = []
    for c in range(ncores):
        r0 = c * TOKC
        xc = xf[r0:r0 + TOKC]
        xh = np.zeros((NPASS, 2, Dm), np.float32)
        if r0 % S != 0:
            xh[0] = xf[r0 - 2:r0]
        xh[1] = xc[T - 2:T]
        m = {"x": np.ascontiguousarray(xc), "xh": xh}
        m.update(shared)
        in_maps.append(m)
    nc = build_nc()
    res = run_bass_kernel_spmd(nc, in_maps, core_ids=list(range(ncores)))
    outs = [np.asarray(r["out"], dtype=np.float32) for r in res.results]
    return np.concatenate(outs, axis=0).reshape(Bn, S, Dm)
```

```python
import numpy as np
from contextlib import ExitStack
import concourse.bass as bass
import concourse.mybir as mybir
from concourse.bass_utils import run_bass_kernel_spmd

F32 = mybir.dt.float32
BF16 = mybir.dt.bfloat16
AF = mybir.ActivationFunctionType
ALU = mybir.AluOpType

D = 4096
T = 512
NTT = 4
NPASS = 2
TOKC = 1024
KC = 32
EIN = 10240
CW = 2048
DFF = 16384
EPS = 1e-5
NSLOT = 4
FB = 1024
NFB = DFF // FB


class Tok:
    __slots__ = ("sem", "val")

    def __init__(self, sem, val):
        self.sem = sem
        self.val = val


class Buf:
    def __init__(self, name):
        self.name = name
        self.w = None
        self.r = {}


class DmaSem:
    def __init__(self, sem):
        self.sem = sem
        self.val = 0


class Prog:
    def __init__(self, nc, es):
        self.nc = nc
        self.q = {e: [] for e in ("pe", "act", "dve", "pool", "sp")}
        self.sems = {}
        for e in ("pe", "act", "dve", "pool"):
            self.sems[e] = es.enter_context(nc.semaphore("c_" + e))
        self.cnt = {e: 0 for e in self.sems}
        self.waited = {}

    def wait(self, eng, tok):
        if tok is None:
            return
        key = (eng, id(tok.sem))
        if self.waited.get(key, 0) >= tok.val:
            return
        self.waited[key] = tok.val
        sem, val = tok.sem, tok.val
        self.q[eng].append(lambda e: e.wait_ge(sem, val))

    def deps(self, eng, reads, writes):
        for b in reads:
            self.wait(eng, b.w)
        for b in writes:
            self.wait(eng, b.w)
            for t in list(b.r.values()):
                self.wait(eng, t)

    def commit(self, tok, reads, writes):
        for b in reads:
            old = b.r.get(id(tok.sem))
            if old is None or old.val < tok.val:
                b.r[id(tok.sem)] = tok
        for b in writes:
            b.w = tok
            b.r = {}

    def fence(self, engs, bufs):
        for e in engs:
            self.deps(e, (), bufs)

    def op(self, eng, fn, reads=(), writes=()):
        self.deps(eng, reads, writes)
        self.cnt[eng] += 1
        sem = self.sems[eng]
        tok = Tok(sem, self.cnt[eng])
        self.q[eng].append(lambda e: fn(e).then_inc(sem, 1))
        self.commit(tok, reads, writes)
        return tok

    def pe(self, fns, reads=(), writes=()):
        self.deps("pe", reads, writes)
        self.cnt["pe"] += 1
        sem = self.sems["pe"]
        tok = Tok(sem, self.cnt["pe"])
        for f in fns[:-1]:
            self.q["pe"].append(f)
        last = fns[-1]
        self.q["pe"].append(lambda e: last(e).then_inc(sem, 1))
        self.commit(tok, reads, writes)
        return tok

    def dma(self, eng, dsem, fn, reads=(), writes=()):
        self.deps(eng, reads, writes)
        dsem.val += 16
        sem = dsem.sem
        tok = Tok(sem, dsem.val)
        self.q[eng].append(lambda e: fn(e).then_inc(sem, 16))
        self.commit(tok, reads, writes)
        return tok


def build_nc(debug=False):
    nc = bass.Bass("TRN2", target_bir_lowering=False)
    dt_in = lambda n, s: nc.dram_tensor(n, s, F32, kind="ExternalInput").ap()
    x = dt_in("x", [TOKC, D])
    xh = dt_in("xh", [NPASS, 2, D])
    w_in = dt_in("w_in", [EIN // 256, 128, 8192])
    w_out = dt_in("w_out", [16, 128, 8192])
    w_up = dt_in("w_up", [DFF // 256, 128, 8192])
    w_down = dt_in("w_down", [64, 128, 8192])
    mix_g = dt_in("mix_g", [D])
    mlp_g = dt_in("mlp_g", [D])
    fin_g = dt_in("fin_g", [D])
    conv_w = dt_in("conv_w", [3, CW])
    sp_w = dt_in("sp_w", [16, 128, 128])
    sp_b = dt_in("sp_b", [16, 128])
    ga_d = dt_in("ga", [CW])
    gb_d = dt_in("gb", [CW])
    out = nc.dram_tensor("out", [TOKC, D], F32, kind="ExternalOutput").ap()

    with ExitStack() as es:
        sb = lambda n, s, d: es.enter_context(nc.sbuf_tensor(n, s, d))
        R1 = sb("R1", [128, KC * 514], BF16)
        R2 = sb("R2", [128, 8192], F32)
        R3 = sb("R3", [128, 16384], F32)
        slots = [sb("ws%d" % i, [128, 8192], BF16) for i in range(NSLOT)]
        ident_f = sb("ident_f", [128, 128], F32)
        ident_b = sb("ident_b", [128, 128], BF16)
        ones_f = sb("ones_f", [128, 128], F32)
        convw = sb("convw", [128, 3, 16], F32)
        ga = sb("ga_sb", [128, 16], F32)
        gb = sb("gb_sb", [128, 16], F32)
        bias_bc = sb("bias_bc", [128, 16 * 128], F32)
        WmT = sb("WmT", [128, 16 * 128], BF16)
        stat = sb("stat", [128, 32], F32)
        inv_tok = sb("inv_tok", [128, 8], F32)
        statp = sb("statp", [128, 128], F32)
        hsave = sb("hsave", [128, 32], F32)
        pst = [es.enter_context(nc.psum_tensor("ps%d" % i, [128, 512], F32)) for i in range(8)]

        P = Prog(nc, es)
        newsem = lambda n: DmaSem(es.enter_context(nc.semaphore(n)))
        s_slot = [newsem("s_slot%d" % i) for i in range(NSLOT)]
        s_x = [newsem("s_x%d" % i) for i in range(NTT)]
        s_o = [newsem("s_o%d" % i) for i in range(NTT)]
        s_h = newsem("s_h")
        s_g = newsem("s_g")
        s_par = [newsem("s_par%d" % i) for i in range(6)]

        dbg = {"n": 0}

        def dump(name, ap, bufs):
            if not debug:
                return
            shp = list(ap.shape)
            dd = nc.dram_tensor(name, shp, ap.dtype, kind="ExternalOutput").ap()
            ds = newsem("s_dbg%d" % dbg["n"])
            dbg["n"] += 1
            t = P.dma("sp", ds, lambda e: e.dma_start(out=dd, in_=ap), reads=bufs)
            P.wait("sp", t)

        x1 = R1[:].rearrange("p (c t) -> p c t", t=514)
        yT = R2[:].bitcast(BF16).rearrange("p (c t) -> p c t", t=512)
        xnb = R2[:, 0:2048].bitcast(BF16)
        gbc = R2[:, 2048:6144]
        aT = [R2[:, 6144:8192].bitcast(BF16).rearrange("p (c t) -> p c t", t=512),
              R2[:, 0:2048].bitcast(BF16).rearrange("p (c t) -> p c t", t=512)]
        rt = [R2[:, 2048:2560], R2[:, 2560:3072]]
        xnbs = [R2[:, 0:2048].bitcast(BF16), R2[:, 6144:8192].bitcast(BF16)]
        R1f = R1[:].bitcast(F32)
        hst = R1f[0:2, 0:4096]
        acc = R3[:].rearrange("p (a d) -> p a d", d=D)
        Xb = pst[7][:].bitcast(BF16)
        Sb = pst[6][:].bitcast(BF16)
        wm3 = WmT[:].rearrange("p (h t) -> p h t", t=128)
        bias3 = bias_bc[:].rearrange("p (h t) -> p h t", t=128)

        def r3(off, n):
            return R3[:, off:off + n]
        tB = []
        o = 0
        for par in range(2):
            d = {}
            for nm, n in (("bS", 512), ("cS", 516), ("hf", 516), ("t1", 512), ("ya", 512),
                          ("sq", 512), ("gu", 512), ("tb", 512), ("yb", 512)):
                d[nm] = r3(o, n)
                o += n
            d["gvT"] = r3(o, 256).bitcast(BF16)
            o += 256
            d["gv"] = r3(o, 256).bitcast(BF16)
            o += 256
            tB.append(d)
        inv_bc = r3(o, 512)
        o += 512
        spw_f = R3[:, 12288:14336]
        spw_b = R3[:, 14336:15360].bitcast(BF16)

        B_x1 = [Buf("x1_%d" % i) for i in range(NTT)]
        B_x1h = Buf("x1h")
        B_yT = Buf("yT")
        B_xnb = Buf("xnb")
        B_gbc = Buf("gbc")
        B_aT0 = Buf("aT0")
        B_aT = [B_aT0, B_xnb]
        B_acc = [[Buf("acc%d_%d" % (t, c)) for c in range(8)] for t in range(NTT)]
        B_slot = [Buf("slot%d" % i) for i in range(NSLOT)]
        B_ps = [Buf("ps%d" % i) for i in range(8)]
        B_tB = [{k: Buf("tB%d_%s" % (par, k)) for k in tB[par]} for par in range(2)]
        B_invbc = Buf("inv_bc")
        B_stat = Buf("stat")
        B_st = [Buf("st%d" % i) for i in range(16)]
        B_stp = [Buf("stp%d" % i) for i in range(16)]
        B_hst = Buf("hst")
        B_hsave = Buf("hsave")
        B_xn = [B_xnb, B_aT0]
        B_invtok = Buf("inv_tok")
        B_const = Buf("const")
        B_cw = Buf("convw")
        B_gab = Buf("ga")
        B_gbb = Buf("gb")
        B_bias = Buf("bias")
        B_wm = Buf("WmT")
        B_spw = Buf("spw")
        all_tB = [b for dd in B_tB for b in dd.values()] + [B_invbc]
        all_acc = [b for row in B_acc for b in row]
        all_x1 = B_x1 + [B_x1h]

        wstate = {"n": 0}

        pre = []

        def wissue(src_ap, view_fn):
            n = wstate["n"]
            wstate["n"] += 1
            s = n % NSLOT
            dst = view_fn(slots[s])
            flat = slots[s][:]
            P.dma("pool", s_slot[s], lambda e: e.dma_start(out=flat, in_=src_ap), writes=[B_slot[s]])
            return dst, B_slot[s]

        def wtile(src_ap, view_fn, key=None):
            if pre:
                k, r = pre.pop(0)
                assert k == key, (k, key)
                return r
            return wissue(src_ap, view_fn)

        v_in = lambda sl: sl[:].rearrange("p (k n) -> p k n", n=256)
        v_out = lambda sl: sl[:].rearrange("p (k n) -> p k n", n=512)
        v_dn = lambda sl: sl[:].rearrange("p (k n) -> p k n", n=1024)

        for tt in range(NTT - 1):
            P.dma("sp", s_x[tt], lambda e, tt=tt: e.dma_start(out=acc[:, tt, :],
                                                               in_=x[tt * 128:(tt + 1) * 128, :]),
                  writes=B_acc[tt])
        P.dma("sp", s_h, lambda e: e.dma_start(out=hst, in_=xh[0]), writes=[B_hst])
        P.dma("sp", s_g, lambda e: e.dma_start(out=gbc, in_=mix_g.partition_broadcast(128)), writes=[B_gbc])
        P.wait("pool", Tok(s_x[1].sem, 16))
        for c0 in (0, CW, 2 * CW, 256):
            pre.append((("in", c0), wissue(w_in[c0 // 256], v_in)))

        pdma = [
            (spw_f.rearrange("p (h s) -> p h s", s=128), sp_w.rearrange("h t s -> t h s")),
            (convw[:], conv_w.rearrange("k (j p) -> p k j", p=128)),
            (ga[:], ga_d.rearrange("(j p) -> p j", p=128)),
            (gb[:], gb_d.rearrange("(j p) -> p j", p=128)),
            (bias_bc[:], sp_b.rearrange("h t -> (h t)").partition_broadcast(128)),
        ]
        for i, (o_ap, i_ap) in enumerate(pdma):
            def f(e, o_ap=o_ap, i_ap=i_ap):
                return e.dma_start(out=o_ap, in_=i_ap, allow_slow_non_contiguous=True)
            P.dma("sp", s_par[i], f, writes=[[B_spw], [B_cw], [B_gab], [B_gbb], [B_bias]][i])

        P.op("dve", lambda e: e.memset(ones_f[:], 1.0), writes=[B_const])
        P.op("dve", lambda e: e.memset(ident_f[:], 0.0), writes=[B_const])
        P.op("pool", lambda e: e.affine_select(out=ident_f[:], in_=ident_f[:], pattern=[[-1, 128]],
                                                compare_op=ALU.not_equal, fill=1.0, base=0,
                                                channel_multiplier=1), writes=[B_const])
        P.op("dve", lambda e: e.tensor_copy(out=ident_b[:], in_=ident_f[:]), writes=[B_const])
        P.op("dve", lambda e: e.memset(stat[:], 0.0), writes=[B_stat] + B_st)
        P.op("dve", lambda e: e.memset(statp[:], 0.0), writes=B_stp)
        P.op("pool", lambda e: e.affine_select(out=spw_f.rearrange("p (h s) -> p h s", s=128),
                                                in_=spw_f.rearrange("p (h s) -> p h s", s=128),
                                                pattern=[[0, 16], [-1, 128]], compare_op=ALU.is_ge,
                                                fill=0.0, base=0, channel_multiplier=1),
             writes=[B_spw])
        P.op("dve", lambda e: e.tensor_copy(out=spw_b, in_=spw_f), reads=[B_spw], writes=[B_spw])
        for half in range(2):
            bank, bb = (Xb, B_ps[7]) if half == 0 else (Sb, B_ps[6])
            fns = []
            for i in range(8):
                h = half * 8 + i
                fns.append(lambda e, i=i, h=h, bank=bank: e.transpose(
                    out=bank[:, i * 128:(i + 1) * 128], in_=spw_b[:, h * 128:(h + 1) * 128],
                    identity=ident_b[:]))
            P.pe(fns, reads=[B_spw, B_const], writes=[bb])
            P.op("dve", lambda e, half=half, bank=bank: e.tensor_copy(
                out=WmT[:, half * 1024:(half + 1) * 1024], in_=bank), reads=[bb], writes=[B_wm])
        P.dma("sp", s_x[NTT - 1], lambda e: e.dma_start(out=acc[:, NTT - 1, :],
                                                         in_=x[(NTT - 1) * 128:NTT * 128, :]),
              writes=B_acc[NTT - 1] + [B_spw])

        bankrot = {"m": 0, "d": 0, "c": 0}

        def main_bank():
            i = bankrot["m"] % 4
            bankrot["m"] += 1
            return pst[i], B_ps[i]

        deferred = []

        def defer(n, fn):
            deferred.append([n, fn])

        def tick():
            ready = []
            for it in deferred:
                it[0] -= 1
            for it in list(deferred):
                if it[0] <= 0:
                    deferred.remove(it)
                    ready.append(it[1])
            for f in ready:
                f()

        def flush():
            while deferred:
                tick()

        def stats_sq(src, src_bufs, slot, n=128):
            for k in range(8):
                jb = k % 2
                P.op("act", lambda e, k=k, jb=jb: e.activation(
                    out=pst[jb][0:n, :], in_=src[0:n, k * 512:(k + 1) * 512], func=AF.Square,
                    accum_out=statp[0:n, slot * 8 + k:slot * 8 + k + 1]),
                    reads=src_bufs, writes=[B_ps[jb], B_stp[slot]])

        def sq_block(tt, dc, slot, jb=7):
            P.op("act", lambda e: e.activation(
                out=pst[jb][:], in_=acc[:, tt, dc * 512:(dc + 1) * 512], func=AF.Square,
                accum_out=statp[:, slot * 8 + dc:slot * 8 + dc + 1]),
                reads=[B_acc[tt][dc]], writes=[B_ps[jb], B_stp[slot]])

        def stats_fin(slot, n=128):
            P.op("dve", lambda e: e.reduce_sum(out=stat[0:n, slot:slot + 1], in_=statp[0:n, slot * 8:slot * 8 + 8],
                                               axis=mybir.AxisListType.X),
                 reads=[B_stp[slot]], writes=[B_st[slot]])
            P.op("act", lambda e: e.activation(out=stat[0:n, slot:slot + 1], in_=stat[0:n, slot:slot + 1],
                                               func=AF.Sqrt, scale=1.0 / D, bias=EPS),
                 reads=[B_st[slot]], writes=[B_st[slot]])
            P.op("dve", lambda e: e.reciprocal(out=stat[0:n, slot:slot + 1], in_=stat[0:n, slot:slot + 1]),
                 reads=[B_st[slot]], writes=[B_st[slot]])

        TB = [(pst[7][:].bitcast(BF16), B_ps[7]), (pst[6][:].bitcast(BF16), B_ps[6]),
              (pst[5][:].bitcast(BF16), B_ps[5]), (pst[4][:].bitcast(BF16), B_ps[4])]

        def nt(tt, slot, g_ap, g_buf, extra_w=()):
            src = acc[:, tt, :]
            xb, bxb = xnbs[tt % 2], B_xn[tt % 2]
            P.op("dve", lambda e: e.scalar_tensor_tensor(out=xb, in0=src, scalar=stat[:, slot:slot + 1],
                                                         in1=g_ap, op0=ALU.mult, op1=ALU.mult),
                 reads=B_acc[tt] + [B_st[slot], g_buf], writes=[bxb] + list(extra_w))
            for f4 in range(4):
                bank, bb = TB[f4]
                fns = []
                for i in range(8):
                    c = f4 * 8 + i
                    fns.append(lambda e, i=i, c=c, bank=bank: e.transpose(
                        out=bank[:, i * 128:(i + 1) * 128], in_=xb[:, c * 128:(c + 1) * 128],
                        identity=ident_b[:]))
                P.pe(fns, reads=[bxb, B_const], writes=[bb])
            for f4 in range(4):
                bank, bb = TB[f4]
                dst = x1[:, f4 * 8:(f4 + 1) * 8, 2 + tt * 128:2 + (tt + 1) * 128]
                srcp = bank.rearrange("p (c t) -> p c t", t=128)
                if tt == 3 and f4 % 2 == 1:
                    P.op("act", lambda e, dst=dst, srcp=srcp: e.copy(out=dst, in_=srcp),
                         reads=[bb], writes=[B_x1[tt]])
                else:
                    P.op("dve", lambda e, dst=dst, srcp=srcp: e.tensor_copy(out=dst, in_=srcp),
                         reads=[bb], writes=[B_x1[tt]])

        for ps_ in range(NPASS):
            tok0 = ps_ * T
            if ps_ > 0:
                P.fence(["sp"], all_tB)
                for tt in range(NTT):
                    r0 = tok0 + tt * 128
                    P.dma("sp", s_x[tt], lambda e, tt=tt, r0=r0: e.dma_start(out=acc[:, tt, :],
                                                                             in_=x[r0:r0 + 128, :]),
                          writes=B_acc[tt])
            if ps_ > 0:
                P.dma("sp", s_g, lambda e: e.dma_start(out=gbc, in_=mix_g.partition_broadcast(128)),
                      writes=[B_gbc])
            gm, bgm = gbc, B_gbc
            P.fence(["dve"], [B_yT, B_aT0])
            if ps_ == 0:
                stats_sq(hst, [B_hst], 8, n=2)
                stats_fin(8, n=2)
                P.op("dve", lambda e, gm=gm: e.scalar_tensor_tensor(out=xnbs[1][0:2, :], in0=hst,
                                                             scalar=stat[0:2, 8:9], in1=gm[0:2, :],
                                                             op0=ALU.mult, op1=ALU.mult),
                     reads=[B_hst, B_st[8], bgm], writes=[B_xn[1]])
                fns = []
                for c in range(KC):
                    fns.append(lambda e, c=c: e.transpose(out=Sb[:, c * 2:(c + 1) * 2],
                                                          in_=xnbs[1][0:2, c * 128:(c + 1) * 128],
                                                          identity=ident_b[0:2, 0:2]))
                P.pe(fns, reads=[B_xn[1], B_const], writes=[B_ps[6]])
                P.op("dve", lambda e: e.tensor_copy(out=x1[:, :, 0:2],
                                                    in_=Sb[:, 0:64].rearrange("p (c t) -> p c t", t=2)),
                     reads=[B_ps[6]], writes=[B_x1h, B_hst])
                P.fence(["act", "dve"], [B_hst])
            for tt in range(NTT):
                stats_sq(acc[:, tt, :], B_acc[tt], tt)
                stats_fin(tt)
                nt(tt, tt, gm, bgm)

            if ps_ == 0:
                dump("d_x1", R1[:], all_x1)
            P.fence(["act", "dve"], all_acc + [B_xnb, B_aT0, B_gbc])
            S_bank, B_S = pst[6], B_ps[6]
            Hb = [(pst[4], B_ps[4]), (pst[5], B_ps[5])]
            hcnt = {"n": 0}

            def in_chunk(slot, bslot, i, with_halo):
                hal = None
                if with_halo:
                    hb, bhb = Hb[hcnt["n"] % 2]
                    hcnt["n"] += 1
                    fns = []
                    for kc in range(KC):
                        fns.append(lambda e, kc=kc, hb=hb: e.matmul(
                            hb[:, 0:2], lhsT=slot[:, kc, i * 128:(i + 1) * 128], rhs=x1[:, kc, 0:2],
                            start=(kc == 0), stop=(kc == KC - 1)))
                    P.pe(fns, reads=[bslot, B_x1h], writes=[bhb])
                    hal = (hb[:, 0:2], bhb)
                bank, bb = main_bank()
                fns = []
                for kc in range(KC):
                    fns.append(lambda e, kc=kc, bank=bank: e.matmul(
                        bank[:], lhsT=slot[:, kc, i * 128:(i + 1) * 128], rhs=x1[:, kc, 2:514],
                        start=(kc == 0), stop=(kc == KC - 1)))
                P.pe(fns, reads=[bslot] + B_x1, writes=[bb])
                return bank, bb, hal

            def ssq_mm(sq_ap, bsq, first, last):
                P.pe([lambda e: e.matmul(S_bank[:], lhsT=ones_f[:], rhs=sq_ap, start=first, stop=last)],
                     reads=[bsq, B_const], writes=[B_S])

            def extract_inv(col0):
                P.op("act", lambda e: e.activation(out=inv_bc, in_=S_bank[:], func=AF.Sqrt,
                                                   scale=1.0 / CW, bias=EPS),
                     reads=[B_S], writes=[B_invbc])
                P.op("dve", lambda e: e.reciprocal(out=inv_bc, in_=inv_bc), reads=[B_invbc], writes=[B_invbc])
                bank, bb = main_bank()
                fns = []
                for tt in range(NTT):
                    fns.append(lambda e, tt=tt, bank=bank: e.transpose(
                        out=bank[:, tt * 128:(tt + 1) * 128], in_=inv_bc[:, tt * 128:(tt + 1) * 128],
                        identity=ident_f[:]))
                P.pe(fns, reads=[B_invbc, B_const], writes=[bb])
                P.op("dve", lambda e, bank=bank: e.tensor_copy(out=inv_tok[:, col0:col0 + 4],
                                                               in_=bank[:, 0:512:128]),
                     reads=[bb], writes=[B_invtok])

            for g in range(8):
                for kind, base in (("b", 0), ("c", CW), ("h", 2 * CW)):
                    c0 = base + g * 256
                    slot, bslot = wtile(w_in[c0 // 256], v_in,
                                        key=("in", c0))
                    for i in range(2):
                        j = g * 2 + i
                        tb, btb = tB[i], B_tB[i]
                        bank, bb, hal = in_chunk(slot, bslot, i, kind != "b" and ps_ == 0)
                        if kind == "b":
                            P.op("act", lambda e, tb=tb, bank=bank: e.copy(out=tb["bS"], in_=bank[:]),
                                 reads=[bb], writes=[btb["bS"]])
                        elif kind == "c":
                            P.op("act", lambda e, tb=tb, bank=bank: e.copy(out=tb["cS"][:, 2:514], in_=bank[:]),
                                 reads=[bb], writes=[btb["cS"]])
                            if ps_ == 0:
                                P.op("act", lambda e, tb=tb, hal=hal: e.copy(out=tb["cS"][:, 0:2], in_=hal[0]),
                                     reads=[hal[1]], writes=[btb["cS"]])
                        else:
                            P.op("dve", lambda e, tb=tb, bank=bank: e.tensor_tensor(
                                out=tb["hf"][:, 2:514], in0=bank[:], in1=tb["cS"][:, 2:514], op=ALU.mult),
                                reads=[bb, btb["cS"]], writes=[btb["hf"]])
                            if ps_ == 0:
                                P.op("dve", lambda e, tb=tb, hal=hal: e.tensor_tensor(
                                    out=tb["hf"][:, 0:2], in0=hal[0], in1=tb["cS"][:, 0:2], op=ALU.mult),
                                    reads=[hal[1], btb["cS"]], writes=[btb["hf"]])
                                P.op("dve", lambda e, tb=tb, j=j: e.tensor_copy(
                                    out=hsave[:, 2 * j:2 * j + 2], in_=tb["hf"][:, 512:514]),
                                    reads=[btb["hf"]], writes=[B_hsave])
                            else:
                                P.op("dve", lambda e, tb=tb, j=j: e.tensor_copy(
                                    out=tb["hf"][:, 0:2], in_=hsave[:, 2 * j:2 * j + 2]),
                                    reads=[B_hsave], writes=[btb["hf"]])
                            P.op("dve", lambda e, tb=tb, j=j: e.tensor_scalar(
                                out=tb["t1"], in0=tb["hf"][:, 0:512], scalar1=convw[:, 0, j:j + 1],
                                scalar2=None, op0=ALU.mult),
                                reads=[btb["hf"], B_cw], writes=[btb["t1"]])
                            for k in (1, 2):
                                P.op("dve", lambda e, tb=tb, j=j, k=k: e.scalar_tensor_tensor(
                                    out=tb["t1"], in0=tb["hf"][:, k:k + 512], scalar=convw[:, k, j:j + 1],
                                    in1=tb["t1"], op0=ALU.mult, op1=ALU.add),
                                    reads=[btb["hf"], btb["t1"], B_cw], writes=[btb["t1"]])
                            P.op("dve", lambda e, tb=tb: e.tensor_tensor(
                                out=tb["ya"], in0=tb["t1"], in1=tb["bS"], op=ALU.mult),
                                reads=[btb["t1"], btb["bS"]], writes=[btb["ya"]])
                            P.op("act", lambda e, tb=tb: e.activation(out=tb["sq"], in_=tb["ya"], func=AF.Square),
                                 reads=[btb["ya"]], writes=[btb["sq"]])
                            P.op("act", lambda e, tb=tb, j=j: e.activation(
                                out=yT[:, j, :], in_=tb["ya"], func=AF.Copy, scale=ga[:, j:j + 1]),
                                reads=[btb["ya"], B_gab], writes=[B_yT])
                            defer(3, lambda tb=tb, btb=btb, j=j: ssq_mm(tb["sq"], btb["sq"], j == 0, j == 15))
                        tick()
            defer(4, lambda: extract_inv(0))

            for g in range(8):
                for kind, base in (("v", 4 * CW), ("u", 3 * CW)):
                    c0 = base + g * 256
                    slot, bslot = wtile(w_in[c0 // 256], v_in)
                    for i in range(2):
                        hd = g * 2 + i
                        tb, btb = tB[i], B_tB[i]
                        bank, bb, _ = in_chunk(slot, bslot, i, False)
                        if kind == "u":
                            P.op("act", lambda e, tb=tb, bank=bank: e.activation(
                                out=tb["gu"], in_=bank[:], func=AF.Gelu_apprx_tanh),
                                reads=[bb], writes=[btb["gu"]])
                        else:
                            P.op("act", lambda e, tb=tb, bank=bank: e.activation(
                                out=tb["gvT"], in_=bank[:], func=AF.Gelu_apprx_tanh),
                                reads=[bb], writes=[btb["gvT"]])

                            def stage1(tb=tb, btb=btb, hd=hd):
                                fns = []
                                for ck in range(4):
                                    fns.append(lambda e, ck=ck: e.transpose(
                                        out=Xb[:, ck * 128:(ck + 1) * 128],
                                        in_=tb["gvT"][:, ck * 128:(ck + 1) * 128], identity=ident_b[:]))
                                P.pe(fns, reads=[btb["gvT"], B_const], writes=[B_ps[7]])
                                P.op("dve", lambda e: e.tensor_copy(out=tb["gv"], in_=Xb[:, 0:512]),
                                     reads=[B_ps[7]], writes=[btb["gv"]])
                                defer(1, lambda: stage2(tb, btb, hd))

                            def stage2(tb, btb, hd):
                                hs, bhs = Hb[hd % 2]
                                fns = []
                                for ck in range(4):
                                    fns.append(lambda e, ck=ck: e.matmul(
                                        hs[:, ck * 128:(ck + 1) * 128],
                                        lhsT=tb["gv"][:, ck * 128:(ck + 1) * 128], rhs=wm3[:, hd, :],
                                        start=True, stop=True))
                                P.pe(fns, reads=[btb["gv"], B_wm], writes=[bhs])
                                P.op("dve", lambda e: e.tensor_tensor(
                                    out=tb["tb"].rearrange("p (c t) -> p c t", t=128),
                                    in0=hs[:].rearrange("p (c t) -> p c t", t=128),
                                    in1=bias3[:, hd, :].unsqueeze(1).to_broadcast([128, 4, 128]), op=ALU.add),
                                    reads=[bhs, B_bias], writes=[btb["tb"]])
                                P.op("dve", lambda e: e.tensor_tensor(out=tb["yb"], in0=tb["tb"], in1=tb["gu"],
                                                                      op=ALU.mult),
                                     reads=[btb["tb"], btb["gu"]], writes=[btb["yb"]])
                                P.op("act", lambda e: e.activation(out=tb["sq"], in_=tb["yb"], func=AF.Square),
                                     reads=[btb["yb"]], writes=[btb["sq"]])
                                P.op("act", lambda e: e.activation(out=yT[:, 16 + hd, :], in_=tb["yb"],
                                                                   func=AF.Copy, scale=gb[:, hd:hd + 1]),
                                     reads=[btb["yb"], B_gbb], writes=[B_yT])
                                defer(1, lambda: ssq_mm(tb["sq"], btb["sq"], hd == 0, hd == 15))

                            defer(2, stage1)
                        tick()
            flush()
            extract_inv(4)

            if ps_ == 0:
                dump("d_yT", R2[:], [B_yT])
                dump("d_inv", inv_tok[:], [B_invtok])
            g2stage = R1f[:, 0:4096]
            P.dma("sp", s_g, lambda e: e.dma_start(out=g2stage, in_=mlp_g.partition_broadcast(128)),
                  writes=all_x1)
            P.fence(["sp"], all_tB)
            for tt in range(NTT):
                r0 = tok0 + tt * 128
                P.dma("sp", s_x[tt], lambda e, tt=tt, r0=r0: e.dma_start(out=acc[:, tt, :],
                                                                         in_=x[r0:r0 + 128, :]),
                      writes=B_acc[tt] + all_tB)
            for dc in range(8):
                for half in range(2):
                    slot, bslot = wtile(w_out[dc * 2 + half], v_out)
                    for tt in range(NTT):
                        bi = bankrot["c"] % 7
                        bankrot["c"] += 1
                        bank, bb = pst[bi], B_ps[bi]
                        fns = []
                        for cc in range(16):
                            fns.append(lambda e, cc=cc, bank=bank, tt=tt, half=half, slot=slot: e.matmul(
                                bank[:], lhsT=yT[:, half * 16 + cc, tt * 128:(tt + 1) * 128],
                                rhs=slot[:, cc, :], start=(cc == 0), stop=(cc == 15)))
                        P.pe(fns, reads=[bslot, B_yT], writes=[bb])
                        dst = acc[:, tt, dc * 512:(dc + 1) * 512]
                        P.op("dve", lambda e, bank=bank, dst=dst, tt=tt, half=half: e.scalar_tensor_tensor(
                            out=dst, in0=bank[:], scalar=inv_tok[:, half * 4 + tt:half * 4 + tt + 1],
                            in1=dst, op0=ALU.mult, op1=ALU.add),
                            reads=[bb, B_invtok, B_acc[tt][dc]], writes=[B_acc[tt][dc]])
                        if half == 1:
                            sq_block(tt, dc, 4 + tt)

            if ps_ == 0:
                dump("d_h", R3[:], all_acc)
            P.op("act", lambda e: e.copy(out=gbc, in_=g2stage), reads=all_x1, writes=[B_gbc, B_yT])
            g2, bg2 = gbc, B_gbc
            P.fence(["act", "dve"], all_x1)
            for tt in range(NTT):
                stats_fin(4 + tt)
            for tt in range(NTT):
                nt(tt, 4 + tt, g2, bg2, extra_w=[B_yT] if tt < 2 else ())

            if ps_ == 0:
                dump("d_x2", R1[:], all_x1)
            P.fence(["act", "dve"], [B_xnb])
            dbank = {"n": 0}
            upb = {"n": 0}
            P.dma("sp", s_g, lambda e: e.dma_start(out=gbc, in_=fin_g.partition_broadcast(128)),
                  writes=[B_gbc])

            def up_block(fb):
                for u in range(4):
                    c0 = fb * FB + u * 256
                    slot, bslot = wtile(w_up[c0 // 256], v_in)
                    for i in range(2):
                        fcl = u * 2 + i
                        bi = upb["n"] % 3
                        upb["n"] += 1
                        bank, bb = pst[bi], B_ps[bi]
                        fns = []
                        for kc in range(KC):
                            fns.append(lambda e, kc=kc, bank=bank, slot=slot, i=i: e.matmul(
                                bank[:], lhsT=slot[:, kc, i * 128:(i + 1) * 128], rhs=x1[:, kc, 2:514],
                                start=(kc == 0), stop=(kc == KC - 1)))
                        P.pe(fns, reads=[bslot] + B_x1, writes=[bb])
                        P.op("act", lambda e, bank=bank: e.activation(out=pst[3][:], in_=bank[:], func=AF.Relu),
                             reads=[bb], writes=[B_ps[3]])
                        P.op("act", lambda e, fb=fb, fcl=fcl: e.activation(
                            out=aT[fb % 2][:, fcl, :], in_=pst[3][:], func=AF.Square),
                            reads=[B_ps[3]], writes=[B_aT[fb % 2]])

            def down_block(fb):
                for dq in range(4):
                    slot, bslot = wtile(w_down[fb * 4 + dq], v_dn)
                    for dcl in range(2):
                        dc = dq * 2 + dcl
                        for tt in range(NTT):
                            bi = 4 + dbank["n"] % 4
                            dbank["n"] += 1
                            bank, bb = pst[bi], B_ps[bi]
                            fns = []
                            for fc in range(8):
                                fns.append(lambda e, fc=fc, bank=bank, slot=slot, tt=tt, dcl=dcl, fb=fb: e.matmul(
                                    bank[:], lhsT=aT[fb % 2][:, fc, tt * 128:(tt + 1) * 128],
                                    rhs=slot[:, fc, dcl * 512:(dcl + 1) * 512], start=(fc == 0), stop=(fc == 7)))
                            P.pe(fns, reads=[bslot, B_aT[fb % 2]], writes=[bb])
                            dst = acc[:, tt, dc * 512:(dc + 1) * 512]
                            P.op("dve", lambda e, bank=bank, dst=dst: e.tensor_tensor(
                                out=dst, in0=bank[:], in1=dst, op=ALU.add),
                                reads=[bb, B_acc[tt][dc]], writes=[B_acc[tt][dc]])
                            if fb == NFB - 1:
                                sq_block(tt, dc, 12 + tt, jb=0)

            up_block(0)
            for fb in range(NFB):
                if fb + 1 < NFB:
                    up_block(fb + 1)
                down_block(fb)

            g3, bg3 = gbc, B_gbc
            for tt in range(NTT):
                stats_fin(12 + tt)
                src = acc[:, tt, :]
                P.op("dve", lambda e, src=src, tt=tt, g3=g3: e.scalar_tensor_tensor(
                    out=src, in0=src, scalar=stat[:, 12 + tt:13 + tt], in1=g3, op0=ALU.mult, op1=ALU.mult),
                    reads=B_acc[tt] + [B_st[12 + tt], bg3], writes=B_acc[tt])
                r0 = tok0 + tt * 128
                P.dma("sp", s_o[tt], lambda e, src=src, r0=r0: e.dma_start(out=out[r0:r0 + 128, :], in_=src),
                      reads=B_acc[tt])
            P.op("dve", lambda e: e.memset(stat[:], 0.0), writes=[B_stat] + B_st)
            P.op("dve", lambda e: e.memset(statp[:], 0.0), writes=B_stp)

        for tt in range(NTT):
            P.wait("sp", Tok(s_o[tt].sem, s_o[tt].val))

        with nc.Block() as block:
            @block.sync
            def _(e):
                for f in P.q["sp"]:
                    f(e)

            @block.gpsimd
            def _(e):
                for f in P.q["pool"]:
                    f(e)

            @block.tensor
            def _(e):
                for f in P.q["pe"]:
                    f(e)

            @block.scalar
            def _(e):
                for f in P.q["act"]:
                    f(e)

            @block.vector
            def _(e):
                for f in P.q["dve"]:
                    f(e)
    return nc


def tile_weights(w_in, w_out, w_up, w_down):
    c = np.ascontiguousarray
    wi = c(w_in.reshape(32, 128, EIN // 256, 256).transpose(2, 1, 0, 3)).reshape(EIN // 256, 128, 8192)
    wu = c(w_up.reshape(32, 128, DFF // 256, 256).transpose(2, 1, 0, 3)).reshape(DFF // 256, 128, 8192)
    wo = c(w_out.reshape(2, 16, 128, 8, 512).transpose(3, 0, 2, 1, 4)).reshape(16, 128, 8192)
    wd = c(w_down.reshape(16, 8, 128, 4, 1024).transpose(0, 3, 2, 1, 4)).reshape(64, 128, 8192)
    return wi, wo, wu, wd


def kernel(x, mix_norm_g, w_in, conv_w, spatial_w, spatial_b, conv_out_norm_g, gmlp_out_norm_g,
           w_out, mlp_norm_g, w_up, w_down, final_norm_g):
    f32 = lambda a: np.ascontiguousarray(np.asarray(a, dtype=np.float32))
    x = f32(x)
    Bn, S, Dm = x.shape
    xf = x.reshape(Bn * S, Dm)
    ncores = 8
    wi, wo, wu, wd = tile_weights(f32(w_in)[0], f32(w_out)[0], f32(w_up)[0], f32(w_down)[0])
    shared = {
        "w_in": wi, "w_out": wo, "w_up": wu, "w_down": wd,
        "mix_g": f32(mix_norm_g)[0], "mlp_g": f32(mlp_norm_g)[0], "fin_g": f32(final_norm_g),
        "conv_w": f32(conv_w)[0], "sp_w": f32(spatial_w)[0], "sp_b": f32(spatial_b)[0],
        "ga": f32(conv_out_norm_g)[0], "gb": f32(gmlp_out_norm_g)[0],
    }
    in_maps = []
    for c in range(ncores):
        r0 = c * TOKC
        xc = xf[r0:r0 + TOKC]
        xh = np.zeros((NPASS, 2, Dm), np.float32)
        if r0 % S != 0:
            xh[0] = xf[r0 - 2:r0]
        xh[1] = xc[T - 2:T]
        m = {"x": np.ascontiguousarray(xc), "xh": xh}
        m.update(shared)
        in_maps.append(m)
    nc = build_nc()
    res = run_bass_kernel_spmd(nc, in_maps, core_ids=list(range(ncores)))
    outs = [np.asarray(r["out"], dtype=np.float32) for r in res.results]
    return np.concatenate(outs, axis=0).reshape(Bn, S, Dm)
```

```python
import numpy as np
from contextlib import ExitStack
import concourse.bass as bass
import concourse.mybir as mybir
from concourse.bass_utils import run_bass_kernel_spmd

F32 = mybir.dt.float32
BF16 = mybir.dt.bfloat16
AF = mybir.ActivationFunctionType
ALU = mybir.AluOpType

D = 4096
T = 512
NTT = 4
NPASS = 2
TOKC = 1024
KC = 32
EIN = 10240
CW = 2048
DFF = 16384
EPS = 1e-5
NSLOT = 4
FB = 1024
NFB = DFF // FB


class Tok:
    __slots__ = ("sem", "val")

    def __init__(self, sem, val):
        self.sem = sem
        self.val = val


class Buf:
    def __init__(self, name):
        self.name = name
        self.w = None
        self.r = {}


class DmaSem:
    def __init__(self, sem):
        self.sem = sem
        self.val = 0


class Prog:
    def __init__(self, nc, es):
        self.nc = nc
        self.q = {e: [] for e in ("pe", "act", "dve", "pool", "sp")}
        self.sems = {}
        for e in ("pe", "act", "dve", "pool"):
            self.sems[e] = es.enter_context(nc.semaphore("c_" + e))
        self.cnt = {e: 0 for e in self.sems}
        self.waited = {}

    def wait(self, eng, tok):
        if tok is None:
            return
        key = (eng, id(tok.sem))
        if self.waited.get(key, 0) >= tok.val:
            return
        self.waited[key] = tok.val
        sem, val = tok.sem, tok.val
        self.q[eng].append(lambda e: e.wait_ge(sem, val))

    def deps(self, eng, reads, writes):
        for b in reads:
            self.wait(eng, b.w)
        for b in writes:
            self.wait(eng, b.w)
            for t in list(b.r.values()):
                self.wait(eng, t)

    def commit(self, tok, reads, writes):
        for b in reads:
            old = b.r.get(id(tok.sem))
            if old is None or old.val < tok.val:
                b.r[id(tok.sem)] = tok
        for b in writes:
            b.w = tok
            b.r = {}

    def fence(self, engs, bufs):
        for e in engs:
            self.deps(e, (), bufs)

    def op(self, eng, fn, reads=(), writes=()):
        self.deps(eng, reads, writes)
        self.cnt[eng] += 1
        sem = self.sems[eng]
        tok = Tok(sem, self.cnt[eng])
        self.q[eng].append(lambda e: fn(e).then_inc(sem, 1))
        self.commit(tok, reads, writes)
        return tok

    def pe(self, fns, reads=(), writes=()):
        self.deps("pe", reads, writes)
        self.cnt["pe"] += 1
        sem = self.sems["pe"]
        tok = Tok(sem, self.cnt["pe"])
        for f in fns[:-1]:
            self.q["pe"].append(f)
        last = fns[-1]
        self.q["pe"].append(lambda e: last(e).then_inc(sem, 1))
        self.commit(tok, reads, writes)
        return tok

    def dma(self, eng, dsem, fn, reads=(), writes=()):
        self.deps(eng, reads, writes)
        dsem.val += 16
        sem = dsem.sem
        tok = Tok(sem, dsem.val)
        self.q[eng].append(lambda e: fn(e).then_inc(sem, 16))
        self.commit(tok, reads, writes)
        return tok


def build_nc(debug=False):
    nc = bass.Bass("TRN2", target_bir_lowering=False)
    dt_in = lambda n, s: nc.dram_tensor(n, s, F32, kind="ExternalInput").ap()
    x = dt_in("x", [TOKC, D])
    xh = dt_in("xh", [NPASS, 2, D])
    w_in = dt_in("w_in", [EIN // 256, 128, 8192])
    w_out = dt_in("w_out", [16, 128, 8192])
    w_up = dt_in("w_up", [DFF // 256, 128, 8192])
    w_down = dt_in("w_down", [64, 128, 8192])
    mix_g = dt_in("mix_g", [D])
    mlp_g = dt_in("mlp_g", [D])
    fin_g = dt_in("fin_g", [D])
    conv_w = dt_in("conv_w", [3, CW])
    sp_w = dt_in("sp_w", [16, 128, 128])
    sp_b = dt_in("sp_b", [16, 128])
    ga_d = dt_in("ga", [CW])
    gb_d = dt_in("gb", [CW])
    out = nc.dram_tensor("out", [TOKC, D], F32, kind="ExternalOutput").ap()

    with ExitStack() as es:
        sb = lambda n, s, d: es.enter_context(nc.sbuf_tensor(n, s, d))
        R1 = sb("R1", [128, KC * 514], BF16)
        R2 = sb("R2", [128, 8192], F32)
        R3 = sb("R3", [128, 16384], F32)
        slots = [sb("ws%d" % i, [128, 8192], BF16) for i in range(NSLOT)]
        ident_f = sb("ident_f", [128, 128], F32)
        ident_b = sb("ident_b", [128, 128], BF16)
        ones_f = sb("ones_f", [128, 128], F32)
        convw = sb("convw", [128, 3, 16], F32)
        ga = sb("ga_sb", [128, 16], F32)
        gb = sb("gb_sb", [128, 16], F32)
        bias_bc = sb("bias_bc", [128, 16 * 128], F32)
        WmT = sb("WmT", [128, 16 * 128], BF16)
        stat = sb("stat", [128, 32], F32)
        inv_tok = sb("inv_tok", [128, 8], F32)
        statp = sb("statp", [128, 128], F32)
        hsave = sb("hsave", [128, 32], F32)
        pst = [es.enter_context(nc.psum_tensor("ps%d" % i, [128, 512], F32)) for i in range(8)]

        P = Prog(nc, es)
        newsem = lambda n: DmaSem(es.enter_context(nc.semaphore(n)))
        s_slot = [newsem("s_slot%d" % i) for i in range(NSLOT)]
        s_x = [newsem("s_x%d" % i) for i in range(NTT)]
        s_o = [newsem("s_o%d" % i) for i in range(NTT)]
        s_h = newsem("s_h")
        s_g = newsem("s_g")
        s_par = [newsem("s_par%d" % i) for i in range(6)]

        dbg = {"n": 0}

        def dump(name, ap, bufs):
            if not debug:
                return
            shp = list(ap.shape)
            dd = nc.dram_tensor(name, shp, ap.dtype, kind="ExternalOutput").ap()
            ds = newsem("s_dbg%d" % dbg["n"])
            dbg["n"] += 1
            t = P.dma("sp", ds, lambda e: e.dma_start(out=dd, in_=ap), reads=bufs)
            P.wait("sp", t)

        x1 = R1[:].rearrange("p (c t) -> p c t", t=514)
        yT = R2[:].bitcast(BF16).rearrange("p (c t) -> p c t", t=512)
        xnb = R2[:, 0:2048].bitcast(BF16)
        gbc = R2[:, 2048:6144]
        aT = [R2[:, 6144:8192].bitcast(BF16).rearrange("p (c t) -> p c t", t=512),
              R2[:, 0:2048].bitcast(BF16).rearrange("p (c t) -> p c t", t=512)]
        rt = [R2[:, 2048:2560], R2[:, 2560:3072]]
        xnbs = [R2[:, 0:2048].bitcast(BF16), R2[:, 6144:8192].bitcast(BF16)]
        R1f = R1[:].bitcast(F32)
        hst = R1f[0:2, 0:4096]
        acc = R3[:].rearrange("p (a d) -> p a d", d=D)
        Xb = pst[7][:].bitcast(BF16)
        Sb = pst[6][:].bitcast(BF16)
        wm3 = WmT[:].rearrange("p (h t) -> p h t", t=128)
        bias3 = bias_bc[:].rearrange("p (h t) -> p h t", t=128)

        def r3(off, n):
            return R3[:, off:off + n]
        tB = []
        o = 0
        for par in range(2):
            d = {}
            for nm, n in (("bS", 512), ("cS", 516), ("hf", 516), ("t1", 512), ("ya", 512),
                          ("sq", 512), ("gu", 512), ("tb", 512), ("yb", 512)):
                d[nm] = r3(o, n)
                o += n
            d["gvT"] = r3(o, 256).bitcast(BF16)
            o += 256
            d["gv"] = r3(o, 256).bitcast(BF16)
            o += 256
            tB.append(d)
        inv_bc = r3(o, 512)
        o += 512
        spw_f = R3[:, 12288:14336]
        spw_b = R3[:, 14336:15360].bitcast(BF16)

        B_x1 = [Buf("x1_%d" % i) for i in range(NTT)]
        B_x1h = Buf("x1h")
        B_yT = Buf("yT")
        B_xnb = Buf("xnb")
        B_gbc = Buf("gbc")
        B_aT0 = Buf("aT0")
        B_aT = [B_aT0, B_xnb]
        B_acc = [[Buf("acc%d_%d" % (t, c)) for c in range(8)] for t in range(NTT)]
        B_slot = [Buf("slot%d" % i) for i in range(NSLOT)]
        B_ps = [Buf("ps%d" % i) for i in range(8)]
        B_tB = [{k: Buf("tB%d_%s" % (par, k)) for k in tB[par]} for par in range(2)]
        B_invbc = Buf("inv_bc")
        B_stat = Buf("stat")
        B_st = [Buf("st%d" % i) for i in range(16)]
        B_stp = [Buf("stp%d" % i) for i in range(16)]
        B_hst = Buf("hst")
        B_hsave = Buf("hsave")
        B_xn = [B_xnb, B_aT0]
        B_invtok = Buf("inv_tok")
        B_const = Buf("const")
        B_cw = Buf("convw")
        B_gab = Buf("ga")
        B_gbb = Buf("gb")
        B_bias = Buf("bias")
        B_wm = Buf("WmT")
        B_spw = Buf("spw")
        all_tB = [b for dd in B_tB for b in dd.values()] + [B_invbc]
        all_acc = [b for row in B_acc for b in row]
        all_x1 = B_x1 + [B_x1h]

        wstate = {"n": 0}

        pre = []

        def wissue(src_ap, view_fn):
            n = wstate["n"]
            wstate["n"] += 1
            s = n % NSLOT
            dst = view_fn(slots[s])
            flat = slots[s][:]
            P.dma("pool", s_slot[s], lambda e: e.dma_start(out=flat, in_=src_ap), writes=[B_slot[s]])
            return dst, B_slot[s]

        def wtile(src_ap, view_fn, key=None):
            if pre:
                k, r = pre.pop(0)
                assert k == key, (k, key)
                return r
            return wissue(src_ap, view_fn)

        v_in = lambda sl: sl[:].rearrange("p (k n) -> p k n", n=256)
        v_out = lambda sl: sl[:].rearrange("p (k n) -> p k n", n=512)
        v_dn = lambda sl: sl[:].rearrange("p (k n) -> p k n", n=1024)

        P.dma("sp", s_par[0], lambda e: e.dma_start(out=spw_f.rearrange("p (h s) -> p h s", s=128),
                                                   in_=sp_w.rearrange("h t s -> t h s")), writes=[B_spw])
        for tt in range(NTT - 1):
            P.dma("sp", s_x[tt], lambda e, tt=tt: e.dma_start(out=acc[:, tt, :],
                                                               in_=x[tt * 128:(tt + 1) * 128, :]),
                  writes=B_acc[tt])
        P.dma("sp", s_h, lambda e: e.dma_start(out=hst, in_=xh[0]), writes=[B_hst])
        P.dma("sp", s_g, lambda e: e.dma_start(out=gbc, in_=mix_g.partition_broadcast(128)), writes=[B_gbc])

        P.op("dve", lambda e: e.memset(ones_f[:], 1.0), writes=[B_const])
        P.op("dve", lambda e: e.memset(ident_f[:], 0.0), writes=[B_const])
        P.op("pool", lambda e: e.affine_select(out=ident_f[:], in_=ident_f[:], pattern=[[-1, 128]],
                                                compare_op=ALU.not_equal, fill=1.0, base=0,
                                                channel_multiplier=1), writes=[B_const])
        P.op("dve", lambda e: e.tensor_copy(out=ident_b[:], in_=ident_f[:]), writes=[B_const])
        P.op("dve", lambda e: e.memset(stat[:], 0.0), writes=[B_stat] + B_st)
        P.op("dve", lambda e: e.memset(statp[:], 0.0), writes=B_stp)
        P.op("pool", lambda e: e.affine_select(out=spw_f.rearrange("p (h s) -> p h s", s=128),
                                                in_=spw_f.rearrange("p (h s) -> p h s", s=128),
                                                pattern=[[0, 16], [-1, 128]], compare_op=ALU.is_ge,
                                                fill=0.0, base=0, channel_multiplier=1),
             writes=[B_spw])
        P.op("dve", lambda e: e.tensor_copy(out=spw_b, in_=spw_f), reads=[B_spw], writes=[B_spw])
        for half in range(2):
            bank, bb = (Xb, B_ps[7]) if half == 0 else (Sb, B_ps[6])
            fns = []
            for i in range(8):
                h = half * 8 + i
                fns.append(lambda e, i=i, h=h, bank=bank: e.transpose(
                    out=bank[:, i * 128:(i + 1) * 128], in_=spw_b[:, h * 128:(h + 1) * 128],
                    identity=ident_b[:]))
            P.pe(fns, reads=[B_spw, B_const], writes=[bb])
            P.op("dve", lambda e, half=half, bank=bank: e.tensor_copy(
                out=WmT[:, half * 1024:(half + 1) * 1024], in_=bank), reads=[bb], writes=[B_wm])
        P.dma("sp", s_x[NTT - 1], lambda e: e.dma_start(out=acc[:, NTT - 1, :],
                                                         in_=x[(NTT - 1) * 128:NTT * 128, :]),
              writes=B_acc[NTT - 1] + [B_spw])
        pdma = [
            (convw[:], conv_w.rearrange("k (j p) -> p k j", p=128), B_cw),
            (ga[:], ga_d.rearrange("(j p) -> p j", p=128), B_gab),
            (gb[:], gb_d.rearrange("(j p) -> p j", p=128), B_gbb),
            (bias_bc[:], sp_b.rearrange("h t -> (h t)").partition_broadcast(128), B_bias),
        ]
        for i, (o_ap, i_ap, bq) in enumerate(pdma):
            def f(e, o_ap=o_ap, i_ap=i_ap):
                return e.dma_start(out=o_ap, in_=i_ap, allow_slow_non_contiguous=True)
            P.dma("sp", s_par[1 + i], f, writes=[bq])
        P.wait("pool", Tok(s_x[NTT - 1].sem, 16))
        for c0 in (0, CW, 2 * CW, 256):
            pre.append((("in", c0), wissue(w_in[c0 // 256], v_in)))

        bankrot = {"m": 0, "d": 0, "c": 0}

        def main_bank():
            i = bankrot["m"] % 4
            bankrot["m"] += 1
            return pst[i], B_ps[i]

        deferred = []

        def defer(n, fn):
            deferred.append([n, fn])

        def tick():
            ready = []
            for it in deferred:
                it[0] -= 1
            for it in list(deferred):
                if it[0] <= 0:
                    deferred.remove(it)
                    ready.append(it[1])
            for f in ready:
                f()

        def flush():
            while deferred:
                tick()

        def stats_sq(src, src_bufs, slot, n=128):
            for k in range(8):
                jb = k % 2
                P.op("act", lambda e, k=k, jb=jb: e.activation(
                    out=pst[jb][0:n, :], in_=src[0:n, k * 512:(k + 1) * 512], func=AF.Square,
                    accum_out=statp[0:n, slot * 8 + k:slot * 8 + k + 1]),
                    reads=src_bufs, writes=[B_ps[jb], B_stp[slot]])

        def sq_block(tt, dc, slot, jb=7):
            P.op("act", lambda e: e.activation(
                out=pst[jb][:], in_=acc[:, tt, dc * 512:(dc + 1) * 512], func=AF.Square,
                accum_out=statp[:, slot * 8 + dc:slot * 8 + dc + 1]),
                reads=[B_acc[tt][dc]], writes=[B_ps[jb], B_stp[slot]])

        def stats_fin(slot, n=128):
            P.op("dve", lambda e: e.reduce_sum(out=stat[0:n, slot:slot + 1], in_=statp[0:n, slot * 8:slot * 8 + 8],
                                               axis=mybir.AxisListType.X),
                 reads=[B_stp[slot]], writes=[B_st[slot]])
            P.op("act", lambda e: e.activation(out=stat[0:n, slot:slot + 1], in_=stat[0:n, slot:slot + 1],
                                               func=AF.Sqrt, scale=1.0 / D, bias=EPS),
                 reads=[B_st[slot]], writes=[B_st[slot]])
            P.op("dve", lambda e: e.reciprocal(out=stat[0:n, slot:slot + 1], in_=stat[0:n, slot:slot + 1]),
                 reads=[B_st[slot]], writes=[B_st[slot]])

        TB = [(pst[7][:].bitcast(BF16), B_ps[7]), (pst[6][:].bitcast(BF16), B_ps[6]),
              (pst[5][:].bitcast(BF16), B_ps[5]), (pst[4][:].bitcast(BF16), B_ps[4])]

        def nt(tt, slot, g_ap, g_buf, extra_w=()):
            src = acc[:, tt, :]
            xb, bxb = xnbs[tt % 2], B_xn[tt % 2]
            P.op("dve", lambda e: e.scalar_tensor_tensor(out=xb, in0=src, scalar=stat[:, slot:slot + 1],
                                                         in1=g_ap, op0=ALU.mult, op1=ALU.mult),
                 reads=B_acc[tt] + [B_st[slot], g_buf], writes=[bxb] + list(extra_w))
            for f4 in range(4):
                bank, bb = TB[f4]
                fns = []
                for i in range(8):
                    c = f4 * 8 + i
                    fns.append(lambda e, i=i, c=c, bank=bank: e.transpose(
                        out=bank[:, i * 128:(i + 1) * 128], in_=xb[:, c * 128:(c + 1) * 128],
                        identity=ident_b[:]))
                P.pe(fns, reads=[bxb, B_const], writes=[bb])
            for f4 in range(4):
                bank, bb = TB[f4]
                dst = x1[:, f4 * 8:(f4 + 1) * 8, 2 + tt * 128:2 + (tt + 1) * 128]
                srcp = bank.rearrange("p (c t) -> p c t", t=128)
                if tt == 3 and f4 % 2 == 1:
                    P.op("act", lambda e, dst=dst, srcp=srcp: e.copy(out=dst, in_=srcp),
                         reads=[bb], writes=[B_x1[tt]])
                else:
                    P.op("dve", lambda e, dst=dst, srcp=srcp: e.tensor_copy(out=dst, in_=srcp),
                         reads=[bb], writes=[B_x1[tt]])

        for ps_ in range(NPASS):
            tok0 = ps_ * T
            if ps_ > 0:
                P.fence(["sp"], all_tB)
                for tt in range(NTT):
                    r0 = tok0 + tt * 128
                    P.dma("sp", s_x[tt], lambda e, tt=tt, r0=r0: e.dma_start(out=acc[:, tt, :],
                                                                             in_=x[r0:r0 + 128, :]),
                          writes=B_acc[tt])
            if ps_ > 0:
                P.dma("sp", s_g, lambda e: e.dma_start(out=gbc, in_=mix_g.partition_broadcast(128)),
                      writes=[B_gbc])
            gm, bgm = gbc, B_gbc
            P.fence(["dve"], [B_yT, B_aT0])
            if ps_ == 0:
                stats_sq(hst, [B_hst], 8, n=2)
                stats_fin(8, n=2)
                P.op("dve", lambda e, gm=gm: e.scalar_tensor_tensor(out=xnbs[1][0:2, :], in0=hst,
                                                             scalar=stat[0:2, 8:9], in1=gm[0:2, :],
                                                             op0=ALU.mult, op1=ALU.mult),
                     reads=[B_hst, B_st[8], bgm], writes=[B_xn[1]])
                fns = []
                for c in range(KC):
                    fns.append(lambda e, c=c: e.transpose(out=Sb[:, c * 2:(c + 1) * 2],
                                                          in_=xnbs[1][0:2, c * 128:(c + 1) * 128],
                                                          identity=ident_b[0:2, 0:2]))
                P.pe(fns, reads=[B_xn[1], B_const], writes=[B_ps[6]])
                P.op("dve", lambda e: e.tensor_copy(out=x1[:, :, 0:2],
                                                    in_=Sb[:, 0:64].rearrange("p (c t) -> p c t", t=2)),
                     reads=[B_ps[6]], writes=[B_x1h, B_hst])
                P.fence(["act", "dve"], [B_hst])
            for tt in range(NTT):
                stats_sq(acc[:, tt, :], B_acc[tt], tt)
                stats_fin(tt)
                nt(tt, tt, gm, bgm)

            if ps_ == 0:
                dump("d_x1", R1[:], all_x1)
            P.fence(["act", "dve"], all_acc + [B_xnb, B_aT0, B_gbc])
            S_bank, B_S = pst[6], B_ps[6]
            Hb = [(pst[4], B_ps[4]), (pst[5], B_ps[5])]
            hcnt = {"n": 0}

            def in_chunk(slot, bslot, i, with_halo):
                hal = None
                if with_halo:
                    hb, bhb = Hb[hcnt["n"] % 2]
                    hcnt["n"] += 1
                    fns = []
                    for kc in range(KC):
                        fns.append(lambda e, kc=kc, hb=hb: e.matmul(
                            hb[:, 0:2], lhsT=slot[:, kc, i * 128:(i + 1) * 128], rhs=x1[:, kc, 0:2],
                            start=(kc == 0), stop=(kc == KC - 1)))
                    P.pe(fns, reads=[bslot, B_x1h], writes=[bhb])
                    hal = (hb[:, 0:2], bhb)
                bank, bb = main_bank()
                fns = []
                for kc in range(KC):
                    fns.append(lambda e, kc=kc, bank=bank: e.matmul(
                        bank[:], lhsT=slot[:, kc, i * 128:(i + 1) * 128], rhs=x1[:, kc, 2:514],
                        start=(kc == 0), stop=(kc == KC - 1)))
                P.pe(fns, reads=[bslot] + B_x1, writes=[bb])
                return bank, bb, hal

            def ssq_mm(sq_ap, bsq, first, last):
                P.pe([lambda e: e.matmul(S_bank[:], lhsT=ones_f[:], rhs=sq_ap, start=first, stop=last)],
                     reads=[bsq, B_const], writes=[B_S])

            def extract_inv(col0):
                P.op("act", lambda e: e.activation(out=inv_bc, in_=S_bank[:], func=AF.Sqrt,
                                                   scale=1.0 / CW, bias=EPS),
                     reads=[B_S], writes=[B_invbc])
                P.op("dve", lambda e: e.reciprocal(out=inv_bc, in_=inv_bc), reads=[B_invbc], writes=[B_invbc])
                bank, bb = main_bank()
                fns = []
                for tt in range(NTT):
                    fns.append(lambda e, tt=tt, bank=bank: e.transpose(
                        out=bank[:, tt * 128:(tt + 1) * 128], in_=inv_bc[:, tt * 128:(tt + 1) * 128],
                        identity=ident_f[:]))
                P.pe(fns, reads=[B_invbc, B_const], writes=[bb])
                P.op("dve", lambda e, bank=bank: e.tensor_copy(out=inv_tok[:, col0:col0 + 4],
                                                               in_=bank[:, 0:512:128]),
                     reads=[bb], writes=[B_invtok])

            for g in range(8):
                for kind, base in (("b", 0), ("c", CW), ("h", 2 * CW)):
                    c0 = base + g * 256
                    if ps_ > 0 and g == 0 and kind == "b":
                        P.wait("pool", Tok(s_x[NTT - 1].sem, s_x[NTT - 1].val))
                    slot, bslot = wtile(w_in[c0 // 256], v_in,
                                        key=("in", c0))
                    for i in range(2):
                        j = g * 2 + i
                        tb, btb = tB[i], B_tB[i]
                        bank, bb, hal = in_chunk(slot, bslot, i, kind != "b" and ps_ == 0)
                        if kind == "b":
                            P.op("act", lambda e, tb=tb, bank=bank: e.copy(out=tb["bS"], in_=bank[:]),
                                 reads=[bb], writes=[btb["bS"]])
                        elif kind == "c":
                            P.op("act", lambda e, tb=tb, bank=bank: e.copy(out=tb["cS"][:, 2:514], in_=bank[:]),
                                 reads=[bb], writes=[btb["cS"]])
                            if ps_ == 0:
                                P.op("act", lambda e, tb=tb, hal=hal: e.copy(out=tb["cS"][:, 0:2], in_=hal[0]),
                                     reads=[hal[1]], writes=[btb["cS"]])
                        else:
                            P.op("dve", lambda e, tb=tb, bank=bank: e.tensor_tensor(
                                out=tb["hf"][:, 2:514], in0=bank[:], in1=tb["cS"][:, 2:514], op=ALU.mult),
                                reads=[bb, btb["cS"]], writes=[btb["hf"]])
                            if ps_ == 0:
                                P.op("dve", lambda e, tb=tb, hal=hal: e.tensor_tensor(
                                    out=tb["hf"][:, 0:2], in0=hal[0], in1=tb["cS"][:, 0:2], op=ALU.mult),
                                    reads=[hal[1], btb["cS"]], writes=[btb["hf"]])
                                P.op("dve", lambda e, tb=tb, j=j: e.tensor_copy(
                                    out=hsave[:, 2 * j:2 * j + 2], in_=tb["hf"][:, 512:514]),
                                    reads=[btb["hf"]], writes=[B_hsave])
                            else:
                                P.op("dve", lambda e, tb=tb, j=j: e.tensor_copy(
                                    out=tb["hf"][:, 0:2], in_=hsave[:, 2 * j:2 * j + 2]),
                                    reads=[B_hsave], writes=[btb["hf"]])
                            P.op("dve", lambda e, tb=tb, j=j: e.tensor_scalar(
                                out=tb["t1"], in0=tb["hf"][:, 0:512], scalar1=convw[:, 0, j:j + 1],
                                scalar2=None, op0=ALU.mult),
                                reads=[btb["hf"], B_cw], writes=[btb["t1"]])
                            for k in (1, 2):
                                P.op("dve", lambda e, tb=tb, j=j, k=k: e.scalar_tensor_tensor(
                                    out=tb["t1"], in0=tb["hf"][:, k:k + 512], scalar=convw[:, k, j:j + 1],
                                    in1=tb["t1"], op0=ALU.mult, op1=ALU.add),
                                    reads=[btb["hf"], btb["t1"], B_cw], writes=[btb["t1"]])
                            P.op("dve", lambda e, tb=tb: e.tensor_tensor(
                                out=tb["ya"], in0=tb["t1"], in1=tb["bS"], op=ALU.mult),
                                reads=[btb["t1"], btb["bS"]], writes=[btb["ya"]])
                            P.op("act", lambda e, tb=tb: e.activation(out=tb["sq"], in_=tb["ya"], func=AF.Square),
                                 reads=[btb["ya"]], writes=[btb["sq"]])
                            P.op("act", lambda e, tb=tb, j=j: e.activation(
                                out=yT[:, j, :], in_=tb["ya"], func=AF.Copy, scale=ga[:, j:j + 1]),
                                reads=[btb["ya"], B_gab], writes=[B_yT])
                            defer(3, lambda tb=tb, btb=btb, j=j: ssq_mm(tb["sq"], btb["sq"], j == 0, j == 15))
                        tick()
            defer(4, lambda: extract_inv(0))

            for g in range(8):
                for kind, base in (("v", 4 * CW), ("u", 3 * CW)):
                    c0 = base + g * 256
                    slot, bslot = wtile(w_in[c0 // 256], v_in)
                    for i in range(2):
                        hd = g * 2 + i
                        tb, btb = tB[i], B_tB[i]
                        bank, bb, _ = in_chunk(slot, bslot, i, False)
                        if kind == "u":
                            P.op("act", lambda e, tb=tb, bank=bank: e.activation(
                                out=tb["gu"], in_=bank[:], func=AF.Gelu_apprx_tanh),
                                reads=[bb], writes=[btb["gu"]])
                        else:
                            P.op("act", lambda e, tb=tb, bank=bank: e.activation(
                                out=tb["gvT"], in_=bank[:], func=AF.Gelu_apprx_tanh),
                                reads=[bb], writes=[btb["gvT"]])

                            def stage1(tb=tb, btb=btb, hd=hd):
                                fns = []
                                for ck in range(4):
                                    fns.append(lambda e, ck=ck: e.transpose(
                                        out=Xb[:, ck * 128:(ck + 1) * 128],
                                        in_=tb["gvT"][:, ck * 128:(ck + 1) * 128], identity=ident_b[:]))
                                P.pe(fns, reads=[btb["gvT"], B_const], writes=[B_ps[7]])
                                P.op("dve", lambda e: e.tensor_copy(out=tb["gv"], in_=Xb[:, 0:512]),
                                     reads=[B_ps[7]], writes=[btb["gv"]])
                                defer(1, lambda: stage2(tb, btb, hd))

                            def stage2(tb, btb, hd):
                                hs, bhs = Hb[hd % 2]
                                fns = []
                                for ck in range(4):
                                    fns.append(lambda e, ck=ck: e.matmul(
                                        hs[:, ck * 128:(ck + 1) * 128],
                                        lhsT=tb["gv"][:, ck * 128:(ck + 1) * 128], rhs=wm3[:, hd, :],
                                        start=True, stop=True))
                                P.pe(fns, reads=[btb["gv"], B_wm], writes=[bhs])
                                P.op("dve", lambda e: e.tensor_tensor(
                                    out=tb["tb"].rearrange("p (c t) -> p c t", t=128),
                                    in0=hs[:].rearrange("p (c t) -> p c t", t=128),
                                    in1=bias3[:, hd, :].unsqueeze(1).to_broadcast([128, 4, 128]), op=ALU.add),
                                    reads=[bhs, B_bias], writes=[btb["tb"]])
                                P.op("dve", lambda e: e.tensor_tensor(out=tb["yb"], in0=tb["tb"], in1=tb["gu"],
                                                                      op=ALU.mult),
                                     reads=[btb["tb"], btb["gu"]], writes=[btb["yb"]])
                                P.op("act", lambda e: e.activation(out=tb["sq"], in_=tb["yb"], func=AF.Square),
                                     reads=[btb["yb"]], writes=[btb["sq"]])
                                P.op("act", lambda e: e.activation(out=yT[:, 16 + hd, :], in_=tb["yb"],
                                                                   func=AF.Copy, scale=gb[:, hd:hd + 1]),
                                     reads=[btb["yb"], B_gbb], writes=[B_yT])
                                defer(1, lambda: ssq_mm(tb["sq"], btb["sq"], hd == 0, hd == 15))

                            defer(2, stage1)
                        tick()
            flush()
            extract_inv(4)

            if ps_ == 0:
                dump("d_yT", R2[:], [B_yT])
                dump("d_inv", inv_tok[:], [B_invtok])
            g2stage = R1f[:, 0:4096]
            P.dma("sp", s_g, lambda e: e.dma_start(out=g2stage, in_=mlp_g.partition_broadcast(128)),
                  writes=all_x1)
            P.fence(["sp"], all_tB)
            for tt in range(NTT):
                r0 = tok0 + tt * 128
                P.dma("sp", s_x[tt], lambda e, tt=tt, r0=r0: e.dma_start(out=acc[:, tt, :],
                                                                         in_=x[r0:r0 + 128, :]),
                      writes=B_acc[tt] + all_tB)
            for dc in range(8):
                for half in range(2):
                    slot, bslot = wtile(w_out[dc * 2 + half], v_out)
                    for tt in range(NTT):
                        bi = bankrot["c"] % 7
                        bankrot["c"] += 1
                        bank, bb = pst[bi], B_ps[bi]
                        fns = []
                        for cc in range(16):
                            fns.append(lambda e, cc=cc, bank=bank, tt=tt, half=half, slot=slot: e.matmul(
                                bank[:], lhsT=yT[:, half * 16 + cc, tt * 128:(tt + 1) * 128],
                                rhs=slot[:, cc, :], start=(cc == 0), stop=(cc == 15)))
                        P.pe(fns, reads=[bslot, B_yT], writes=[bb])
                        dst = acc[:, tt, dc * 512:(dc + 1) * 512]
                        P.op("dve", lambda e, bank=bank, dst=dst, tt=tt, half=half: e.scalar_tensor_tensor(
                            out=dst, in0=bank[:], scalar=inv_tok[:, half * 4 + tt:half * 4 + tt + 1],
                            in1=dst, op0=ALU.mult, op1=ALU.add),
                            reads=[bb, B_invtok, B_acc[tt][dc]], writes=[B_acc[tt][dc]])
                        if half == 1:
                            sq_block(tt, dc, 4 + tt)

            if ps_ == 0:
                dump("d_h", R3[:], all_acc)
            P.op("act", lambda e: e.copy(out=gbc, in_=g2stage), reads=all_x1, writes=[B_gbc, B_yT])
            g2, bg2 = gbc, B_gbc
            P.fence(["act", "dve"], all_x1)
            for tt in range(NTT):
                stats_fin(4 + tt)
            for tt in range(NTT):
                nt(tt, 4 + tt, g2, bg2, extra_w=[B_yT] if tt < 2 else ())

            if ps_ == 0:
                dump("d_x2", R1[:], all_x1)
            P.fence(["act", "dve"], [B_xnb])
            dbank = {"n": 0}
            upb = {"n": 0}
            P.dma("sp", s_g, lambda e: e.dma_start(out=gbc, in_=fin_g.partition_broadcast(128)),
                  writes=[B_gbc])

            def up_block(fb):
                for u in range(4):
                    c0 = fb * FB + u * 256
                    slot, bslot = wtile(w_up[c0 // 256], v_in)
                    for i in range(2):
                        fcl = u * 2 + i
                        bi = upb["n"] % 3
                        upb["n"] += 1
                        bank, bb = pst[bi], B_ps[bi]
                        fns = []
                        for kc in range(KC):
                            fns.append(lambda e, kc=kc, bank=bank, slot=slot, i=i: e.matmul(
                                bank[:], lhsT=slot[:, kc, i * 128:(i + 1) * 128], rhs=x1[:, kc, 2:514],
                                start=(kc == 0), stop=(kc == KC - 1)))
                        P.pe(fns, reads=[bslot] + B_x1, writes=[bb])
                        P.op("act", lambda e, bank=bank: e.activation(out=pst[3][:], in_=bank[:], func=AF.Relu),
                             reads=[bb], writes=[B_ps[3]])
                        P.op("act", lambda e, fb=fb, fcl=fcl: e.activation(
                            out=aT[fb % 2][:, fcl, :], in_=pst[3][:], func=AF.Square),
                            reads=[B_ps[3]], writes=[B_aT[fb % 2]])

            def down_block(fb):
                for dq in range(4):
                    slot, bslot = wtile(w_down[fb * 4 + dq], v_dn)
                    for dcl in range(2):
                        dc = dq * 2 + dcl
                        for tt in range(NTT):
                            bi = 4 + dbank["n"] % 4
                            dbank["n"] += 1
                            bank, bb = pst[bi], B_ps[bi]
                            fns = []
                            for fc in range(8):
                                fns.append(lambda e, fc=fc, bank=bank, slot=slot, tt=tt, dcl=dcl, fb=fb: e.matmul(
                                    bank[:], lhsT=aT[fb % 2][:, fc, tt * 128:(tt + 1) * 128],
                                    rhs=slot[:, fc, dcl * 512:(dcl + 1) * 512], start=(fc == 0), stop=(fc == 7)))
                            P.pe(fns, reads=[bslot, B_aT[fb % 2]], writes=[bb])
                            dst = acc[:, tt, dc * 512:(dc + 1) * 512]
                            P.op("dve", lambda e, bank=bank, dst=dst: e.tensor_tensor(
                                out=dst, in0=bank[:], in1=dst, op=ALU.add),
                                reads=[bb, B_acc[tt][dc]], writes=[B_acc[tt][dc]])
                            if fb == NFB - 1:
                                sq_block(tt, dc, 12 + tt, jb=0)

            up_block(0)
            for fb in range(NFB):
                if fb + 1 < NFB:
                    up_block(fb + 1)
                down_block(fb)

            g3, bg3 = gbc, B_gbc
            for tt in range(NTT):
                stats_fin(12 + tt)
                src = acc[:, tt, :]
                P.op("dve", lambda e, src=src, tt=tt, g3=g3: e.scalar_tensor_tensor(
                    out=src, in0=src, scalar=stat[:, 12 + tt:13 + tt], in1=g3, op0=ALU.mult, op1=ALU.mult),
                    reads=B_acc[tt] + [B_st[12 + tt], bg3], writes=B_acc[tt])
                r0 = tok0 + tt * 128
                P.dma("sp", s_o[tt], lambda e, src=src, r0=r0: e.dma_start(out=out[r0:r0 + 128, :], in_=src),
                      reads=B_acc[tt])
            P.op("dve", lambda e: e.memset(stat[:], 0.0), writes=[B_stat] + B_st)
            P.op("dve", lambda e: e.memset(statp[:], 0.0), writes=B_stp)

        for tt in range(NTT):
            P.wait("sp", Tok(s_o[tt].sem, s_o[tt].val))

        with nc.Block() as block:
            @block.sync
            def _(e):
                for f in P.q["sp"]:
                    f(e)

            @block.gpsimd
            def _(e):
                for f in P.q["pool"]:
                    f(e)

            @block.tensor
            def _(e):
                for f in P.q["pe"]:
                    f(e)

            @block.scalar
            def _(e):
                for f in P.q["act"]:
                    f(e)

            @block.vector
            def _(e):
                for f in P.q["dve"]:
                    f(e)
    return nc


def tile_weights(w_in, w_out, w_up, w_down):
    c = np.ascontiguousarray
    wi = c(w_in.reshape(32, 128, EIN // 256, 256).transpose(2, 1, 0, 3)).reshape(EIN // 256, 128, 8192)
    wu = c(w_up.reshape(32, 128, DFF // 256, 256).transpose(2, 1, 0, 3)).reshape(DFF // 256, 128, 8192)
    wo = c(w_out.reshape(2, 16, 128, 8, 512).transpose(3, 0, 2, 1, 4)).reshape(16, 128, 8192)
    wd = c(w_down.reshape(16, 8, 128, 4, 1024).transpose(0, 3, 2, 1, 4)).reshape(64, 128, 8192)
    return wi, wo, wu, wd


def kernel(x, mix_norm_g, w_in, conv_w, spatial_w, spatial_b, conv_out_norm_g, gmlp_out_norm_g,
           w_out, mlp_norm_g, w_up, w_down, final_norm_g):
    f32 = lambda a: np.ascontiguousarray(np.asarray(a, dtype=np.float32))
    x = f32(x)
    Bn, S, Dm = x.shape
    xf = x.reshape(Bn * S, Dm)
    ncores = 8
    wi, wo, wu, wd = tile_weights(f32(w_in)[0], f32(w_out)[0], f32(w_up)[0], f32(w_down)[0])
    shared = {
        "w_in": wi, "w_out": wo, "w_up": wu, "w_down": wd,
        "mix_g": f32(mix_norm_g)[0], "mlp_g": f32(mlp_norm_g)[0], "fin_g": f32(final_norm_g),
        "conv_w": f32(conv_w)[0], "sp_w": f32(spatial_w)[0], "sp_b": f32(spatial_b)[0],
        "ga": f32(conv_out_norm_g)[0], "gb": f32(gmlp_out_norm_g)[0],
    }
    in_maps = []
    for c in range(ncores):
        r0 = c * TOKC
        xc = xf[r0:r0 + TOKC]
        xh = np.zeros((NPASS, 2, Dm), np.float32)
        if r0 % S != 0:
            xh[0] = xf[r0 - 2:r0]
        xh[1] = xc[T - 2:T]
        m = {"x": np.ascontiguousarray(xc), "xh": xh}
        m.update(shared)
        in_maps.append(m)
    nc = build_nc()
    res = run_bass_kernel_spmd(nc, in_maps, core_ids=list(range(ncores)))
    outs = [np.asarray(r["out"], dtype=np.float32) for r in res.results]
    return np.concatenate(outs, axis=0).reshape(Bn, S, Dm)
```

```python
import numpy as np
from contextlib import ExitStack
import concourse.bass as bass
import concourse.mybir as mybir
from concourse.bass_utils import run_bass_kernel_spmd

F32 = mybir.dt.float32
BF16 = mybir.dt.bfloat16
AF = mybir.ActivationFunctionType
ALU = mybir.AluOpType

D = 4096
T = 512
NTT = 4
NPASS = 2
TOKC = 1024
KC = 32
EIN = 10240
CW = 2048
DFF = 16384
EPS = 1e-5
NSLOT = 4
FB = 1024
NFB = DFF // FB


class Tok:
    __slots__ = ("sem", "val")

    def __init__(self, sem, val):
        self.sem = sem
        self.val = val


class Buf:
    def __init__(self, name):
        self.name = name
        self.w = None
        self.r = {}


class DmaSem:
    def __init__(self, sem):
        self.sem = sem
        self.val = 0


class Prog:
    def __init__(self, nc, es):
        self.nc = nc
        self.q = {e: [] for e in ("pe", "act", "dve", "pool", "sp")}
        self.sems = {}
        for e in ("pe", "act", "dve", "pool"):
            self.sems[e] = es.enter_context(nc.semaphore("c_" + e))
        self.cnt = {e: 0 for e in self.sems}
        self.waited = {}

    def wait(self, eng, tok):
        if tok is None:
            return
        key = (eng, id(tok.sem))
        if self.waited.get(key, 0) >= tok.val:
            return
        self.waited[key] = tok.val
        sem, val = tok.sem, tok.val
        self.q[eng].append(lambda e: e.wait_ge(sem, val))

    def deps(self, eng, reads, writes):
        for b in reads:
            self.wait(eng, b.w)
        for b in writes:
            self.wait(eng, b.w)
            for t in list(b.r.values()):
                self.wait(eng, t)

    def commit(self, tok, reads, writes):
        for b in reads:
            old = b.r.get(id(tok.sem))
            if old is None or old.val < tok.val:
                b.r[id(tok.sem)] = tok
        for b in writes:
            b.w = tok
            b.r = {}

    def fence(self, engs, bufs):
        for e in engs:
            self.deps(e, (), bufs)

    def op(self, eng, fn, reads=(), writes=()):
        self.deps(eng, reads, writes)
        self.cnt[eng] += 1
        sem = self.sems[eng]
        tok = Tok(sem, self.cnt[eng])
        self.q[eng].append(lambda e: fn(e).then_inc(sem, 1))
        self.commit(tok, reads, writes)
        return tok

    def pe(self, fns, reads=(), writes=()):
        self.deps("pe", reads, writes)
        self.cnt["pe"] += 1
        sem = self.sems["pe"]
        tok = Tok(sem, self.cnt["pe"])
        for f in fns[:-1]:
            self.q["pe"].append(f)
        last = fns[-1]
        self.q["pe"].append(lambda e: last(e).then_inc(sem, 1))
        self.commit(tok, reads, writes)
        return tok

    def dma(self, eng, dsem, fn, reads=(), writes=()):
        self.deps(eng, reads, writes)
        dsem.val += 16
        sem = dsem.sem
        tok = Tok(sem, dsem.val)
        self.q[eng].append(lambda e: fn(e).then_inc(sem, 16))
        self.commit(tok, reads, writes)
        return tok


def build_nc(debug=False):
    nc = bass.Bass("TRN2", target_bir_lowering=False)
    dt_in = lambda n, s: nc.dram_tensor(n, s, F32, kind="ExternalInput").ap()
    x = dt_in("x", [TOKC, D])
    xh = dt_in("xh", [NPASS, 2, D])
    w_in = dt_in("w_in", [EIN // 256, 128, 8192])
    w_out = dt_in("w_out", [16, 128, 8192])
    w_up = dt_in("w_up", [DFF // 256, 128, 8192])
    w_down = dt_in("w_down", [64, 128, 8192])
    mix_g = dt_in("mix_g", [D])
    mlp_g = dt_in("mlp_g", [D])
    fin_g = dt_in("fin_g", [D])
    conv_w = dt_in("conv_w", [3, CW])
    sp_w = dt_in("sp_w", [16, 128, 128])
    sp_b = dt_in("sp_b", [16, 128])
    ga_d = dt_in("ga", [CW])
    gb_d = dt_in("gb", [CW])
    out = nc.dram_tensor("out", [TOKC, D], F32, kind="ExternalOutput").ap()

    with ExitStack() as es:
        sb = lambda n, s, d: es.enter_context(nc.sbuf_tensor(n, s, d))
        R1 = sb("R1", [128, KC * 514], BF16)
        R2 = sb("R2", [128, 8192], F32)
        R3 = sb("R3", [128, 16384], F32)
        slots = [sb("ws%d" % i, [128, 8192], BF16) for i in range(NSLOT)]
        ident_f = sb("ident_f", [128, 128], F32)
        ident_b = sb("ident_b", [128, 128], BF16)
        ones_f = sb("ones_f", [128, 128], F32)
        convw = sb("convw", [128, 3, 16], F32)
        ga = sb("ga_sb", [128, 16], F32)
        gb = sb("gb_sb", [128, 16], F32)
        bias_bc = sb("bias_bc", [128, 16 * 128], F32)
        WmT = sb("WmT", [128, 16 * 128], BF16)
        stat = sb("stat", [128, 32], F32)
        inv_tok = sb("inv_tok", [128, 8], F32)
        statp = sb("statp", [128, 128], F32)
        hsave = sb("hsave", [128, 32], F32)
        pst = [es.enter_context(nc.psum_tensor("ps%d" % i, [128, 512], F32)) for i in range(8)]

        P = Prog(nc, es)
        newsem = lambda n: DmaSem(es.enter_context(nc.semaphore(n)))
        s_slot = [newsem("s_slot%d" % i) for i in range(NSLOT)]
        s_x = [newsem("s_x%d" % i) for i in range(NTT)]
        s_o = [newsem("s_o%d" % i) for i in range(NTT)]
        s_h = newsem("s_h")
        s_g = newsem("s_g")
        s_par = [newsem("s_par%d" % i) for i in range(6)]

        dbg = {"n": 0}

        def dump(name, ap, bufs):
            if not debug:
                return
            shp = list(ap.shape)
            dd = nc.dram_tensor(name, shp, ap.dtype, kind="ExternalOutput").ap()
            ds = newsem("s_dbg%d" % dbg["n"])
            dbg["n"] += 1
            t = P.dma("sp", ds, lambda e: e.dma_start(out=dd, in_=ap), reads=bufs)
            P.wait("sp", t)

        x1 = R1[:].rearrange("p (c t) -> p c t", t=514)
        yT = R2[:].bitcast(BF16).rearrange("p (c t) -> p c t", t=512)
        xnb = R2[:, 0:2048].bitcast(BF16)
        gbc = R2[:, 2048:6144]
        aT = [R2[:, 6144:8192].bitcast(BF16).rearrange("p (c t) -> p c t", t=512),
              R2[:, 0:2048].bitcast(BF16).rearrange("p (c t) -> p c t", t=512)]
        rt = [R2[:, 2048:2560], R2[:, 2560:3072]]
        xnbs = [R2[:, 0:2048].bitcast(BF16), R2[:, 6144:8192].bitcast(BF16)]
        R1f = R1[:].bitcast(F32)
        hst = R1f[0:2, 0:4096]
        acc = R3[:].rearrange("p (a d) -> p a d", d=D)
        Xb = pst[7][:].bitcast(BF16)
        Sb = pst[6][:].bitcast(BF16)
        wm3 = WmT[:].rearrange("p (h t) -> p h t", t=128)
        bias3 = bias_bc[:].rearrange("p (h t) -> p h t", t=128)

        def r3(off, n):
            return R3[:, off:off + n]
        tB = []
        o = 0
        for par in range(2):
            d = {}
            for nm, n in (("bS", 512), ("cS", 516), ("hf", 516), ("t1", 512), ("ya", 512),
                          ("sq", 512), ("gu", 512), ("tb", 512), ("yb", 512)):
                d[nm] = r3(o, n)
                o += n
            d["gvT"] = r3(o, 256).bitcast(BF16)
            o += 256
            d["gv"] = r3(o, 256).bitcast(BF16)
            o += 256
            tB.append(d)
        inv_bc = r3(o, 512)
        o += 512
        spw_f = R3[:, 12288:14336]
        spw_b = R3[:, 14336:15360].bitcast(BF16)

        B_x1 = [Buf("x1_%d" % i) for i in range(NTT)]
        B_x1h = Buf("x1h")
        B_yT = Buf("yT")
        B_xnb = Buf("xnb")
        B_gbc = Buf("gbc")
        B_aT0 = Buf("aT0")
        B_aT = [B_aT0, B_xnb]
        B_acc = [[Buf("acc%d_%d" % (t, c)) for c in range(8)] for t in range(NTT)]
        B_slot = [Buf("slot%d" % i) for i in range(NSLOT)]
        B_ps = [Buf("ps%d" % i) for i in range(8)]
        B_tB = [{k: Buf("tB%d_%s" % (par, k)) for k in tB[par]} for par in range(2)]
        B_invbc = Buf("inv_bc")
        B_stat = Buf("stat")
        B_st = [Buf("st%d" % i) for i in range(16)]
        B_stp = [Buf("stp%d" % i) for i in range(16)]
        B_hst = Buf("hst")
        B_hsave = Buf("hsave")
        B_xn = [B_xnb, B_aT0]
        B_invtok = Buf("inv_tok")
        B_const = Buf("const")
        B_cw = Buf("convw")
        B_gab = Buf("ga")
        B_gbb = Buf("gb")
        B_bias = Buf("bias")
        B_wm = Buf("WmT")
        B_spw = Buf("spw")
        all_tB = [b for dd in B_tB for b in dd.values()] + [B_invbc]
        all_acc = [b for row in B_acc for b in row]
        all_x1 = B_x1 + [B_x1h]

        wstate = {"n": 0}

        pre = []

        def wissue(src_ap, view_fn):
            n = wstate["n"]
            wstate["n"] += 1
            s = n % NSLOT
            dst = view_fn(slots[s])
            flat = slots[s][:]
            P.dma("pool", s_slot[s], lambda e: e.dma_start(out=flat, in_=src_ap), writes=[B_slot[s]])
            return dst, B_slot[s]

        def wtile(src_ap, view_fn, key=None):
            if pre:
                k, r = pre.pop(0)
                assert k == key, (k, key)
                return r
            return wissue(src_ap, view_fn)

        v_in = lambda sl: sl[:].rearrange("p (k n) -> p k n", n=256)
        v_out = lambda sl: sl[:].rearrange("p (k n) -> p k n", n=512)
        v_dn = lambda sl: sl[:].rearrange("p (k n) -> p k n", n=1024)

        P.dma("sp", s_par[0], lambda e: e.dma_start(out=spw_f.rearrange("p (h s) -> p h s", s=128),
                                                   in_=sp_w.rearrange("h t s -> t h s")), writes=[B_spw])
        for tt in range(NTT - 1):
            P.dma("sp", s_x[tt], lambda e, tt=tt: e.dma_start(out=acc[:, tt, :],
                                                               in_=x[tt * 128:(tt + 1) * 128, :]),
                  writes=B_acc[tt])
        P.dma("sp", s_h, lambda e: e.dma_start(out=hst, in_=xh[0]), writes=[B_hst])
        P.dma("sp", s_g, lambda e: e.dma_start(out=gbc, in_=mix_g.partition_broadcast(128)), writes=[B_gbc])

        P.op("dve", lambda e: e.memset(ones_f[:], 1.0), writes=[B_const])
        P.op("dve", lambda e: e.memset(ident_f[:], 0.0), writes=[B_const])
        P.op("pool", lambda e: e.affine_select(out=ident_f[:], in_=ident_f[:], pattern=[[-1, 128]],
                                                compare_op=ALU.not_equal, fill=1.0, base=0,
                                                channel_multiplier=1), writes=[B_const])
        P.op("dve", lambda e: e.tensor_copy(out=ident_b[:], in_=ident_f[:]), writes=[B_const])
        P.op("dve", lambda e: e.memset(stat[:], 0.0), writes=[B_stat] + B_st)
        P.op("dve", lambda e: e.memset(statp[:], 0.0), writes=B_stp)
        P.op("pool", lambda e: e.affine_select(out=spw_f.rearrange("p (h s) -> p h s", s=128),
                                                in_=spw_f.rearrange("p (h s) -> p h s", s=128),
                                                pattern=[[0, 16], [-1, 128]], compare_op=ALU.is_ge,
                                                fill=0.0, base=0, channel_multiplier=1),
             writes=[B_spw])
        P.op("dve", lambda e: e.tensor_copy(out=spw_b, in_=spw_f), reads=[B_spw], writes=[B_spw])
        for half in range(2):
            bank, bb = (Xb, B_ps[7]) if half == 0 else (Sb, B_ps[6])
            fns = []
            for i in range(8):
                h = half * 8 + i
                fns.append(lambda e, i=i, h=h, bank=bank: e.transpose(
                    out=bank[:, i * 128:(i + 1) * 128], in_=spw_b[:, h * 128:(h + 1) * 128],
                    identity=ident_b[:]))
            P.pe(fns, reads=[B_spw, B_const], writes=[bb])
            P.op("dve", lambda e, half=half, bank=bank: e.tensor_copy(
                out=WmT[:, half * 1024:(half + 1) * 1024], in_=bank), reads=[bb], writes=[B_wm])
        P.dma("sp", s_x[NTT - 1], lambda e: e.dma_start(out=acc[:, NTT - 1, :],
                                                         in_=x[(NTT - 1) * 128:NTT * 128, :]),
              writes=B_acc[NTT - 1] + [B_spw])
        pdma = [
            (convw[:], conv_w.rearrange("k (j p) -> p k j", p=128), B_cw),
            (ga[:], ga_d.rearrange("(j p) -> p j", p=128), B_gab),
            (gb[:], gb_d.rearrange("(j p) -> p j", p=128), B_gbb),
            (bias_bc[:], sp_b.rearrange("h t -> (h t)").partition_broadcast(128), B_bias),
        ]
        for i, (o_ap, i_ap, bq) in enumerate(pdma):
            def f(e, o_ap=o_ap, i_ap=i_ap):
                return e.dma_start(out=o_ap, in_=i_ap, allow_slow_non_contiguous=True)
            P.dma("sp", s_par[1 + i], f, writes=[bq])
        P.wait("pool", Tok(s_x[NTT - 1].sem, 16))
        for c0 in (0, CW, 2 * CW, 256):
            pre.append((("in", c0), wissue(w_in[c0 // 256], v_in)))

        bankrot = {"m": 0, "d": 0, "c": 0}

        def main_bank():
            i = bankrot["m"] % 4
            bankrot["m"] += 1
            return pst[i], B_ps[i]

        deferred = []

        def defer(n, fn):
            deferred.append([n, fn])

        def tick():
            ready = []
            for it in deferred:
                it[0] -= 1
            for it in list(deferred):
                if it[0] <= 0:
                    deferred.remove(it)
                    ready.append(it[1])
            for f in ready:
                f()

        def flush():
            while deferred:
                tick()

        def stats_sq(src, src_bufs, slot, n=128):
            for k in range(8):
                jb = k % 2
                P.op("act", lambda e, k=k, jb=jb: e.activation(
                    out=pst[jb][0:n, :], in_=src[0:n, k * 512:(k + 1) * 512], func=AF.Square,
                    accum_out=statp[0:n, slot * 8 + k:slot * 8 + k + 1]),
                    reads=src_bufs, writes=[B_ps[jb], B_stp[slot]])

        def sq_block(tt, dc, slot, jb=7):
            P.op("act", lambda e: e.activation(
                out=pst[jb][:], in_=acc[:, tt, dc * 512:(dc + 1) * 512], func=AF.Square,
                accum_out=statp[:, slot * 8 + dc:slot * 8 + dc + 1]),
                reads=[B_acc[tt][dc]], writes=[B_ps[jb], B_stp[slot]])

        def stats_fin(slot, n=128):
            P.op("dve", lambda e: e.reduce_sum(out=stat[0:n, slot:slot + 1], in_=statp[0:n, slot * 8:slot * 8 + 8],
                                               axis=mybir.AxisListType.X),
                 reads=[B_stp[slot]], writes=[B_st[slot]])
            P.op("act", lambda e: e.activation(out=stat[0:n, slot:slot + 1], in_=stat[0:n, slot:slot + 1],
                                               func=AF.Sqrt, scale=1.0 / D, bias=EPS),
                 reads=[B_st[slot]], writes=[B_st[slot]])
            P.op("dve", lambda e: e.reciprocal(out=stat[0:n, slot:slot + 1], in_=stat[0:n, slot:slot + 1]),
                 reads=[B_st[slot]], writes=[B_st[slot]])

        TB = [(pst[7][:].bitcast(BF16), B_ps[7]), (pst[6][:].bitcast(BF16), B_ps[6]),
              (pst[5][:].bitcast(BF16), B_ps[5]), (pst[4][:].bitcast(BF16), B_ps[4])]

        def nt(tt, slot, g_ap, g_buf, extra_w=()):
            src = acc[:, tt, :]
            xb, bxb = xnbs[tt % 2], B_xn[tt % 2]
            P.op("dve", lambda e: e.scalar_tensor_tensor(out=xb, in0=src, scalar=stat[:, slot:slot + 1],
                                                         in1=g_ap, op0=ALU.mult, op1=ALU.mult),
                 reads=B_acc[tt] + [B_st[slot], g_buf], writes=[bxb] + list(extra_w))
            for f4 in range(4):
                bank, bb = TB[f4]
                fns = []
                for i in range(8):
                    c = f4 * 8 + i
                    fns.append(lambda e, i=i, c=c, bank=bank: e.transpose(
                        out=bank[:, i * 128:(i + 1) * 128], in_=xb[:, c * 128:(c + 1) * 128],
                        identity=ident_b[:]))
                P.pe(fns, reads=[bxb, B_const], writes=[bb])
            for f4 in range(4):
                bank, bb = TB[f4]
                dst = x1[:, f4 * 8:(f4 + 1) * 8, 2 + tt * 128:2 + (tt + 1) * 128]
                srcp = bank.rearrange("p (c t) -> p c t", t=128)
                if tt == 3 and f4 % 2 == 1:
                    P.op("act", lambda e, dst=dst, srcp=srcp: e.copy(out=dst, in_=srcp),
                         reads=[bb], writes=[B_x1[tt]])
                else:
                    P.op("dve", lambda e, dst=dst, srcp=srcp: e.tensor_copy(out=dst, in_=srcp),
                         reads=[bb], writes=[B_x1[tt]])

        for ps_ in range(NPASS):
            tok0 = ps_ * T
            def xload(tt):
                r0 = tok0 + tt * 128
                P.dma("act", s_x[tt], lambda e, tt=tt, r0=r0: e.dma_start(out=acc[:, tt, :],
                                                                          in_=x[r0:r0 + 128, :]),
                      writes=B_acc[tt])

            if ps_ > 0:
                P.fence(["act"], all_tB)
                xload(0)
                xload(1)
            if ps_ > 0:
                P.dma("sp", s_g, lambda e: e.dma_start(out=gbc, in_=mix_g.partition_broadcast(128)),
                      writes=[B_gbc])
            gm, bgm = gbc, B_gbc
            P.fence(["dve"], [B_yT, B_aT0])
            if ps_ == 0:
                stats_sq(hst, [B_hst], 8, n=2)
                stats_fin(8, n=2)
                P.op("dve", lambda e, gm=gm: e.scalar_tensor_tensor(out=xnbs[1][0:2, :], in0=hst,
                                                             scalar=stat[0:2, 8:9], in1=gm[0:2, :],
                                                             op0=ALU.mult, op1=ALU.mult),
                     reads=[B_hst, B_st[8], bgm], writes=[B_xn[1]])
                fns = []
                for c in range(KC):
                    fns.append(lambda e, c=c: e.transpose(out=Sb[:, c * 2:(c + 1) * 2],
                                                          in_=xnbs[1][0:2, c * 128:(c + 1) * 128],
                                                          identity=ident_b[0:2, 0:2]))
                P.pe(fns, reads=[B_xn[1], B_const], writes=[B_ps[6]])
                P.op("dve", lambda e: e.tensor_copy(out=x1[:, :, 0:2],
                                                    in_=Sb[:, 0:64].rearrange("p (c t) -> p c t", t=2)),
                     reads=[B_ps[6]], writes=[B_x1h, B_hst])
                P.fence(["act", "dve"], [B_hst])
            for tt in range(NTT):
                stats_sq(acc[:, tt, :], B_acc[tt], tt)
                stats_fin(tt)
                if ps_ > 0 and tt + 2 < NTT:
                    xload(tt + 2)
                nt(tt, tt, gm, bgm)

            if ps_ == 0:
                dump("d_x1", R1[:], all_x1)
            P.fence(["act", "dve"], all_acc + [B_xnb, B_aT0, B_gbc])
            S_bank, B_S = pst[6], B_ps[6]
            Hb = [(pst[4], B_ps[4]), (pst[5], B_ps[5])]
            hcnt = {"n": 0}

            def in_chunk(slot, bslot, i, with_halo):
                hal = None
                if with_halo:
                    hb, bhb = Hb[hcnt["n"] % 2]
                    hcnt["n"] += 1
                    fns = []
                    for kc in range(KC):
                        fns.append(lambda e, kc=kc, hb=hb: e.matmul(
                            hb[:, 0:2], lhsT=slot[:, kc, i * 128:(i + 1) * 128], rhs=x1[:, kc, 0:2],
                            start=(kc == 0), stop=(kc == KC - 1)))
                    P.pe(fns, reads=[bslot, B_x1h], writes=[bhb])
                    hal = (hb[:, 0:2], bhb)
                bank, bb = main_bank()
                fns = []
                for kc in range(KC):
                    fns.append(lambda e, kc=kc, bank=bank: e.matmul(
                        bank[:], lhsT=slot[:, kc, i * 128:(i + 1) * 128], rhs=x1[:, kc, 2:514],
                        start=(kc == 0), stop=(kc == KC - 1)))
                P.pe(fns, reads=[bslot] + B_x1, writes=[bb])
                return bank, bb, hal

            def ssq_mm(sq_ap, bsq, first, last):
                P.pe([lambda e: e.matmul(S_bank[:], lhsT=ones_f[:], rhs=sq_ap, start=first, stop=last)],
                     reads=[bsq, B_const], writes=[B_S])

            def extract_inv(col0):
                P.op("act", lambda e: e.activation(out=inv_bc, in_=S_bank[:], func=AF.Sqrt,
                                                   scale=1.0 / CW, bias=EPS),
                     reads=[B_S], writes=[B_invbc])
                P.op("dve", lambda e: e.reciprocal(out=inv_bc, in_=inv_bc), reads=[B_invbc], writes=[B_invbc])
                bank, bb = main_bank()
                fns = []
                for tt in range(NTT):
                    fns.append(lambda e, tt=tt, bank=bank: e.transpose(
                        out=bank[:, tt * 128:(tt + 1) * 128], in_=inv_bc[:, tt * 128:(tt + 1) * 128],
                        identity=ident_f[:]))
                P.pe(fns, reads=[B_invbc, B_const], writes=[bb])
                P.op("dve", lambda e, bank=bank: e.tensor_copy(out=inv_tok[:, col0:col0 + 4],
                                                               in_=bank[:, 0:512:128]),
                     reads=[bb], writes=[B_invtok])

            for g in range(8):
                for kind, base in (("b", 0), ("c", CW), ("h", 2 * CW)):
                    c0 = base + g * 256
                    if ps_ > 0 and g == 0 and kind == "b":
                        P.wait("pool", Tok(s_x[NTT - 1].sem, s_x[NTT - 1].val))
                    slot, bslot = wtile(w_in[c0 // 256], v_in,
                                        key=("in", c0))
                    for i in range(2):
                        j = g * 2 + i
                        tb, btb = tB[i], B_tB[i]
                        bank, bb, hal = in_chunk(slot, bslot, i, kind != "b" and ps_ == 0)
                        if kind == "b":
                            P.op("act", lambda e, tb=tb, bank=bank: e.copy(out=tb["bS"], in_=bank[:]),
                                 reads=[bb], writes=[btb["bS"]])
                        elif kind == "c":
                            P.op("act", lambda e, tb=tb, bank=bank: e.copy(out=tb["cS"][:, 2:514], in_=bank[:]),
                                 reads=[bb], writes=[btb["cS"]])
                            if ps_ == 0:
                                P.op("act", lambda e, tb=tb, hal=hal: e.copy(out=tb["cS"][:, 0:2], in_=hal[0]),
                                     reads=[hal[1]], writes=[btb["cS"]])
                        else:
                            P.op("dve", lambda e, tb=tb, bank=bank: e.tensor_tensor(
                                out=tb["hf"][:, 2:514], in0=bank[:], in1=tb["cS"][:, 2:514], op=ALU.mult),
                                reads=[bb, btb["cS"]], writes=[btb["hf"]])
                            if ps_ == 0:
                                P.op("dve", lambda e, tb=tb, hal=hal: e.tensor_tensor(
                                    out=tb["hf"][:, 0:2], in0=hal[0], in1=tb["cS"][:, 0:2], op=ALU.mult),
                                    reads=[hal[1], btb["cS"]], writes=[btb["hf"]])
                                P.op("dve", lambda e, tb=tb, j=j: e.tensor_copy(
                                    out=hsave[:, 2 * j:2 * j + 2], in_=tb["hf"][:, 512:514]),
                                    reads=[btb["hf"]], writes=[B_hsave])
                            else:
                                P.op("dve", lambda e, tb=tb, j=j: e.tensor_copy(
                                    out=tb["hf"][:, 0:2], in_=hsave[:, 2 * j:2 * j + 2]),
                                    reads=[B_hsave], writes=[btb["hf"]])
                            P.op("dve", lambda e, tb=tb, j=j: e.tensor_scalar(
                                out=tb["t1"], in0=tb["hf"][:, 0:512], scalar1=convw[:, 0, j:j + 1],
                                scalar2=None, op0=ALU.mult),
                                reads=[btb["hf"], B_cw], writes=[btb["t1"]])
                            for k in (1, 2):
                                P.op("dve", lambda e, tb=tb, j=j, k=k: e.scalar_tensor_tensor(
                                    out=tb["t1"], in0=tb["hf"][:, k:k + 512], scalar=convw[:, k, j:j + 1],
                                    in1=tb["t1"], op0=ALU.mult, op1=ALU.add),
                                    reads=[btb["hf"], btb["t1"], B_cw], writes=[btb["t1"]])
                            P.op("dve", lambda e, tb=tb: e.tensor_tensor(
                                out=tb["ya"], in0=tb["t1"], in1=tb["bS"], op=ALU.mult),
                                reads=[btb["t1"], btb["bS"]], writes=[btb["ya"]])
                            P.op("act", lambda e, tb=tb: e.activation(out=tb["sq"], in_=tb["ya"], func=AF.Square),
                                 reads=[btb["ya"]], writes=[btb["sq"]])
                            P.op("act", lambda e, tb=tb, j=j: e.activation(
                                out=yT[:, j, :], in_=tb["ya"], func=AF.Copy, scale=ga[:, j:j + 1]),
                                reads=[btb["ya"], B_gab], writes=[B_yT])
                            defer(3, lambda tb=tb, btb=btb, j=j: ssq_mm(tb["sq"], btb["sq"], j == 0, j == 15))
                        tick()
            defer(4, lambda: extract_inv(0))

            for g in range(8):
                for kind, base in (("v", 4 * CW), ("u", 3 * CW)):
                    c0 = base + g * 256
                    slot, bslot = wtile(w_in[c0 // 256], v_in)
                    for i in range(2):
                        hd = g * 2 + i
                        tb, btb = tB[i], B_tB[i]
                        bank, bb, _ = in_chunk(slot, bslot, i, False)
                        if kind == "u":
                            P.op("act", lambda e, tb=tb, bank=bank: e.activation(
                                out=tb["gu"], in_=bank[:], func=AF.Gelu_apprx_tanh),
                                reads=[bb], writes=[btb["gu"]])
                        else:
                            P.op("act", lambda e, tb=tb, bank=bank: e.activation(
                                out=tb["gvT"], in_=bank[:], func=AF.Gelu_apprx_tanh),
                                reads=[bb], writes=[btb["gvT"]])

                            def stage1(tb=tb, btb=btb, hd=hd):
                                fns = []
                                for ck in range(4):
                                    fns.append(lambda e, ck=ck: e.transpose(
                                        out=Xb[:, ck * 128:(ck + 1) * 128],
                                        in_=tb["gvT"][:, ck * 128:(ck + 1) * 128], identity=ident_b[:]))
                                P.pe(fns, reads=[btb["gvT"], B_const], writes=[B_ps[7]])
                                P.op("dve", lambda e: e.tensor_copy(out=tb["gv"], in_=Xb[:, 0:512]),
                                     reads=[B_ps[7]], writes=[btb["gv"]])
                                defer(1, lambda: stage2(tb, btb, hd))

                            def stage2(tb, btb, hd):
                                hs, bhs = Hb[hd % 2]
                                fns = []
                                for ck in range(4):
                                    fns.append(lambda e, ck=ck: e.matmul(
                                        hs[:, ck * 128:(ck + 1) * 128],
                                        lhsT=tb["gv"][:, ck * 128:(ck + 1) * 128], rhs=wm3[:, hd, :],
                                        start=True, stop=True))
                                P.pe(fns, reads=[btb["gv"], B_wm], writes=[bhs])
                                P.op("dve", lambda e: e.tensor_tensor(
                                    out=tb["tb"].rearrange("p (c t) -> p c t", t=128),
                                    in0=hs[:].rearrange("p (c t) -> p c t", t=128),
                                    in1=bias3[:, hd, :].unsqueeze(1).to_broadcast([128, 4, 128]), op=ALU.add),
                                    reads=[bhs, B_bias], writes=[btb["tb"]])
                                P.op("dve", lambda e: e.tensor_tensor(out=tb["yb"], in0=tb["tb"], in1=tb["gu"],
                                                                      op=ALU.mult),
                                     reads=[btb["tb"], btb["gu"]], writes=[btb["yb"]])
                                P.op("act", lambda e: e.activation(out=tb["sq"], in_=tb["yb"], func=AF.Square),
                                     reads=[btb["yb"]], writes=[btb["sq"]])
                                P.op("act", lambda e: e.activation(out=yT[:, 16 + hd, :], in_=tb["yb"],
                                                                   func=AF.Copy, scale=gb[:, hd:hd + 1]),
                                     reads=[btb["yb"], B_gbb], writes=[B_yT])
                                defer(1, lambda: ssq_mm(tb["sq"], btb["sq"], hd == 0, hd == 15))

                            defer(2, stage1)
                        tick()
            flush()
            extract_inv(4)

            if ps_ == 0:
                dump("d_yT", R2[:], [B_yT])
                dump("d_inv", inv_tok[:], [B_invtok])
            g2stage = R1f[:, 0:4096]
            P.dma("sp", s_g, lambda e: e.dma_start(out=g2stage, in_=mlp_g.partition_broadcast(128)),
                  writes=all_x1)
            P.fence(["sp"], all_tB)
            for tt in range(NTT):
                r0 = tok0 + tt * 128
                P.dma("sp", s_x[tt], lambda e, tt=tt, r0=r0: e.dma_start(out=acc[:, tt, :],
                                                                         in_=x[r0:r0 + 128, :]),
                      writes=B_acc[tt] + all_tB)
            for dc in range(8):
                for half in range(2):
                    slot, bslot = wtile(w_out[dc * 2 + half], v_out)
                    for tt in range(NTT):
                        bi = bankrot["c"] % 7
                        bankrot["c"] += 1
                        bank, bb = pst[bi], B_ps[bi]
                        fns = []
                        for cc in range(16):
                            fns.append(lambda e, cc=cc, bank=bank, tt=tt, half=half, slot=slot: e.matmul(
                                bank[:], lhsT=yT[:, half * 16 + cc, tt * 128:(tt + 1) * 128],
                                rhs=slot[:, cc, :], start=(cc == 0), stop=(cc == 15)))
                        P.pe(fns, reads=[bslot, B_yT], writes=[bb])
                        dst = acc[:, tt, dc * 512:(dc + 1) * 512]
                        P.op("dve", lambda e, bank=bank, dst=dst, tt=tt, half=half: e.scalar_tensor_tensor(
                            out=dst, in0=bank[:], scalar=inv_tok[:, half * 4 + tt:half * 4 + tt + 1],
                            in1=dst, op0=ALU.mult, op1=ALU.add),
                            reads=[bb, B_invtok, B_acc[tt][dc]], writes=[B_acc[tt][dc]])
                        if half == 1:
                            sq_block(tt, dc, 4 + tt)

            if ps_ == 0:
                dump("d_h", R3[:], all_acc)
            P.op("act", lambda e: e.copy(out=gbc, in_=g2stage), reads=all_x1, writes=[B_gbc, B_yT])
            g2, bg2 = gbc, B_gbc
            P.fence(["act", "dve"], all_x1)
            for tt in range(NTT):
                stats_fin(4 + tt)
            for tt in range(NTT):
                nt(tt, 4 + tt, g2, bg2, extra_w=[B_yT] if tt < 2 else ())

            if ps_ == 0:
                dump("d_x2", R1[:], all_x1)
            P.fence(["act", "dve"], [B_xnb])
            dbank = {"n": 0}
            upb = {"n": 0}
            P.dma("sp", s_g, lambda e: e.dma_start(out=gbc, in_=fin_g.partition_broadcast(128)),
                  writes=[B_gbc])

            def up_block(fb):
                for u in range(4):
                    c0 = fb * FB + u * 256
                    slot, bslot = wtile(w_up[c0 // 256], v_in)
                    for i in range(2):
                        fcl = u * 2 + i
                        bi = upb["n"] % 3
                        upb["n"] += 1
                        bank, bb = pst[bi], B_ps[bi]
                        fns = []
                        for kc in range(KC):
                            fns.append(lambda e, kc=kc, bank=bank, slot=slot, i=i: e.matmul(
                                bank[:], lhsT=slot[:, kc, i * 128:(i + 1) * 128], rhs=x1[:, kc, 2:514],
                                start=(kc == 0), stop=(kc == KC - 1)))
                        P.pe(fns, reads=[bslot] + B_x1, writes=[bb])
                        P.op("act", lambda e, bank=bank: e.activation(out=pst[3][:], in_=bank[:], func=AF.Relu),
                             reads=[bb], writes=[B_ps[3]])
                        P.op("act", lambda e, fb=fb, fcl=fcl: e.activation(
                            out=aT[fb % 2][:, fcl, :], in_=pst[3][:], func=AF.Square),
                            reads=[B_ps[3]], writes=[B_aT[fb % 2]])

            def down_block(fb):
                for dq in range(4):
                    slot, bslot = wtile(w_down[fb * 4 + dq], v_dn)
                    for dcl in range(2):
                        dc = dq * 2 + dcl
                        for tt in range(NTT):
                            bi = 4 + dbank["n"] % 4
                            dbank["n"] += 1
                            bank, bb = pst[bi], B_ps[bi]
                            fns = []
                            for fc in range(8):
                                fns.append(lambda e, fc=fc, bank=bank, slot=slot, tt=tt, dcl=dcl, fb=fb: e.matmul(
                                    bank[:], lhsT=aT[fb % 2][:, fc, tt * 128:(tt + 1) * 128],
                                    rhs=slot[:, fc, dcl * 512:(dcl + 1) * 512], start=(fc == 0), stop=(fc == 7)))
                            P.pe(fns, reads=[bslot, B_aT[fb % 2]], writes=[bb])
                            dst = acc[:, tt, dc * 512:(dc + 1) * 512]
                            P.op("dve", lambda e, bank=bank, dst=dst: e.tensor_tensor(
                                out=dst, in0=bank[:], in1=dst, op=ALU.add),
                                reads=[bb, B_acc[tt][dc]], writes=[B_acc[tt][dc]])
                            if fb == NFB - 1:
                                sq_block(tt, dc, 12 + tt, jb=0)

            up_block(0)
            for fb in range(NFB):
                if fb + 1 < NFB:
                    up_block(fb + 1)
                down_block(fb)

            g3, bg3 = gbc, B_gbc
            for tt in range(NTT):
                stats_fin(12 + tt)
                src = acc[:, tt, :]
                P.op("dve", lambda e, src=src, tt=tt, g3=g3: e.scalar_tensor_tensor(
                    out=src, in0=src, scalar=stat[:, 12 + tt:13 + tt], in1=g3, op0=ALU.mult, op1=ALU.mult),
                    reads=B_acc[tt] + [B_st[12 + tt], bg3], writes=B_acc[tt])
                r0 = tok0 + tt * 128
                P.dma("sp", s_o[tt], lambda e, src=src, r0=r0: e.dma_start(out=out[r0:r0 + 128, :], in_=src),
                      reads=B_acc[tt])
            P.op("dve", lambda e: e.memset(stat[:], 0.0), writes=[B_stat] + B_st)
            P.op("dve", lambda e: e.memset(statp[:], 0.0), writes=B_stp)

        for tt in range(NTT):
            P.wait("sp", Tok(s_o[tt].sem, s_o[tt].val))

        with nc.Block() as block:
            @block.sync
            def _(e):
                for f in P.q["sp"]:
                    f(e)

            @block.gpsimd
            def _(e):
                for f in P.q["pool"]:
                    f(e)

            @block.tensor
            def _(e):
                for f in P.q["pe"]:
                    f(e)

            @block.scalar
            def _(e):
                for f in P.q["act"]:
                    f(e)

            @block.vector
            def _(e):
                for f in P.q["dve"]:
                    f(e)
    return nc


def tile_weights(w_in, w_out, w_up, w_down):
    c = np.ascontiguousarray
    wi = c(w_in.reshape(32, 128, EIN // 256, 256).transpose(2, 1, 0, 3)).reshape(EIN // 256, 128, 8192)
    wu = c(w_up.reshape(32, 128, DFF // 256, 256).transpose(2, 1, 0, 3)).reshape(DFF // 256, 128, 8192)
    wo = c(w_out.reshape(2, 16, 128, 8, 512).transpose(3, 0, 2, 1, 4)).reshape(16, 128, 8192)
    wd = c(w_down.reshape(16, 8, 128, 4, 1024).transpose(0, 3, 2, 1, 4)).reshape(64, 128, 8192)
    return wi, wo, wu, wd


def kernel(x, mix_norm_g, w_in, conv_w, spatial_w, spatial_b, conv_out_norm_g, gmlp_out_norm_g,
           w_out, mlp_norm_g, w_up, w_down, final_norm_g):
    f32 = lambda a: np.ascontiguousarray(np.asarray(a, dtype=np.float32))
    x = f32(x)
    Bn, S, Dm = x.shape
    xf = x.reshape(Bn * S, Dm)
    ncores = 8
    wi, wo, wu, wd = tile_weights(f32(w_in)[0], f32(w_out)[0], f32(w_up)[0], f32(w_down)[0])
    shared = {
        "w_in": wi, "w_out": wo, "w_up": wu, "w_down": wd,
        "mix_g": f32(mix_norm_g)[0], "mlp_g": f32(mlp_norm_g)[0], "fin_g": f32(final_norm_g),
        "conv_w": f32(conv_w)[0], "sp_w": f32(spatial_w)[0], "sp_b": f32(spatial_b)[0],
        "ga": f32(conv_out_norm_g)[0], "gb": f32(gmlp_out_norm_g)[0],
    }
    in_maps = []
    for c in range(ncores):
        r0 = c * TOKC
        xc = xf[r0:r0 + TOKC]
        xh = np.zeros((NPASS, 2, Dm), np.float32)
        if r0 % S != 0:
            xh[0] = xf[r0 - 2:r0]
        xh[1] = xc[T - 2:T]
        m = {"x": np.ascontiguousarray(xc), "xh": xh}
        m.update(shared)
        in_maps.append(m)
    nc = build_nc()
    res = run_bass_kernel_spmd(nc, in_maps, core_ids=list(range(ncores)))
    outs = [np.asarray(r["out"], dtype=np.float32) for r in res.results]
    return np.concatenate(outs, axis=0).reshape(Bn, S, Dm)
```
